# Optimizing a Trainium2 kernel written in Bass

```python
import math
import jax, jax.numpy as jnp
from jax import lax
import numpy as np

D_MODEL = 1024
BATCH = 16
SEQ = 2048
DEPTH = 4

N_MIXERS = 4
EPS = 1e-6
POOL_WINDOWS = (2, 4, 8, 16)
POOL_GROUP = D_MODEL // len(POOL_WINDOWS)
S5_GROUP = 16
S5_GROUPS = D_MODEL // S5_GROUP
S5_STATE = 64
S5_DT_MIN = 1e-3
S5_DT_MAX = 1e-1
LRU_WIDTH = D_MODEL
LRU_BLOCKS = 4
LRU_BLOCK = LRU_WIDTH // LRU_BLOCKS
LRU_CONV = 4
LRU_C = 8.0
SB_HEADS = 16
SB_HEAD_DIM = D_MODEL // SB_HEADS
SB_Q_BLOCK = 128
FFN_HIDDEN = 2816
FFN_CONV = 3

kernel_name = "interleaved_pool_s5_rglru_stickbreak_trunk"


def n_layers_of(m):
    return len(range(m, DEPTH, N_MIXERS))


def rms_norm(x, g):
    xf = x.astype(jnp.float32)
    y = xf * lax.rsqrt(jnp.mean(xf * xf, axis=-1, keepdims=True) + EPS)
    return (y * g.astype(jnp.float32)).astype(x.dtype)


def causal_depthwise_conv(x, w, b):
    k_width = w.shape[0]
    seq = x.shape[1]
    xp = jnp.pad(x, ((0, 0), (k_width - 1, 0), (0, 0)))
    y = b
    for k in range(k_width):
        y = y + w[k] * xp[:, k:k + seq]
    return y


def linear_scan_combine(left, right):
    a_l, b_l = left
    a_r, b_r = right
    return a_r * a_l, a_r * b_l + b_r


def pool_mixer(x, w, b, scale):
    bsz, seq, _ = x.shape
    xf = x.astype(jnp.float32)
    cs = jnp.pad(jnp.cumsum(xf, axis=1), ((0, 0), (1, 0), (0, 0)))
    pos = jnp.arange(seq)
    groups = []
    for gi, w_len in enumerate(POOL_WINDOWS):
        c = cs[..., gi * POOL_GROUP:(gi + 1) * POOL_GROUP]
        lo = jnp.maximum(pos + 1 - w_len, 0)
        window_sum = c[:, 1:] - jnp.take(c, lo, axis=1)
        count = (pos + 1 - lo).astype(jnp.float32)[:, None]
        groups.append(window_sum / count - xf[..., gi * POOL_GROUP:(gi + 1) * POOL_GROUP])
    d = jnp.stack(groups, axis=2)
    y = jnp.einsum('bsgc,gcd->bsgd', d, w.astype(jnp.float32)).reshape(bsz, seq, D_MODEL) + b
    return (scale * y).astype(x.dtype)


def s5_mixer(x, lam_re, lam_im, log_dt, b_re, b_im, c_re, c_im, d_skip, w_out, b_out):
    f32 = jnp.float32
    bsz, seq, _ = x.shape
    xf = x.astype(f32)
    u = xf.reshape(bsz, seq, S5_GROUPS, S5_GROUP)
    lam = lax.complex(jnp.minimum(lam_re.astype(f32), -1e-4), lam_im.astype(f32))
    dt = jnp.exp(log_dt.astype(f32))[:, None]
    lam_bar = jnp.exp(lam * dt)
    b_bar = ((lam_bar - 1.0) / lam)[..., None] * lax.complex(b_re.astype(f32), b_im.astype(f32))
    bu = lax.complex(jnp.einsum('bsgh,gph->bsgp', u, jnp.real(b_bar)),
                     jnp.einsum('bsgh,gph->bsgp', u, jnp.imag(b_bar)))
    a = jnp.broadcast_to(lam_bar, (seq,) + lam_bar.shape)[None]
    _, states = lax.associative_scan(linear_scan_combine, (a, bu), axis=1)
    y = (jnp.einsum('bsgp,ghp->bsgh', jnp.real(states), c_re.astype(f32))
         - jnp.einsum('bsgp,ghp->bsgh', jnp.imag(states), c_im.astype(f32)))
    y = y.reshape(bsz, seq, D_MODEL) + d_skip.astype(f32) * xf
    y = jax.nn.gelu(y).astype(x.dtype)
    val, gate = jnp.split(y @ w_out + b_out, 2, axis=-1)
    return val * jax.nn.sigmoid(gate)


def rglru_mixer(x, w_in, conv_w, conv_b, w_a, b_a, w_x, b_x, lam, w_out):
    f32 = jnp.float32
    bsz, seq, _ = x.shape
    gate_branch, rec = jnp.split(x @ w_in, 2, axis=-1)
    rec = causal_depthwise_conv(rec, conv_w, conv_b).astype(f32)
    rb = rec.reshape(bsz, seq, LRU_BLOCKS, LRU_BLOCK)
    r = jax.nn.sigmoid(jnp.einsum('bsnc,ncd->bsnd', rb, w_a.astype(f32)).reshape(bsz, seq, LRU_WIDTH) + b_a)
    i = jax.nn.sigmoid(jnp.einsum('bsnc,ncd->bsnd', rb, w_x.astype(f32)).reshape(bsz, seq, LRU_WIDTH) + b_x)
    log_a = -LRU_C * r * jax.nn.softplus(-lam.astype(f32))
    a = jnp.exp(log_a)
    mult = jnp.sqrt(-jnp.expm1(2.0 * log_a))
    _, h = lax.associative_scan(linear_scan_combine, (a, mult * (i * rec)), axis=1)
    y = jax.nn.gelu(gate_branch.astype(f32)) * h
    return y.astype(x.dtype) @ w_out


def stick_breaking_mixer(x, w_qkv, q_g, k_g, w_o):
    bsz, seq, _ = x.shape
    q, k, v = jnp.split(x @ w_qkv, 3, axis=-1)
    to_heads = lambda t: t.reshape(bsz, seq, SB_HEADS, SB_HEAD_DIM).transpose(0, 2, 1, 3)
    q = rms_norm(to_heads(q), q_g)
    k = rms_norm(to_heads(k), k_g)
    v = to_heads(v)
    scale = 1.0 / math.sqrt(SB_HEAD_DIM)
    outs = []
    for start in range(0, seq, SB_Q_BLOCK):
        end = start + SB_Q_BLOCK
        kb, vb = k[:, :, :end], v[:, :, :end]
        z = jnp.einsum('bhtd,bhsd->bhts', q[:, :, start:end], kb).astype(jnp.float32) * scale
        t_pos = start + jnp.arange(SB_Q_BLOCK)[:, None]
        s_pos = jnp.arange(end)[None, :]
        mask = s_pos < t_pos
        log_1m_beta = jnp.where(mask, jax.nn.log_sigmoid(-z), 0.0)
        rest = lax.cumsum(log_1m_beta, axis=3, reverse=True) - log_1m_beta
        att = jnp.where(mask, jnp.exp(jax.nn.log_sigmoid(z) + rest), 0.0)
        outs.append(jnp.einsum('bhts,bhsd->bhtd', att.astype(v.dtype), vb))
    o = jnp.concatenate(outs, axis=2).transpose(0, 2, 1, 3).reshape(bsz, seq, D_MODEL)
    return o @ w_o


def conv_ffn(x, w_in, conv_w, conv_b, w_out):
    h = causal_depthwise_conv(x @ w_in, conv_w, conv_b)
    val, gate = jnp.split(h, 2, axis=-1)
    return (jax.nn.silu(gate) * val) @ w_out


def setup_inputs(seed: int = 0) -> dict:
    key = jax.random.key(seed)
    ks = iter(jax.random.split(key, 40))
    nrm = lambda shape, std: jax.random.normal(next(ks), shape, jnp.float32) * std
    gain = lambda shape: 1.0 + nrm(shape, 0.02)
    nA, nB, nC, nD = (n_layers_of(m) for m in range(N_MIXERS))
    G, P, H = S5_GROUPS, S5_STATE, S5_GROUP
    a0 = jax.random.uniform(next(ks), (nC, LRU_WIDTH), jnp.float32, 0.9, 0.999)
    return {
        "x": nrm((BATCH, SEQ, D_MODEL), 1.0),
        "norm_mix_g": gain((DEPTH, D_MODEL)),
        "norm_ffn_g": gain((DEPTH, D_MODEL)),
        "pool_w": nrm((nA, len(POOL_WINDOWS), POOL_GROUP, POOL_GROUP), POOL_GROUP ** -0.5),
        "pool_b": nrm((nA, D_MODEL), 0.01),
        "pool_scale": 1.0 + nrm((nA, D_MODEL), 0.1),
        "s5_lam_re": -0.5 + nrm((nA * 0 + nB, G, P), 0.01),
        "s5_lam_im": math.pi * jnp.arange(P, dtype=jnp.float32) + nrm((nB, G, P), 0.01),
        "s5_log_dt": jax.random.uniform(next(ks), (nB, G), jnp.float32, math.log(S5_DT_MIN), math.log(S5_DT_MAX)),
        "s5_b_re": nrm((nB, G, P, H), (2 * H) ** -0.5),
        "s5_b_im": nrm((nB, G, P, H), (2 * H) ** -0.5),
        "s5_c_re": nrm((nB, G, H, P), P ** -0.5),
        "s5_c_im": nrm((nB, G, H, P), P ** -0.5),
        "s5_d": nrm((nB, D_MODEL), 1.0),
        "s5_w_out": nrm((nB, D_MODEL, 2 * D_MODEL), D_MODEL ** -0.5),
        "s5_b_out": nrm((nB, 2 * D_MODEL), 0.01),
        "lru_w_in": nrm((nC, D_MODEL, 2 * LRU_WIDTH), D_MODEL ** -0.5),
        "lru_conv_w": nrm((nC, LRU_CONV, LRU_WIDTH), LRU_CONV ** -0.5),
        "lru_conv_b": nrm((nC, LRU_WIDTH), 0.01),
        "lru_w_a": nrm((nC, LRU_BLOCKS, LRU_BLOCK, LRU_BLOCK), LRU_BLOCK ** -0.5),
        "lru_b_a": nrm((nC, LRU_WIDTH), 0.01),
        "lru_w_x": nrm((nC, LRU_BLOCKS, LRU_BLOCK, LRU_BLOCK), LRU_BLOCK ** -0.5),
        "lru_b_x": nrm((nC, LRU_WIDTH), 0.01),
        "lru_lam": jnp.log(a0) - jnp.log1p(-a0),
        "lru_w_out": nrm((nC, LRU_WIDTH, D_MODEL), LRU_WIDTH ** -0.5),
        "sb_w_qkv": nrm((nD, D_MODEL, 3 * D_MODEL), D_MODEL ** -0.5),
        "sb_q_g": gain((nD, SB_HEAD_DIM)),
        "sb_k_g": gain((nD, SB_HEAD_DIM)),
        "sb_w_o": nrm((nD, D_MODEL, D_MODEL), D_MODEL ** -0.5),
        "ffn_w_in": nrm((DEPTH, D_MODEL, 2 * FFN_HIDDEN), D_MODEL ** -0.5),
        "ffn_conv_w": nrm((DEPTH, FFN_CONV, 2 * FFN_HIDDEN), FFN_CONV ** -0.5),
        "ffn_conv_b": nrm((DEPTH, 2 * FFN_HIDDEN), 0.01),
        "ffn_w_out": nrm((DEPTH, FFN_HIDDEN, D_MODEL), FFN_HIDDEN ** -0.5),
    }


def reference(x, norm_mix_g, norm_ffn_g,
              pool_w, pool_b, pool_scale,
              s5_lam_re, s5_lam_im, s5_log_dt, s5_b_re, s5_b_im, s5_c_re, s5_c_im, s5_d, s5_w_out, s5_b_out,
              lru_w_in, lru_conv_w, lru_conv_b, lru_w_a, lru_b_a, lru_w_x, lru_b_x, lru_lam, lru_w_out,
              sb_w_qkv, sb_q_g, sb_k_g, sb_w_o,
              ffn_w_in, ffn_conv_w, ffn_conv_b, ffn_w_out):
    for layer in range(DEPTH):
        m, j = layer % N_MIXERS, layer // N_MIXERS
        h = rms_norm(x, norm_mix_g[layer])
        if m == 0:
            y = pool_mixer(h, pool_w[j], pool_b[j], pool_scale[j])
        elif m == 1:
            y = s5_mixer(h, s5_lam_re[j], s5_lam_im[j], s5_log_dt[j], s5_b_re[j], s5_b_im[j],
                         s5_c_re[j], s5_c_im[j], s5_d[j], s5_w_out[j], s5_b_out[j])
        elif m == 2:
            y = rglru_mixer(h, lru_w_in[j], lru_conv_w[j], lru_conv_b[j], lru_w_a[j], lru_b_a[j],
                            lru_w_x[j], lru_b_x[j], lru_lam[j], lru_w_out[j])
        else:
            y = stick_breaking_mixer(h, sb_w_qkv[j], sb_q_g[j], sb_k_g[j], sb_w_o[j])
        x = x + y.astype(x.dtype)
        f = conv_ffn(rms_norm(x, norm_ffn_g[layer]), ffn_w_in[layer], ffn_conv_w[layer], ffn_conv_b[layer], ffn_w_out[layer])
        x = x + f.astype(x.dtype)
    return x
```

```python
import contextlib
import math
import numpy as np
import concourse.bass as bass
import concourse.mybir as mybir
from concourse.bass_utils import run_bass_kernel_spmd

F32 = mybir.dt.float32
BF16 = mybir.dt.bfloat16
I32 = mybir.dt.int32
AF = mybir.ActivationFunctionType
ALU = mybir.AluOpType

ENGS = ["pe", "act", "dve", "pool", "sp"]
NDMASEM = 8
SAME_ENGINE_WAITS = True

D = 1024
S = 2048
NCH = 8
FH = 2816
NF = 22
EPS = 1e-6
NT = 4
TW = 512
RING = 12
ARENA = 29200
POOL_WINDOWS = (2, 4, 8, 16)
SB_HEADS_DBG = 4


class Prog:
    def __init__(self):
        self.nc = bass.Bass("TRN2", target_bir_lowering=False)
        self.es = contextlib.ExitStack()
        self.ops = {e: [] for e in ENGS}
        self.lastw = {}
        self.readers = {}
        self.known = {e: {} for e in ENGS}
        self.ndma = {e: 0 for e in ENGS}
        self.dma_last = {}
        self.last_compute = {}
        self.fence_toks = []
        self.n_alloc = 0

    def sb(self, shape, dt, name=None):
        self.n_alloc += 1
        return self.es.enter_context(self.nc.sbuf_tensor(name or f"sb{self.n_alloc}", list(shape), dt))

    def ps(self, shape, dt=F32, name=None):
        self.n_alloc += 1
        return self.es.enter_context(self.nc.psum_tensor(name or f"ps{self.n_alloc}", list(shape), dt))

    def dram(self, name, shape, dt, kind):
        return self.nc.dram_tensor(name, list(shape), dt, kind=kind).ap()

    def _deps(self, reads, writes):
        deps = []
        for k in reads:
            t = self.lastw.get(k)
            if t is not None:
                deps.append(t)
        for k in writes:
            t = self.lastw.get(k)
            if t is not None:
                deps.append(t)
            deps.extend(self.readers.get(k, ()))
        deps.extend(self.fence_toks)
        return deps

    def _commit(self, tok, reads, writes):
        for k in reads:
            self.readers.setdefault(k, []).append(tok)
        for k in writes:
            self.lastw[k] = tok
            self.readers[k] = []

    def _waits(self, eng, deps):
        waits = []
        kn = self.known[eng]
        for t in deps:
            if t[0] == "eng":
                _, e, idx = t
                if e == eng and (eng == "pe" or not SAME_ENGINE_WAITS):
                    continue
                if kn.get(e, -1) >= idx:
                    continue
                kn[e] = idx
                self.ops[e][idx]["marked"] = True
                waits.append(t)
            else:
                _, q, slot, val = t
                key = ("dma", q, slot)
                if kn.get(key, 0) >= val:
                    continue
                kn[key] = val
                waits.append(t)
        return waits

    def op(self, eng, fn, reads=(), writes=()):
        deps = self._deps(reads, writes)
        waits = self._waits(eng, deps)
        idx = len(self.ops[eng])
        self.ops[eng].append(dict(fn=fn, waits=waits, marked=False, dma=None))
        self.last_compute[eng] = idx
        self._commit(("eng", eng, idx), reads, writes)

    def dma(self, eng, fn, reads=(), writes=()):
        deps = self._deps(reads, writes)
        k = self.ndma[eng]
        self.ndma[eng] += 1
        slot = k % NDMASEM
        val = 16 * (k // NDMASEM + 1)
        if k >= NDMASEM:
            deps.append(("dma", eng, slot, val - 16))
        waits = self._waits(eng, deps)
        tok = ("dma", eng, slot, val)
        self.ops[eng].append(dict(fn=fn, waits=waits, marked=False, dma=(slot, val)))
        self.dma_last[(eng, slot)] = val
        self._commit(tok, reads, writes)

    def fence(self):
        self.fence_toks = [("eng", e, i) for e, i in self.last_compute.items()]

    def finalize(self):
        nc = self.nc
        deps = [("dma", q, s, v) for (q, s), v in self.dma_last.items()]
        waits = self._waits("sp", deps)
        self.ops["sp"].append(dict(fn=None, waits=waits, marked=False, dma=None))
        EPOCH = 1000
        nmark = {e: sum(o["marked"] for o in self.ops[e]) for e in ENGS}
        sems = {e: [self.es.enter_context(nc.semaphore(f"s_{e}{j}")) for j in range(nmark[e] // EPOCH + 1)] for e in ENGS}
        dsems = {}
        for e in ENGS:
            for s in range(min(NDMASEM, self.ndma[e])):
                dsems[(e, s)] = self.es.enter_context(nc.semaphore(f"d_{e}{s}"))
        semval = {}
        for e in ENGS:
            c = 0
            for i, o in enumerate(self.ops[e]):
                if o["marked"]:
                    semval[(e, i)] = (c // EPOCH, c % EPOCH + 1)
                    c += 1
        self.stats = {e: (len(self.ops[e]), sum(len(o["waits"]) for o in self.ops[e])) for e in ENGS}

        def emit(e, eng):
            for i, o in enumerate(self.ops[e]):
                for t in o["waits"]:
                    if t[0] == "eng":
                        ep, val = semval[(t[1], t[2])]
                        eng.wait_ge(sems[t[1]][ep], val)
                    else:
                        eng.wait_ge(dsems[(t[1], t[2])], t[3])
                if o["fn"] is None:
                    continue
                inst = o["fn"](eng)
                if o["dma"] is not None:
                    inst.then_inc(dsems[(e, o["dma"][0])], 16)
                elif o["marked"]:
                    inst.then_inc(sems[e][semval[(e, i)][0]], 1)

        with nc.Block() as block:
            @block.tensor
            def _(eng):
                emit("pe", eng)

            @block.scalar
            def _(eng):
                emit("act", eng)

            @block.vector
            def _(eng):
                emit("dve", eng)

            @block.gpsimd
            def _(eng):
                emit("pool", eng)

            @block.sync
            def _(eng):
                emit("sp", eng)
        self.es.close()
        return nc


def chan_cols(v):
    v = np.asarray(v, np.float32)
    return np.ascontiguousarray(v.reshape(-1, 128).T)


class VecLayout:
    def __init__(self):
        self.off = {}
        self.cols = []
        self.n = 0

    def add(self, name, arr):
        arr = np.asarray(arr, np.float32)
        assert arr.shape[0] == 128
        arr = arr.reshape(128, -1)
        self.off[name] = self.n
        self.cols.append(arr)
        self.n += arr.shape[1]

    def pack(self):
        return np.ascontiguousarray(np.concatenate(self.cols, axis=1))


def prep_vecs(inp):
    V = VecLayout()
    V.add("norm_mix_g", np.concatenate([chan_cols(inp["norm_mix_g"][l]) for l in range(4)], axis=1))
    V.add("norm_ffn_g", np.concatenate([chan_cols(inp["norm_ffn_g"][l]) for l in range(4)], axis=1))
    V.add("pool_b", chan_cols(inp["pool_b"][0]))
    V.add("pool_scale", chan_cols(inp["pool_scale"][0]))
    V.add("ffn_conv_w", np.concatenate([chan_cols(inp["ffn_conv_w"][l, k]) for l in range(4) for k in range(3)], axis=1))
    V.add("ffn_conv_b", np.concatenate([chan_cols(inp["ffn_conv_b"][l]) for l in range(4)], axis=1))
    V.add("lru_conv_w", np.concatenate([chan_cols(inp["lru_conv_w"][0, k]) for k in range(4)], axis=1))
    for nm in ("lru_conv_b", "lru_b_a", "lru_b_x", "lru_lam"):
        V.add(nm, chan_cols(inp[nm][0]))
    lre, lim, ldt = (np.asarray(inp[k][0], np.float32) for k in ("s5_lam_re", "s5_lam_im", "s5_log_dt"))
    V.add("s5_lre", np.tile(lre.T, (2, 1)))
    V.add("s5_lim", np.tile(lim.T, (2, 1)))
    V.add("s5_ldt", np.broadcast_to(ldt.reshape(1, 64), (128, 64)))
    V.add("s5_d", chan_cols(inp["s5_d"][0]))
    V.add("s5_b_out", chan_cols(inp["s5_b_out"][0]))
    V.add("sb_q_g", np.tile(np.asarray(inp["sb_q_g"][0], np.float32), 2).reshape(128, 1))
    V.add("sb_k_g", np.tile(np.asarray(inp["sb_k_g"][0], np.float32), 2).reshape(128, 1))
    return V


def prep_consts():
    C = VecLayout()
    C.add("ident", np.eye(128, dtype=np.float32))
    rc = np.zeros((4, 16), np.float32)
    for wi, w in enumerate(POOL_WINDOWS):
        for t in range(16):
            rc[wi, t] = 1.0 / min(t + 1, w)
    C.add("pool_rc", np.broadcast_to(rc.reshape(1, 64), (128, 64)))
    jt = np.zeros((128, 128), np.float32)
    for k in range(64):
        jt[k, k + 64] = 1.0
        jt[k + 64, k] = -1.0
    C.add("jt", jt)
    return C


def prep_cmask():
    p = np.arange(128)[:, None]
    j = np.arange(512)[None, :]
    m01 = np.concatenate([(j - p > off).astype(np.float32) for off in (0, 128, 256, 384)], axis=1)
    neg = (1.0 - m01) * -30000.0
    jj = np.arange(128)[:, None]
    ss = np.arange(128)[None, :]
    negtri = -(jj >= ss).astype(np.float32)
    ident = np.eye(128, dtype=np.float32)
    return np.ascontiguousarray(np.concatenate([m01, neg, negtri, ident], axis=1))


def prep_s5(inp):
    lre, lim, ldt = (np.asarray(inp[k][0], np.float32) for k in ("s5_lam_re", "s5_lam_im", "s5_log_dt"))
    row = np.stack([lre.reshape(-1), lim.reshape(-1), np.repeat(ldt, 64)], axis=0)
    s5row = np.ascontiguousarray(np.broadcast_to(row.reshape(1, 3 * 4096), (128, 3 * 4096)))
    bre, bim = np.asarray(inp["s5_b_re"][0], np.float32), np.asarray(inp["s5_b_im"][0], np.float32)
    bt = np.zeros((8, 128, 2, 8, 64), np.float32)
    for g in range(64):
        c, gl = g // 8, g % 8
        bt[c, 16 * gl:16 * gl + 16, 0, gl, :] = bre[g].T
        bt[c, 16 * gl:16 * gl + 16, 1, gl, :] = bim[g].T
    cre, cim = np.asarray(inp["s5_c_re"][0], np.float32), np.asarray(inp["s5_c_im"][0], np.float32)
    cc = np.concatenate([cre.transpose(2, 0, 1), cim.transpose(2, 0, 1)], axis=0)
    return s5row, np.ascontiguousarray(bt.reshape(8, 128, 1024)), np.ascontiguousarray(cc.reshape(128, 1024))


def wblock_v(W, qq, half):
    blk = W[half * 512:(half + 1) * 512, 2 * D + qq * 256:2 * D + (qq + 1) * 256]
    return np.ascontiguousarray(blk.reshape(4, 128, 256).transpose(1, 0, 2).reshape(128, 1024))


def wblock_in(W, col0, kc=8):
    blk = W[:kc * 128, col0:col0 + 128].reshape(kc, 128, 128).transpose(1, 0, 2).reshape(128, kc * 128)
    out = np.zeros((128, 1024), np.float32)
    out[:, :kc * 128] = blk
    return out


def prep_weights(inp, layers):
    blocks = []
    for l in layers:
        m = l % 4
        if m == 0:
            pw = inp["pool_w"][0]
            for gi in range(4):
                for mo in range(2):
                    blocks.append(wblock_in(pw[gi], mo * 128, kc=2))
        if m == 2:
            win, wa, wx, wout = inp["lru_w_in"][0], inp["lru_w_a"][0], inp["lru_w_x"][0], inp["lru_w_out"][0]
            for nb in range(4):
                for cc in range(2):
                    blocks.append(wblock_in(win, D + (2 * nb + cc) * 128))
                for mo in range(2):
                    blocks.append(wblock_in(win, (2 * nb + mo) * 128))
                    blocks.append(wblock_in(wa[nb], mo * 128, kc=2))
                    blocks.append(wblock_in(wx[nb], mo * 128, kc=2))
            for mo in range(NCH):
                blocks.append(wblock_in(wout, mo * 128))
        if m == 1:
            w5 = inp["s5_w_out"][0]
            for mo in range(NCH):
                blocks.append(wblock_in(w5, mo * 128))
                blocks.append(wblock_in(w5, D + mo * 128))
        if m == 3:
            wqkv, wo_ = inp["sb_w_qkv"][0], inp["sb_w_o"][0]
            for qq in range(4):
                for c in (2 * qq, 2 * qq + 1):
                    blocks.append(wblock_in(wqkv, c * 128))
                for c in (2 * qq, 2 * qq + 1):
                    blocks.append(wblock_in(wqkv, D + c * 128))
                blocks.append(wblock_v(wqkv, qq, 0))
                blocks.append(wblock_v(wqkv, qq, 1))
                for c in (2 * qq, 2 * qq + 1):
                    blocks.append(np.ascontiguousarray(wo_[c * 128:(c + 1) * 128, :]))
        wi, wo = inp["ffn_w_in"][l], inp["ffn_w_out"][l]
        for f in range(NF):
            blocks.append(wblock_in(wi, f * 128))
            blocks.append(wblock_in(wi, FH + f * 128))
            blocks.append(np.ascontiguousarray(wo[f * 128:(f + 1) * 128, :]))
    return np.ascontiguousarray(np.stack(blocks, axis=0))


class Builder:
    def __init__(self, layers, nseq, nblocks, voff, nv, coff, ncst):
        self.layers = layers
        self.nseq = nseq
        self.voff, self.coff = voff, coff
        P = self.P = Prog()
        self.x_d = P.dram("xT", [nseq, 128, NCH, S], F32, "ExternalInput")
        self.w_d = P.dram("wts", [nblocks, 128, 1024], F32, "ExternalInput")
        self.v_d = P.dram("vecs", [128, nv], F32, "ExternalInput")
        self.c_d = P.dram("consts", [128, ncst], F32, "ExternalInput")
        self.o_d = P.dram("outT", [nseq, 128, NCH, S], F32, "ExternalOutput")
        self.m_d = P.dram("cmask", [128, 4352], F32, "ExternalInput")
        self.s5row_d = P.dram("s5row", [128, 3 * 4096], F32, "ExternalInput")
        self.s5bt_d = P.dram("s5bt", [8, 128, 1024], F32, "ExternalInput")
        self.s5c_d = P.dram("s5c", [128, 1024], F32, "ExternalInput")
        self.bank_mod = 8
        self.nblocks = nblocks
        self.xT = P.sb([128, NCH, S], F32, "xT_sb")
        self.vecs = P.sb([128, nv], F32, "vecs_sb")
        self.cst = P.sb([128, ncst], F32, "cst_sb")
        self.ring = [P.sb([128, 1024], BF16, f"ring{i}") for i in range(RING)]
        self.ones = P.sb([128, 128], BF16, "ones")
        self.bsv = P.sb([128, 8], F32, "bsv")
        self.lsc = P.sb([128, 8], F32, "lsc")
        self.gqs = P.sb([128, 1], F32, "gqs")
        self.arena = P.sb([128, ARENA], F32, "arena")
        self.psb = [P.ps([128, TW], F32, f"psb{i}") for i in range(8)]
        self.bank_i = 0
        self.blk_i = 0
        self.blk_issued = 0
        self.blk_total = nblocks * nseq
        self.arena_off = 0
        self.slot_free = [True] * RING

    def bank(self):
        b = self.bank_i
        self.bank_i = (self.bank_i + 1) % self.bank_mod
        return b

    def v(self, name, col, n=1):
        o = self.voff[name] + col
        return self.vecs[:, o:o + n]

    def c(self, name, col, n=1):
        o = self.coff[name] + col
        return self.cst[:, o:o + n]

    def carve(self, n_f32, dt=F32):
        a = self.arena_off
        self.arena_off += n_f32
        assert self.arena_off <= ARENA, self.arena_off
        ap = self.arena[:, a:a + n_f32]
        if dt == BF16:
            ap = ap.bitcast(BF16)
        return ap

    def reset_arena(self, off=0):
        self.P.fence()
        self.arena_off = off

    def norm_carve(self):
        self.rstd = self.carve(S)
        self.sq = [self.carve(NCH * TW // 2, BF16).rearrange("p (c s) -> p c s", c=NCH) for _ in range(2)]
        self.nrm_tmp = self.carve(TW)

    def try_issue(self):
        while self.blk_issued < self.blk_total and self.slot_free[self.blk_issued % RING]:
            g = self.blk_issued
            self.blk_issued += 1
            slot = g % RING
            self.slot_free[slot] = False
            src = self.w_d[g % self.nblocks]
            dst = self.ring[slot]
            self.P.dma("pool", lambda e, dst=dst, src=src: e.dma_start(out=dst[:], in_=src), writes=[("ring", slot)])

    def next_block(self):
        g = self.blk_i
        self.blk_i += 1
        assert g < self.blk_issued, (g, self.blk_issued)
        return g % RING

    def done_block(self, slot):
        self.slot_free[slot] = True
        self.try_issue()

    def setup(self):
        P = self.P
        P.dma("sp", lambda e: e.dma_start(out=self.vecs[:], in_=self.v_d[:, :]), writes=["vecs"])
        P.dma("sp", lambda e: e.dma_start(out=self.cst[:], in_=self.c_d[:, :]), writes=["cst"])
        P.op("dve", lambda e: e.memset(self.ones[:], 1.0), writes=["ones"])
        if 0 in self.layers:
            P.op("dve", lambda e: e.tensor_tensor(out=self.bsv[:], in0=self.v("pool_b", 0, 8), in1=self.v("pool_scale", 0, 8),
                                                  op=ALU.mult), reads=["vecs"], writes=["bsv"])
        if 3 in self.layers:
            P.op("dve", lambda e: e.tensor_scalar(out=self.gqs[:], in0=self.v("sb_q_g", 0), scalar1=0.125, scalar2=None, op0=ALU.mult),
                 reads=["vecs"], writes=["gqs"])
        if 2 in self.layers:
            P.op("act", lambda e: e.activation(out=self.lsc[:], in_=self.v("lru_lam", 0, 8), func=AF.Exp, scale=-1.0),
                 reads=["vecs"], writes=["lsc"])
            P.op("act", lambda e: e.activation(out=self.lsc[:], in_=self.lsc[:], func=AF.Ln, bias=1.0), reads=["lsc"], writes=["lsc"])
            P.op("dve", lambda e: e.tensor_scalar(out=self.lsc[:], in0=self.lsc[:], scalar1=-8.0, scalar2=None, op0=ALU.mult),
                 reads=["lsc"], writes=["lsc"])
        self.try_issue()

    def load_x(self, s):
        P = self.P
        for c in range(NCH):
            P.dma("sp", lambda e, c=c: e.dma_start(out=self.xT[:, c, :], in_=self.x_d[s, :, c, :]),
                  writes=[("xT", c, n) for n in range(NT)])

    def store_x(self, s):
        P = self.P
        for c in range(NCH):
            P.dma("sp", lambda e, c=c: e.dma_start(out=self.o_d[s, :, c, :], in_=self.xT[:, c, :]),
                  reads=[("xT", c, n) for n in range(NT)])

    def rms_rstd(self):
        P = self.P
        rstd, nrm_tmp = self.rstd, self.nrm_tmp
        for n in range(NT):
            sl = slice(n * TW, (n + 1) * TW)
            sq = self.sq[n % 2]
            for c in range(NCH):
                P.op("act", lambda e, c=c, sq=sq, sl=sl: e.activation(out=sq[:, c, :], in_=self.xT[:, c, sl], func=AF.Square),
                     reads=[("xT", c, n)], writes=[("sq", n % 2, c)])
            b = self.bank()
            for c in range(NCH):
                P.op("pe", lambda e, c=c, sq=sq, b=b: e.matmul(self.psb[b][:], lhsT=self.ones[:], rhs=sq[:, c, :],
                                                               start=(c == 0), stop=(c == NCH - 1)),
                     reads=["ones", ("sq", n % 2, c)], writes=[("ps", b)])
            P.op("act", lambda e, b=b: e.activation(out=nrm_tmp[:], in_=self.psb[b][:], func=AF.Sqrt, scale=1.0 / D, bias=EPS),
                 reads=[("ps", b)], writes=["nrm_tmp"])
            P.op("dve", lambda e, sl=sl: e.reciprocal(out=rstd[:, sl], in_=nrm_tmp[:]),
                 reads=["nrm_tmp"], writes=[("rstd", n)])

    def make_xn(self, xn, gname, l):
        P = self.P
        rstd = self.rstd
        for c in range(NCH):
            g = self.v(gname, l * 8 + c)
            P.op("dve", lambda e, c=c, g=g: e.scalar_tensor_tensor(out=xn[:, c, :], in0=self.xT[:, c, :], scalar=g, in1=rstd[:],
                                                                   op0=ALU.mult, op1=ALU.mult),
                 reads=[("xT", c, n) for n in range(NT)] + [("rstd", n) for n in range(NT)] + ["vecs"],
                 writes=[("xn", c)])

    def norm_xn(self, gname, l):
        self.reset_arena()
        xn = self.carve(NCH * S // 2, BF16).rearrange("p (c s) -> p c s", c=NCH)
        mark = self.arena_off
        self.norm_carve()
        self.rms_rstd()
        self.make_xn(xn, gname, l)
        self.reset_arena(mark)
        return xn

    def pool_mixer(self, l):
        P = self.P
        self.reset_arena()
        self.norm_carve()
        self.rms_rstd()
        rstd = self.rstd
        d = self.carve(NCH * S // 2, BF16)
        H = [self.carve(16 + S) for _ in range(2)]
        SA = [self.carve(16 + S) for _ in range(2)]
        SB = [self.carve(16 + S) for _ in range(2)]
        t16 = self.carve(16)
        for buf in H + SA + SB:
            P.op("dve", lambda e, buf=buf: e.memset(buf[:, 0:16], 0.0), writes=[("pm", id(buf))])
        for c in range(NCH):
            gi = c // 2
            w = POOL_WINDOWS[gi]
            h, sa, sb_ = H[c % 2], SA[c % 2], SB[c % 2]
            g = self.v("norm_mix_g", l * 8 + c)
            P.op("dve", lambda e, c=c, g=g, h=h: e.scalar_tensor_tensor(out=h[:, 16:16 + S], in0=self.xT[:, c, :], scalar=g, in1=rstd[:],
                                                                        op0=ALU.mult, op1=ALU.mult),
                 reads=[("xT", c, n) for n in range(NT)] + [("rstd", n) for n in range(NT)] + ["vecs"], writes=[("pm", id(h))])
            src = h
            sh = 1
            pp = [sa, sb_]
            j = 0
            while sh < w:
                dst = pp[j % 2]
                P.op("dve", lambda e, src=src, dst=dst, sh=sh: e.tensor_tensor(out=dst[:, 16:16 + S], in0=src[:, 16:16 + S],
                                                                               in1=src[:, 16 - sh:16 + S - sh], op=ALU.add),
                     reads=[("pm", id(src))], writes=[("pm", id(dst))])
                src = dst
                sh *= 2
                j += 1
            dc = d[:, c * S:(c + 1) * S]
            P.op("dve", lambda e, src=src, h=h, dc=dc, w=w: e.scalar_tensor_tensor(out=dc, in0=src[:, 16:16 + S], scalar=1.0 / w,
                                                                                   in1=h[:, 16:16 + S], op0=ALU.mult, op1=ALU.subtract),
                 reads=[("pm", id(src)), ("pm", id(h))], writes=[("pd", c)])
            rc = self.c("pool_rc", gi * 16, 16)
            P.op("dve", lambda e, src=src, rc=rc: e.tensor_tensor(out=t16[:], in0=src[:, 16:32], in1=rc, op=ALU.mult),
                 reads=[("pm", id(src)), "cst"], writes=["t16"])
            P.op("dve", lambda e, h=h, dc=dc: e.tensor_tensor(out=dc[:, 0:16], in0=t16[:], in1=h[:, 16:32], op=ALU.subtract),
                 reads=["t16", ("pm", id(h))], writes=[("pd", c)])
        tmp = [self.carve(TW) for _ in range(2)]
        ti = 0
        for gi in range(4):
            for mo in range(2):
                c = 2 * gi + mo
                slot = self.next_block()
                blk = self.ring[slot]
                for n in range(NT):
                    sl = slice(n * TW, (n + 1) * TW)
                    b = self.bank()
                    for k in range(2):
                        kc = 2 * gi + k
                        P.op("pe", lambda e, b=b, blk=blk, k=k, kc=kc, sl=sl: e.matmul(
                            self.psb[b][:], lhsT=blk[:, k * 128:(k + 1) * 128], rhs=d[:, kc * S + sl.start:kc * S + sl.stop],
                            start=(k == 0), stop=(k == 1)),
                            reads=[("ring", slot), ("pd", kc)], writes=[("ps", b)])
                    t = tmp[ti % 2]
                    ti += 1
                    P.op("act", lambda e, b=b, t=t, c=c: e.activation(out=t[:], in_=self.psb[b][:], func=AF.Identity,
                                                                      scale=self.v("pool_scale", c), bias=self.bsv[:, c:c + 1]),
                         reads=[("ps", b), "vecs", "bsv"], writes=[("ptmp", id(t))])
                    P.op("dve", lambda e, t=t, c=c, sl=sl: e.tensor_tensor(out=self.xT[:, c, sl], in0=self.xT[:, c, sl], in1=t[:], op=ALU.add),
                         reads=[("ptmp", id(t)), ("xT", c, n)], writes=[("xT", c, n)])
                self.done_block(slot)

    def linear_tile(self, slot, rhs_fn, nk, n, evac):
        P = self.P
        b = self.bank()
        blk = self.ring[slot]
        for k in range(nk):
            rhs, rkeys = rhs_fn(k, n)
            P.op("pe", lambda e, b=b, blk=blk, k=k, rhs=rhs: e.matmul(self.psb[b][:], lhsT=blk[:, k * 128:(k + 1) * 128], rhs=rhs,
                                                                     start=(k == 0), stop=(k == nk - 1)),
                 reads=[("ring", slot)] + rkeys, writes=[("ps", b)])
        evac(b, n)

    def resid_add(self, m):
        def evac(b, n):
            sl = slice(n * TW, (n + 1) * TW)
            self.P.op("dve", lambda e: e.tensor_tensor(out=self.xT[:, m, sl], in0=self.xT[:, m, sl], in1=self.psb[b][:], op=ALU.add),
                      reads=[("ps", b), ("xT", m, n)], writes=[("xT", m, n)])
        return evac

    def lru_mixer(self, l):
        P = self.P
        xn = self.norm_xn("norm_mix_g", l)
        yb = self.carve(NCH * S // 2, BF16).rearrange("p (c s) -> p c s", c=NCH)
        hc = self.carve(1028, BF16)
        cacc = self.carve(S)
        recb = [self.carve(S // 2, BF16) for _ in range(2)]
        A, B, C = self.carve(S), self.carve(S), self.carve(S)
        Dg = self.carve(S // 2, BF16)
        P.op("dve", lambda e: e.memset(hc[:, 0:3], 0.0), writes=["hc"])
        xn_rhs = lambda k, n: (xn[:, k, n * TW:(n + 1) * TW], [("xn", k)])
        cw = lambda k, c: self.v("lru_conv_w", k * 8 + c)
        for nb in range(4):
            for cc in range(2):
                c = 2 * nb + cc
                slot = self.next_block()

                def evac_rec(b, n, c=c):
                    sl = slice(n * TW, (n + 1) * TW)
                    P.op("act", lambda e: e.activation(out=hc[:, 3 + n * TW:3 + (n + 1) * TW], in_=self.psb[b][:], func=AF.Copy),
                         reads=[("ps", b)], writes=["hc"])
                    P.op("act", lambda e: e.activation(out=cacc[:, sl], in_=self.psb[b][:], func=AF.Identity,
                                                       scale=cw(3, c), bias=self.v("lru_conv_b", c)),
                         reads=[("ps", b), "vecs"], writes=["cacc"])
                for n in range(NT):
                    self.linear_tile(slot, xn_rhs, NCH, n, evac_rec)
                self.done_block(slot)
                for k in (2, 1):
                    P.op("dve", lambda e, k=k, c=c: e.scalar_tensor_tensor(out=cacc[:], in0=hc[:, k:k + S], scalar=cw(k, c), in1=cacc[:],
                                                                           op0=ALU.mult, op1=ALU.add),
                         reads=["hc", "cacc", "vecs"], writes=["cacc"])
                rb = recb[cc]
                P.op("dve", lambda e, c=c, rb=rb: e.scalar_tensor_tensor(out=rb[:], in0=hc[:, 0:S], scalar=cw(0, c), in1=cacc[:],
                                                                         op0=ALU.mult, op1=ALU.add),
                     reads=["hc", "cacc", "vecs"], writes=[("recb", cc)])
            rec_rhs = lambda k, n: (recb[k][:, n * TW:(n + 1) * TW], [("recb", k)])
            for mo in range(2):
                c = 2 * nb + mo
                slot = self.next_block()

                def evac_g(b, n):
                    sl = slice(n * TW, (n + 1) * TW)
                    P.op("act", lambda e: e.activation(out=Dg[:, sl], in_=self.psb[b][:], func=AF.Gelu), reads=[("ps", b)], writes=["Dg"])
                for n in range(NT):
                    self.linear_tile(slot, xn_rhs, NCH, n, evac_g)
                self.done_block(slot)
                for (dst, dkey, bname) in ((A, "lA", "lru_b_a"), (C, "lC", "lru_b_x")):
                    slot = self.next_block()

                    def evac_s(b, n, dst=dst, dkey=dkey, bname=bname, c=c):
                        sl = slice(n * TW, (n + 1) * TW)
                        P.op("act", lambda e: e.activation(out=dst[:, sl], in_=self.psb[b][:], func=AF.Sigmoid, bias=self.v(bname, c)),
                             reads=[("ps", b), "vecs"], writes=[dkey])
                    for n in range(NT):
                        self.linear_tile(slot, rec_rhs, 2, n, evac_s)
                    self.done_block(slot)
                P.op("act", lambda e, c=c: e.activation(out=A[:], in_=A[:], func=AF.Exp, scale=self.lsc[:, c:c + 1]),
                     reads=["lA", "lsc"], writes=["lA"])
                P.op("dve", lambda e: e.tensor_tensor(out=B[:], in0=A[:], in1=A[:], op=ALU.mult), reads=["lA"], writes=["lB"])
                P.op("act", lambda e: e.activation(out=B[:], in_=B[:], func=AF.Sqrt, scale=-1.0, bias=1.0), reads=["lB"], writes=["lB"])
                rb = recb[mo]
                P.op("dve", lambda e, rb=rb: e.tensor_tensor(out=C[:], in0=C[:], in1=rb[:], op=ALU.mult),
                     reads=["lC", ("recb", mo)], writes=["lC"])
                P.op("dve", lambda e: e.tensor_tensor(out=C[:], in0=C[:], in1=B[:], op=ALU.mult), reads=["lC", "lB"], writes=["lC"])
                P.op("dve", lambda e: e.tensor_tensor_scan(out=B[:], data0=A[:], data1=C[:], initial=0.0, op0=ALU.mult, op1=ALU.add),
                     reads=["lA", "lC", "lB"], writes=["lB"])
                P.op("dve", lambda e, c=c: e.tensor_tensor(out=yb[:, c, :], in0=Dg[:], in1=B[:], op=ALU.mult),
                     reads=["Dg", "lB"], writes=[("yb", c)])
        yb_rhs = lambda k, n: (yb[:, k, n * TW:(n + 1) * TW], [("yb", k)])
        for m in range(NCH):
            slot = self.next_block()
            for n in range(NT):
                self.linear_tile(slot, yb_rhs, NCH, n, self.resid_add(m))
            self.done_block(slot)

    def cexp(self, n, lre_dt, lim_dt, out_r, out_i, tmps, itmp, F):
        P = self.P
        er, y, f, m = tmps
        uid = id(out_r)
        kk = ("cx", uid)
        P.op("act", lambda e: e.activation(out=er, in_=lre_dt, func=AF.Exp, scale=float(n)), reads=["s5in"], writes=[kk + ("er",)])
        for (dst, add) in ((out_i, 0.5), (out_r, 0.75)):
            P.op("dve", lambda e, add=add: e.tensor_scalar(out=y, in0=lim_dt, scalar1=float(n) / (2 * math.pi), scalar2=add,
                                                           op0=ALU.mult, op1=ALU.add), reads=["s5in"], writes=[kk + ("y",)])
            P.op("dve", lambda e: e.tensor_copy(out=itmp, in_=y), reads=[kk + ("y",)], writes=[kk + ("i",)])
            P.op("dve", lambda e: e.tensor_copy(out=f, in_=itmp), reads=[kk + ("i",)], writes=[kk + ("f",)])
            P.op("dve", lambda e: e.tensor_tensor(out=f, in0=y, in1=f, op=ALU.subtract), reads=[kk + ("y",), kk + ("f",)], writes=[kk + ("f",)])
            P.op("dve", lambda e: e.tensor_scalar(out=m, in0=f, scalar1=0.0, scalar2=None, op0=ALU.is_lt), reads=[kk + ("f",)], writes=[kk + ("m",)])
            P.op("dve", lambda e: e.tensor_tensor(out=f, in0=f, in1=m, op=ALU.add), reads=[kk + ("f",), kk + ("m",)], writes=[kk + ("f",)])
            P.op("dve", lambda e: e.tensor_scalar(out=m, in0=f, scalar1=1.0, scalar2=None, op0=ALU.is_ge), reads=[kk + ("f",)], writes=[kk + ("m",)])
            P.op("dve", lambda e: e.tensor_tensor(out=f, in0=f, in1=m, op=ALU.subtract), reads=[kk + ("f",), kk + ("m",)], writes=[kk + ("f",)])
            P.op("dve", lambda e: e.tensor_scalar(out=f, in0=f, scalar1=2 * math.pi, scalar2=-math.pi, op0=ALU.mult, op1=ALU.add),
                 reads=[kk + ("f",)], writes=[kk + ("f",)])
            P.op("dve", lambda e: e.tensor_scalar(out=f, in0=f, scalar1=3.14159, scalar2=-3.14159, op0=ALU.min, op1=ALU.max),
                 reads=[kk + ("f",)], writes=[kk + ("f",)])
            P.op("act", lambda e, dst=dst: e.activation(out=dst, in_=f, func=AF.Sin), reads=[kk + ("f",)], writes=[kk + ("o", id(dst))])
        P.op("dve", lambda e: e.tensor_tensor(out=out_r, in0=out_r, in1=er, op=ALU.mult),
             reads=[kk + ("o", id(out_r)), kk + ("er",)], writes=[kk + ("o", id(out_r)), "s5tab"])
        P.op("dve", lambda e: e.tensor_tensor(out=out_i, in0=out_i, in1=er, op=ALU.mult),
             reads=[kk + ("o", id(out_i)), kk + ("er",)], writes=[kk + ("o", id(out_i)), "s5tab"])

    def s5_mixer(self, l):
        P = self.P
        xn = self.norm_xn("norm_mix_g", l)
        POWS = [1, 2, 3, 4, 8, 12, 16, 32, 48, 64, 128, 192, 256, 512, 768, 1024]
        NPW = len(POWS)
        AR = self.carve(NPW * 64).rearrange("p (w g) -> p w g", w=NPW)
        AI = self.carve(NPW * 64).rearrange("p (w g) -> p w g", w=NPW)
        BTs = self.carve(64 * 128 // 2, BF16).rearrange("p (g z) -> p g z", g=64)
        Cc = self.carve(1024 // 2, BF16).rearrange("p (g h) -> p g h", g=64)
        Cpad = self.carve(8 * 128 // 2, BF16).rearrange("p (g c) -> p g c", g=8)
        mark = self.arena_off
        ident = self.c("ident", 0, 128)
        jt = self.c("jt", 0, 128)
        col_ld = self.carve(64)
        col_li = self.carve(64)
        ctm = [self.carve(64) for _ in range(4)]
        cit = self.carve(64).bitcast(I32)
        lre, lim, ldt = self.v("s5_lre", 0, 64), self.v("s5_lim", 0, 64), self.v("s5_ldt", 0, 64)

        def prep_in(dst_ld, dst_li, lre, lim, ldt, dtt):
            P.op("act", lambda e: e.activation(out=dtt, in_=ldt, func=AF.Exp), reads=["vecs", "s5row"], writes=["s5dtt"])
            P.op("dve", lambda e: e.tensor_scalar(out=dst_ld, in0=lre, scalar1=-1e-4, scalar2=None, op0=ALU.min),
                 reads=["vecs", "s5row"], writes=["s5in"])
            P.op("dve", lambda e: e.tensor_tensor(out=dst_ld, in0=dst_ld, in1=dtt, op=ALU.mult), reads=["s5in", "s5dtt"], writes=["s5in"])
            P.op("dve", lambda e: e.tensor_tensor(out=dst_li, in0=lim, in1=dtt, op=ALU.mult), reads=["s5dtt", "vecs", "s5row"], writes=["s5in"])
        prep_in(col_ld, col_li, lre, lim, ldt, ctm[0])
        for w, n in enumerate(POWS):
            self.cexp(n, col_ld, col_li, AR[:, w, :], AI[:, w, :], ctm, cit, 64)
        RW = 512
        R = [self.carve(RW) for _ in range(16)]
        rit = self.carve(RW).bitcast(I32)
        braw = self.carve(1024)
        bt1 = [self.carve(64) for _ in range(2)]
        bt2 = [self.carve(64) for _ in range(2)]
        r_lre, r_lim, r_ldt, r_ld, r_li, r_dt, lr, li_, sre, sim, den, t_ = R[:12]
        for c in range(NCH):
            for i, dstt in enumerate((r_lre, r_lim, r_ldt)):
                P.dma("sp", lambda e, i=i, c=c, dstt=dstt: e.dma_start(out=dstt[:], in_=self.s5row_d[:, i * 4096 + c * RW:i * 4096 + (c + 1) * RW]),
                      reads=["s5in", "s5dtt", "s5sre", "s5sim", "s5den", "s5t"], writes=["s5row"])
            P.dma("sp", lambda e, c=c: e.dma_start(out=braw[:], in_=self.s5bt_d[c]), reads=[("bt1", 0), ("bt1", 1), ("bt2", 0), ("bt2", 1)],
                  writes=["s5bt"])
            P.op("dve", lambda e: e.tensor_scalar(out=r_lre[:], in0=r_lre[:], scalar1=-1e-4, scalar2=None, op0=ALU.min), reads=["s5row"], writes=["s5row"])
            prep_in(r_ld[:], r_li[:], r_lre[:], r_lim[:], r_ldt[:], r_dt[:])
            self.cexp(1, r_ld[:], r_li[:], lr[:], li_[:], [R[12][:], R[13][:], R[14][:], R[15][:]], rit, RW)
            P.op("dve", lambda e: e.tensor_scalar(out=lr[:], in0=lr[:], scalar1=-1.0, scalar2=None, op0=ALU.add), reads=["s5tab"], writes=["s5tab"])
            P.op("dve", lambda e: e.tensor_tensor(out=den[:], in0=r_lre[:], in1=r_lre[:], op=ALU.mult), reads=["s5row"], writes=["s5den"])
            P.op("dve", lambda e: e.tensor_tensor(out=t_[:], in0=r_lim[:], in1=r_lim[:], op=ALU.mult), reads=["s5row"], writes=["s5t"])
            P.op("dve", lambda e: e.tensor_tensor(out=den[:], in0=den[:], in1=t_[:], op=ALU.add), reads=["s5den", "s5t"], writes=["s5den"])
            P.op("dve", lambda e: e.reciprocal(out=den[:], in_=den[:]), reads=["s5den"], writes=["s5den"])
            P.op("dve", lambda e: e.tensor_tensor(out=sre[:], in0=lr[:], in1=r_lre[:], op=ALU.mult), reads=["s5tab", "s5row"], writes=["s5sre"])
            P.op("dve", lambda e: e.tensor_tensor(out=t_[:], in0=li_[:], in1=r_lim[:], op=ALU.mult), reads=["s5tab", "s5row", "s5den"], writes=["s5t"])
            P.op("dve", lambda e: e.tensor_tensor(out=sre[:], in0=sre[:], in1=t_[:], op=ALU.add), reads=["s5sre", "s5t"], writes=["s5sre"])
            P.op("dve", lambda e: e.tensor_tensor(out=sre[:], in0=sre[:], in1=den[:], op=ALU.mult), reads=["s5sre", "s5den"], writes=["s5sre"])
            P.op("dve", lambda e: e.tensor_tensor(out=sim[:], in0=li_[:], in1=r_lre[:], op=ALU.mult), reads=["s5tab", "s5row"], writes=["s5sim"])
            P.op("dve", lambda e: e.tensor_tensor(out=t_[:], in0=lr[:], in1=r_lim[:], op=ALU.mult), reads=["s5tab", "s5row", "s5sre"], writes=["s5t"])
            P.op("dve", lambda e: e.tensor_tensor(out=sim[:], in0=sim[:], in1=t_[:], op=ALU.subtract), reads=["s5sim", "s5t"], writes=["s5sim"])
            P.op("dve", lambda e: e.tensor_tensor(out=sim[:], in0=sim[:], in1=den[:], op=ALU.mult), reads=["s5sim", "s5den"], writes=["s5sim"])
            for gl in range(8):
                g = c * 8 + gl
                bre = braw[:, gl * 64:(gl + 1) * 64]
                bim = braw[:, 512 + gl * 64:512 + (gl + 1) * 64]
                sr = sre[:, gl * 64:(gl + 1) * 64]
                si = sim[:, gl * 64:(gl + 1) * 64]
                i2 = g % 2
                eng2 = "dve" if g % 2 == 0 else "pool"
                for (o0, a_, b_, op2) in ((0, sr, si, ALU.subtract), (64, si, sr, ALU.add)):
                    P.op(eng2, lambda e, a_=a_, bre=bre, i2=i2: e.tensor_tensor(out=bt1[i2][:], in0=bre, in1=a_, op=ALU.mult),
                         reads=["s5bt", "s5sre", "s5sim"], writes=[("bt1", i2)])
                    P.op(eng2, lambda e, b_=b_, bim=bim, i2=i2: e.tensor_tensor(out=bt2[i2][:], in0=bim, in1=b_, op=ALU.mult),
                         reads=["s5bt", "s5sre", "s5sim"], writes=[("bt2", i2)])
                    P.op(eng2, lambda e, g=g, o0=o0, op2=op2, i2=i2: e.tensor_tensor(out=BTs[:, g, o0:o0 + 64], in0=bt1[i2][:], in1=bt2[i2][:], op=op2),
                         reads=[("bt1", i2), ("bt2", i2)], writes=[("BTs", g)])
        P.dma("pool", lambda e: e.dma_start(out=Cc.rearrange("p g h -> p (g h)"), in_=self.s5c_d[:, :]), writes=["Cc"])
        P.op("dve", lambda e: e.tensor_scalar(out=Cc[64:128, :, :], in0=Cc[64:128, :, :], scalar1=-1.0, scalar2=None, op0=ALU.mult),
             reads=["Cc"], writes=["Cc"])
        P.op("dve", lambda e: e.memset(Cpad.rearrange("p g c -> p (g c)"), 0.0), writes=[("Cpad", i) for i in range(8)])
        self.reset_arena(mark)
        Z = [self.carve(S) for _ in range(2)]
        Zb = [[self.carve(S // 2, BF16) for _ in range(2)] for _ in range(2)]
        MT = [[self.carve(64, BF16) for _ in range(NPW)] for _ in range(2)]
        T1 = [self.carve(128) for _ in range(2)]
        ytmp = [self.carve(TW) for _ in range(2)]
        self.bank_mod = 4
        self.bank_i = 0
        for g in range(64):
            c, gl = g // 8, g % 8
            gi = g % 2
            z, zb, mt = Z[gi], Zb[gi], MT[gi]
            for w in range(NPW):
                t1 = T1[w % 2]
                P.op("pool", lambda e, t1=t1, w=w, g=g: e.tensor_scalar(out=t1[:], in0=ident, scalar1=AR[:, w, g:g + 1], scalar2=None, op0=ALU.mult),
                     reads=["cst", "s5tab"], writes=[("T1", w % 2)])
                P.op("dve", lambda e, t1=t1, w=w, g=g, mt=mt: e.scalar_tensor_tensor(out=mt[w][:], in0=jt, scalar=AI[:, w, g:g + 1], in1=t1[:],
                                                                                      op0=ALU.mult, op1=ALU.add),
                     reads=["cst", "s5tab", ("T1", w % 2)], writes=[("MT", gi, w)])
            P.op("pool", lambda e, gl=gl, g=g: e.tensor_copy(out=Cpad[:, gl, 16 * gl:16 * gl + 16], in_=Cc[:, g, :]),
                 reads=["Cc"], writes=[("Cpad", gl)])
            for n in range(NT):
                sl = slice(n * TW, (n + 1) * TW)
                b = self.bank()
                P.op("pe", lambda e, b=b, g=g, c=c, sl=sl: e.matmul(self.psb[b][:], lhsT=BTs[:, g, :], rhs=xn[:, c, sl], start=True, stop=True),
                     reads=[("BTs", g), ("xn", c)], writes=[("ps", b)])
                P.op("act", lambda e, b=b, z=z, sl=sl: e.activation(out=z[:, sl], in_=self.psb[b][:], func=AF.Copy),
                     reads=[("ps", b)], writes=[("Z", gi, n)])
                P.op("act", lambda e, b=b, zb=zb, sl=sl: e.activation(out=zb[0][:, sl], in_=self.psb[b][:], func=AF.Copy),
                     reads=[("ps", b)], writes=[("Zb", gi, 0)])
            for j in range(6):
                sstep = 4 ** j
                src, dst = zb[j % 2], zb[(j + 1) % 2]
                for n in range(NT):
                    lo1 = max(n * TW, sstep)
                    hi = (n + 1) * TW
                    if lo1 >= hi:
                        continue
                    b = self.bank()
                    mms = [(m, max(n * TW, m * sstep)) for m in (1, 2, 3) if m * sstep < hi and m * sstep < S and m * sstep in POWS]
                    for ii, (m, lo) in enumerate(mms):
                        w = POWS.index(m * sstep)
                        sh = m * sstep
                        P.op("pe", lambda e, b=b, w=w, lo=lo, hi=hi, sh=sh, n=n, ii=ii, src=src, mt=mt, last=(ii == len(mms) - 1): e.matmul(
                            self.psb[b][:, lo - n * TW:hi - n * TW], lhsT=mt[w][:], rhs=src[:, lo - sh:hi - sh], start=(ii == 0), stop=last),
                            reads=[("MT", gi, w), ("Zb", gi, j % 2)], writes=[("ps", b)])
                    P.op("dve", lambda e, b=b, z=z, lo1=lo1, hi=hi, n=n: e.tensor_tensor(out=z[:, lo1:hi], in0=z[:, lo1:hi],
                                                                                      in1=self.psb[b][:, lo1 - n * TW:hi - n * TW], op=ALU.add),
                         reads=[("ps", b), ("Z", gi, n)], writes=[("Z", gi, n)])
                P.op("act", lambda e, z=z, dst=dst: e.activation(out=dst[:], in_=z[:], func=AF.Copy),
                     reads=[("Z", gi, n) for n in range(NT)], writes=[("Zb", gi, (j + 1) % 2)])
            zfin = zb[0]
            for n in range(NT):
                sl = slice(n * TW, (n + 1) * TW)
                P.op("pe", lambda e, n=n, gl=gl, zfin=zfin, sl=sl: e.matmul(self.psb[4 + n][:], lhsT=Cpad[:, gl, :], rhs=zfin[:, sl],
                                                                          start=(gl == 0), stop=(gl == 7)),
                     reads=[("Cpad", gl), ("Zb", gi, 0)], writes=[("ps", 4 + n)])
            if gl == 7:
                for n in range(NT):
                    sl = slice(n * TW, (n + 1) * TW)
                    yt = ytmp[n % 2]
                    P.op("dve", lambda e, n=n, c=c, sl=sl, yt=yt: e.scalar_tensor_tensor(out=yt[:], in0=xn[:, c, sl], scalar=self.v("s5_d", c),
                                                                                         in1=self.psb[4 + n][:], op0=ALU.mult, op1=ALU.add),
                         reads=[("ps", 4 + n), ("xn", c), "vecs"], writes=[("ytmp", n % 2)])
                    P.op("act", lambda e, c=c, sl=sl, yt=yt: e.activation(out=xn[:, c, sl], in_=yt[:], func=AF.Gelu),
                         reads=[("ytmp", n % 2)], writes=[("xn", c)])
        self.bank_mod = 8
        vt = [self.carve(TW) for _ in range(2)]
        gt = [self.carve(TW) for _ in range(2)]
        xn_rhs = lambda k, n: (xn[:, k, n * TW:(n + 1) * TW], [("xn", k)])
        cnt = [0]
        for m in range(NCH):
            sv, sg = self.next_block(), self.next_block()
            for n in range(NT):
                i = cnt[0] % 2
                cnt[0] += 1
                sl = slice(n * TW, (n + 1) * TW)

                def evac_v(b, n, i=i, m=m):
                    P.op("act", lambda e: e.activation(out=vt[i][:], in_=self.psb[b][:], func=AF.Identity, bias=self.v("s5_b_out", m)),
                         reads=[("ps", b), "vecs"], writes=[("vt", i)])

                def evac_g(b, n, i=i, m=m):
                    P.op("act", lambda e: e.activation(out=gt[i][:], in_=self.psb[b][:], func=AF.Sigmoid, bias=self.v("s5_b_out", 8 + m)),
                         reads=[("ps", b), "vecs"], writes=[("gt", i)])
                self.linear_tile(sv, xn_rhs, NCH, n, evac_v)
                self.linear_tile(sg, xn_rhs, NCH, n, evac_g)
                P.op("dve", lambda e, i=i: e.tensor_tensor(out=vt[i][:], in0=vt[i][:], in1=gt[i][:], op=ALU.mult),
                     reads=[("vt", i), ("gt", i)], writes=[("vt", i)])
                P.op("dve", lambda e, i=i, m=m, sl=sl: e.tensor_tensor(out=self.xT[:, m, sl], in0=self.xT[:, m, sl], in1=vt[i][:], op=ALU.add),
                     reads=[("vt", i), ("xT", m, n)], writes=[("xT", m, n)])
            self.done_block(sv)
            self.done_block(sg)

    def sb_mixer(self, l):
        P = self.P
        xn = self.norm_xn("norm_mix_g", l)
        cm = self.carve(4352 // 2, BF16)
        m01 = [cm[:, i * 512:(i + 1) * 512] for i in range(4)]
        negm = [cm[:, 2048 + i * 512:2048 + (i + 1) * 512] for i in range(4)]
        negtri = cm[:, 4096:4224]
        identb = cm[:, 4224:4352]
        for (a, b_) in ((0, 2048), (2048, 4096), (4096, 4352)):
            P.dma("pool", lambda e, a=a, b_=b_: e.dma_start(out=cm[:, a:b_], in_=self.m_d[:, a:b_]), writes=["cm"])
        ones2 = self.carve(64, BF16)
        negones = self.carve(64, BF16)
        P.op("dve", lambda e: e.memset(ones2[:], 0.0), writes=["ones2"])
        P.op("dve", lambda e: e.memset(ones2[0:64, 0:64], 1.0), reads=["ones2"], writes=["ones2"])
        P.op("dve", lambda e: e.memset(ones2[64:128, 64:128], 1.0), reads=["ones2"], writes=["ones2"])
        P.op("dve", lambda e: e.memset(negones[:], -1.0), writes=["negones"])
        qT = self.carve(2 * S // 2, BF16).rearrange("p (c s) -> p c s", c=2)
        kT = self.carve(2 * S // 2, BF16).rearrange("p (c s) -> p c s", c=2)
        vtm = self.carve(16 * 256 // 2, BF16).rearrange("p (t j) -> p t j", t=16)
        oT = self.carve(2 * S // 2, BF16).rearrange("p (c s) -> p c s", c=2)
        qraw = [self.carve(TW) for _ in range(2)]
        sqb = [self.carve(TW // 2, BF16) for _ in range(2)]
        rq = [self.carve(TW) for _ in range(2)]
        Eb = [self.carve(TW) for _ in range(2)]
        Lb = [self.carve(TW // 2, BF16) for _ in range(3)]
        att = [self.carve(TW // 2, BF16) for _ in range(2)]
        Lacc = self.carve(TW // 2, BF16)
        xn_rhs = lambda k, n: (xn[:, k, n * TW:(n + 1) * TW], [("xn", k)])
        cnt = [0]
        for qq in range(4):
            for (dstT, gap, dname) in ((qT, self.gqs[:, 0:1], "qT"), (kT, self.v("sb_k_g", 0), "kT")):
                for cc in range(2):
                    slot = self.next_block()

                    def evac_qk(b, n, cc=cc, dstT=dstT, gap=gap, dname=dname):
                        i = cnt[0] % 2
                        cnt[0] += 1
                        sl = slice(n * TW, (n + 1) * TW)
                        P.op("act", lambda e: e.activation(out=qraw[i][:], in_=self.psb[b][:], func=AF.Copy),
                             reads=[("ps", b)], writes=[("qraw", i)])
                        P.op("act", lambda e: e.activation(out=sqb[i][:], in_=self.psb[b][:], func=AF.Square),
                             reads=[("ps", b)], writes=[("sqb", i)])
                        b2 = self.bank()
                        P.op("pe", lambda e: e.matmul(self.psb[b2][:], lhsT=ones2[:], rhs=sqb[i][:], start=True, stop=True),
                             reads=["ones2", ("sqb", i)], writes=[("ps", b2)])
                        P.op("act", lambda e: e.activation(out=rq[i][:], in_=self.psb[b2][:], func=AF.Sqrt, scale=1.0 / 64, bias=EPS),
                             reads=[("ps", b2)], writes=[("rq", i)])
                        P.op("dve", lambda e: e.reciprocal(out=rq[i][:], in_=rq[i][:]), reads=[("rq", i)], writes=[("rq", i)])
                        P.op("dve", lambda e: e.scalar_tensor_tensor(out=dstT[:, cc, sl], in0=qraw[i][:], scalar=gap, in1=rq[i][:],
                                                                     op0=ALU.mult, op1=ALU.mult),
                             reads=[("qraw", i), ("rq", i), "vecs", "gqs"], writes=[(dname, cc, n)])
                    for n in range(NT):
                        self.linear_tile(slot, xn_rhs, NCH, n, evac_qk)
                    self.done_block(slot)
            sva, svb = self.next_block(), self.next_block()
            for tb in range(16):
                b = self.bank()
                for k in range(NCH):
                    slot = sva if k < 4 else svb
                    P.op("pe", lambda e, b=b, k=k, slot=slot, tb=tb: e.matmul(
                        self.psb[b][:, 0:256], lhsT=xn[:, k, tb * 128:(tb + 1) * 128], rhs=self.ring[slot][:, (k % 4) * 256:(k % 4 + 1) * 256],
                        start=(k == 0), stop=(k == NCH - 1)), reads=[("ring", slot), ("xn", k)], writes=[("ps", b)])
                P.op("act", lambda e, b=b, tb=tb: e.activation(out=vtm[:, tb, :], in_=self.psb[b][:, 0:256], func=AF.Copy),
                     reads=[("ps", b)], writes=[("vtm", tb)])
            self.done_block(sva)
            self.done_block(svb)
            self.bank_mod = 6
            self.bank_i = 0
            ui = 0
            for hq in range(SB_HEADS_DBG):
                cq, pb = hq // 2, 64 * (hq % 2)
                for qb in range(4):
                    pob = 6 + (hq * 4 + qb) % 2
                    po = self.psb[pob]
                    kbs = list(range(4 * qb + 3, -1, -1))
                    for ii, kb in enumerate(kbs):
                        first, last = ii == 0, kb == 0
                        diag = kb >= 4 * qb
                        oi = kb - 4 * qb
                        b = self.bank()
                        ps = self.psb[b]
                        e_i, l_i, a_i = ui % 2, ui % 3, ui % 2
                        ui += 1
                        P.op("pe", lambda e, ps=ps, kb=kb, qb=qb, cq=cq, pb=pb: e.matmul(
                            ps[:], lhsT=kT[pb:pb + 64, cq, kb * 128:(kb + 1) * 128], rhs=qT[pb:pb + 64, cq, qb * TW:(qb + 1) * TW],
                            start=True, stop=False),
                            reads=[("kT", cq, kb // 4), ("qT", cq, qb)], writes=[("ps", b)])
                        P.op("act", lambda e, ps=ps, e_i=e_i: e.activation(out=Eb[e_i][:], in_=ps[:], func=AF.Exp),
                             reads=[("ps", b)], writes=[("Eb", e_i)])
                        P.op("act", lambda e, e_i=e_i, l_i=l_i: e.activation(out=Lb[l_i][:], in_=Eb[e_i][:], func=AF.Ln, bias=1.0),
                             reads=[("Eb", e_i)], writes=[("Lb", l_i)])
                        if diag:
                            P.op("dve", lambda e, l_i=l_i, oi=oi: e.tensor_tensor(out=Lb[l_i][:], in0=Lb[l_i][:], in1=m01[oi], op=ALU.mult),
                                 reads=[("Lb", l_i), "cm"], writes=[("Lb", l_i)])
                        more = (not first) or diag
                        P.op("pe", lambda e, ps=ps, l_i=l_i, more=more: e.matmul(ps[:], lhsT=negtri, rhs=Lb[l_i][:], start=False, stop=not more),
                             reads=["cm", ("Lb", l_i)], writes=[("ps", b)])
                        if not first:
                            P.op("pe", lambda e, ps=ps, diag=diag: e.matmul(ps[:], lhsT=negones[:], rhs=Lacc[:], start=False, stop=not diag),
                                 reads=["negones", "Lacc"], writes=[("ps", b)])
                        if diag:
                            P.op("pe", lambda e, ps=ps, oi=oi: e.matmul(ps[:], lhsT=identb, rhs=negm[oi], start=False, stop=True),
                                 reads=["cm"], writes=[("ps", b)])
                        P.op("act", lambda e, ps=ps, a_i=a_i: e.activation(out=att[a_i][:], in_=ps[:], func=AF.Exp),
                             reads=[("ps", b)], writes=[("att", a_i)])
                        P.op("pe", lambda e, po=po, kb=kb, hq=hq, pb=pb, a_i=a_i, first=first, last=last: e.matmul(
                            po[pb:pb + 64, :], lhsT=vtm[:, kb, hq * 64:(hq + 1) * 64], rhs=att[a_i][:], start=first, stop=last),
                            reads=[("vtm", kb), ("att", a_i)], writes=[("ps", pob)])
                        if not last:
                            if first:
                                P.op("dve", lambda e, l_i=l_i: e.tensor_copy(out=Lacc[:], in_=Lb[l_i][:]),
                                     reads=[("Lb", l_i)], writes=["Lacc"])
                            else:
                                P.op("dve", lambda e, l_i=l_i: e.tensor_tensor(out=Lacc[:], in0=Lacc[:], in1=Lb[l_i][:], op=ALU.add),
                                     reads=[("Lb", l_i), "Lacc"], writes=["Lacc"])
                    P.op("act", lambda e, po=po, pb=pb, cq=cq, qb=qb: e.activation(out=oT[pb:pb + 64, cq, qb * TW:(qb + 1) * TW],
                                                                                  in_=po[pb:pb + 64, :], func=AF.Copy),
                         reads=[("ps", pob)], writes=[("oT", cq, qb)])
            self.bank_mod = 8
            so0, so1 = self.next_block(), self.next_block()
            for m in range(NCH):
                for n in range(NT):
                    sl = slice(n * TW, (n + 1) * TW)
                    b = self.bank()
                    for k, so in enumerate((so0, so1)):
                        P.op("pe", lambda e, b=b, k=k, so=so, m=m, sl=sl: e.matmul(
                            self.psb[b][:], lhsT=self.ring[so][:, m * 128:(m + 1) * 128], rhs=oT[:, k, sl], start=(k == 0), stop=(k == 1)),
                            reads=[("ring", so), ("oT", k, n)], writes=[("ps", b)])
                    self.resid_add(m)(b, n)
            self.done_block(so0)
            self.done_block(so1)

    def ffn(self, l):
        P = self.P
        xn = self.norm_xn("norm_ffn_g", l)
        hv = [self.carve((2 + S + 2) // 2, BF16) for _ in range(2)]
        hg = [self.carve((2 + S + 2) // 2, BF16) for _ in range(2)]
        cv = [self.carve(S)] * 2
        cg = [self.carve(S)] * 2
        gb = [[self.carve(S // 2, BF16) for _ in range(2)] for _ in range(2)]
        for buf in hv + hg:
            P.op("dve", lambda e, buf=buf: e.memset(buf[:, 0:2], 0.0), writes=[("fh", id(buf), n) for n in range(NT)])
        cwo = self.voff["ffn_conv_w"] + l * 3 * 44
        cbo = self.voff["ffn_conv_b"] + l * 44
        prev = None
        for f in range(NF):
            par = f % 2
            sv, sg, so = self.next_block(), self.next_block(), self.next_block()
            for (slot, h, cc, ch) in ((sv, hv[par], cv[par], f), (sg, hg[par], cg[par], NF + f)):
                blk = self.ring[slot]
                w0 = self.vecs[:, cwo + 0 * 44 + ch:cwo + 0 * 44 + ch + 1]
                w1 = self.vecs[:, cwo + 1 * 44 + ch:cwo + 1 * 44 + ch + 1]
                w2 = self.vecs[:, cwo + 2 * 44 + ch:cwo + 2 * 44 + ch + 1]
                bb = self.vecs[:, cbo + ch:cbo + ch + 1]
                for n in range(NT):
                    sl = slice(n * TW, (n + 1) * TW)
                    b = self.bank()
                    for k in range(NCH):
                        P.op("pe", lambda e, b=b, blk=blk, k=k, sl=sl: e.matmul(
                            self.psb[b][:], lhsT=blk[:, k * 128:(k + 1) * 128], rhs=xn[:, k, sl], start=(k == 0), stop=(k == NCH - 1)),
                            reads=[("ring", slot), ("xn", k)], writes=[("ps", b)])
                    P.op("act", lambda e, b=b, h=h, n=n: e.activation(out=h[:, 2 + n * TW:2 + (n + 1) * TW], in_=self.psb[b][:], func=AF.Copy),
                         reads=[("ps", b)], writes=[("fh", id(h), n)])
                    P.op("act", lambda e, b=b, cc=cc, sl=sl, w2=w2, bb=bb: e.activation(out=cc[:, sl], in_=self.psb[b][:], func=AF.Identity,
                                                                                         scale=w2, bias=bb),
                         reads=[("ps", b), "vecs"], writes=[("fc", id(cc))])
                P.op("dve", lambda e, h=h, cc=cc, w1=w1: e.scalar_tensor_tensor(out=cc[:], in0=h[:, 1:1 + S], scalar=w1, in1=cc[:],
                                                                                op0=ALU.mult, op1=ALU.add),
                     reads=[("fh", id(h), n) for n in range(NT)] + [("fc", id(cc)), "vecs"], writes=[("fc", id(cc))])
                P.op("dve", lambda e, h=h, cc=cc, w0=w0: e.scalar_tensor_tensor(out=cc[:], in0=h[:, 0:S], scalar=w0, in1=cc[:],
                                                                                op0=ALU.mult, op1=ALU.add),
                     reads=[("fh", id(h), n) for n in range(NT)] + [("fc", id(cc)), "vecs"], writes=[("fc", id(cc))])
            self.done_block(sv)
            self.done_block(sg)
            cgp, cvp = cg[par], cv[par]
            P.op("act", lambda e, cgp=cgp: e.activation(out=cgp[:], in_=cgp[:], func=AF.Silu),
                 reads=[("fc", id(cgp))], writes=[("fc", id(cgp))])
            g = gb[(f // 2) % 2][par]
            P.op("dve", lambda e, g=g, cgp=cgp, cvp=cvp: e.tensor_tensor(out=g[:], in0=cgp[:], in1=cvp[:], op=ALU.mult),
                 reads=[("fc", id(cgp)), ("fc", id(cvp))], writes=[("fg", id(g))])
            if par == 0:
                prev = (so, g)
            else:
                so0, g0 = prev
                for m in range(NCH):
                    for n in range(NT):
                        sl = slice(n * TW, (n + 1) * TW)
                        b = self.bank()
                        P.op("pe", lambda e, b=b, m=m, sl=sl, so0=so0, g0=g0: e.matmul(
                            self.psb[b][:], lhsT=self.ring[so0][:, m * 128:(m + 1) * 128], rhs=g0[:, sl], start=True, stop=False),
                            reads=[("ring", so0), ("fg", id(g0))], writes=[("ps", b)])
                        P.op("pe", lambda e, b=b, m=m, sl=sl, so=so, g=g: e.matmul(
                            self.psb[b][:], lhsT=self.ring[so][:, m * 128:(m + 1) * 128], rhs=g[:, sl], start=False, stop=True),
                            reads=[("ring", so), ("fg", id(g))], writes=[("ps", b)])
                        P.op("dve", lambda e, b=b, m=m, sl=sl: e.tensor_tensor(out=self.xT[:, m, sl], in0=self.xT[:, m, sl], in1=self.psb[b][:],
                                                                               op=ALU.add),
                             reads=[("ps", b), ("xT", m, n)], writes=[("xT", m, n)])
                self.done_block(so0)
                self.done_block(so)

    def build(self):
        self.setup()
        for s in range(self.nseq):
            self.load_x(s)
            for l in self.layers:
                m = l % 4
                if m == 0:
                    self.pool_mixer(l)
                elif m == 1:
                    self.s5_mixer(l)
                elif m == 2:
                    self.lru_mixer(l)
                elif m == 3:
                    self.sb_mixer(l)
                self.ffn(l)
            self.store_x(s)
        assert self.blk_i == self.blk_total, (self.blk_i, self.blk_total)
        return self.P.finalize()


LAYERS = [0, 1, 2, 3]
NSEQ = 2
NCORES = 8


def run(inputs, layers=LAYERS, nseq=NSEQ, ncores=NCORES, trace=False):
    x = np.asarray(inputs["x"], np.float32)
    V = prep_vecs(inputs)
    C = prep_consts()
    vecs = V.pack()
    consts = C.pack()
    wts = prep_weights(inputs, layers)
    cmask = prep_cmask()
    s5row, s5bt, s5c = prep_s5(inputs)
    bld = Builder(layers, nseq, wts.shape[0], V.off, V.n, C.off, C.n)
    nc = bld.build()
    in_maps = []
    for core in range(ncores):
        xs = x[core * nseq:(core + 1) * nseq]
        xt = np.ascontiguousarray(xs.reshape(nseq, S, NCH, 128).transpose(0, 3, 2, 1))
        in_maps.append({"xT": xt, "wts": wts, "vecs": vecs, "consts": consts, "cmask": cmask,
                        "s5row": s5row, "s5bt": s5bt, "s5c": s5c})
    res = run_bass_kernel_spmd(nc, in_maps, core_ids=list(range(ncores)), trace=trace)
    outs = []
    for core in range(ncores):
        o = res.results[core]["outT"]
        outs.append(np.ascontiguousarray(o.transpose(0, 3, 2, 1)).reshape(nseq, S, D))
    return np.concatenate(outs, axis=0).astype(np.float32), res, bld


def kernel(**inputs):
    out, _, _ = run(inputs)
    return out
```

```python
import contextlib
import math
import numpy as np
import concourse.bass as bass
import concourse.mybir as mybir
from concourse.bass_utils import run_bass_kernel_spmd

F32 = mybir.dt.float32
BF16 = mybir.dt.bfloat16
I32 = mybir.dt.int32
AF = mybir.ActivationFunctionType
ALU = mybir.AluOpType

ENGS = ["pe", "act", "dve", "pool", "sp"]
NDMASEM = 8
SAME_ENGINE_WAITS = True

D = 1024
S = 2048
NCH = 8
FH = 2816
NF = 22
EPS = 1e-6
NT = 4
TW = 512
RING = 12
ARENA = 29200
POOL_WINDOWS = (2, 4, 8, 16)
SB_HEADS_DBG = 4
S5_DBG = [1, 1, 1]


class Prog:
    def __init__(self):
        self.nc = bass.Bass("TRN2", target_bir_lowering=False)
        self.es = contextlib.ExitStack()
        self.ops = {e: [] for e in ENGS}
        self.lastw = {}
        self.readers = {}
        self.known = {e: {} for e in ENGS}
        self.ndma = {e: 0 for e in ENGS}
        self.dma_last = {}
        self.last_compute = {}
        self.fence_toks = []
        self.n_alloc = 0

    def sb(self, shape, dt, name=None):
        self.n_alloc += 1
        return self.es.enter_context(self.nc.sbuf_tensor(name or f"sb{self.n_alloc}", list(shape), dt))

    def ps(self, shape, dt=F32, name=None):
        self.n_alloc += 1
        return self.es.enter_context(self.nc.psum_tensor(name or f"ps{self.n_alloc}", list(shape), dt))

    def dram(self, name, shape, dt, kind):
        return self.nc.dram_tensor(name, list(shape), dt, kind=kind).ap()

    def _deps(self, reads, writes):
        deps = []
        for k in reads:
            t = self.lastw.get(k)
            if t is not None:
                deps.append(t)
        for k in writes:
            t = self.lastw.get(k)
            if t is not None:
                deps.append(t)
            deps.extend(self.readers.get(k, ()))
        deps.extend(self.fence_toks)
        return deps

    def _commit(self, tok, reads, writes):
        for k in reads:
            self.readers.setdefault(k, []).append(tok)
        for k in writes:
            self.lastw[k] = tok
            self.readers[k] = []

    def _waits(self, eng, deps):
        waits = []
        kn = self.known[eng]
        for t in deps:
            if t[0] == "eng":
                _, e, idx = t
                if e == eng and (eng == "pe" or not SAME_ENGINE_WAITS):
                    continue
                if kn.get(e, -1) >= idx:
                    continue
                kn[e] = idx
                self.ops[e][idx]["marked"] = True
                waits.append(t)
            else:
                _, q, slot, val = t
                key = ("dma", q, slot)
                if kn.get(key, 0) >= val:
                    continue
                kn[key] = val
                waits.append(t)
        return waits

    def op(self, eng, fn, reads=(), writes=()):
        deps = self._deps(reads, writes)
        waits = self._waits(eng, deps)
        idx = len(self.ops[eng])
        self.ops[eng].append(dict(fn=fn, waits=waits, marked=False, dma=None))
        self.last_compute[eng] = idx
        self._commit(("eng", eng, idx), reads, writes)

    def dma(self, eng, fn, reads=(), writes=()):
        deps = self._deps(reads, writes)
        k = self.ndma[eng]
        self.ndma[eng] += 1
        slot = k % NDMASEM
        val = 16 * (k // NDMASEM + 1)
        if k >= NDMASEM:
            deps.append(("dma", eng, slot, val - 16))
        waits = self._waits(eng, deps)
        tok = ("dma", eng, slot, val)
        self.ops[eng].append(dict(fn=fn, waits=waits, marked=False, dma=(slot, val)))
        self.dma_last[(eng, slot)] = val
        self._commit(tok, reads, writes)

    def fence(self):
        self.fence_toks = [("eng", e, i) for e, i in self.last_compute.items()]

    def finalize(self):
        nc = self.nc
        deps = [("dma", q, s, v) for (q, s), v in self.dma_last.items()]
        waits = self._waits("sp", deps)
        self.ops["sp"].append(dict(fn=None, waits=waits, marked=False, dma=None))
        EPOCH = 1000
        nmark = {e: sum(o["marked"] for o in self.ops[e]) for e in ENGS}
        sems = {e: [self.es.enter_context(nc.semaphore(f"s_{e}{j}")) for j in range(nmark[e] // EPOCH + 1)] for e in ENGS}
        dsems = {}
        for e in ENGS:
            for s in range(min(NDMASEM, self.ndma[e])):
                dsems[(e, s)] = self.es.enter_context(nc.semaphore(f"d_{e}{s}"))
        semval = {}
        for e in ENGS:
            c = 0
            for i, o in enumerate(self.ops[e]):
                if o["marked"]:
                    semval[(e, i)] = (c // EPOCH, c % EPOCH + 1)
                    c += 1
        self.stats = {e: (len(self.ops[e]), sum(len(o["waits"]) for o in self.ops[e])) for e in ENGS}

        def emit(e, eng):
            for i, o in enumerate(self.ops[e]):
                for t in o["waits"]:
                    if t[0] == "eng":
                        ep, val = semval[(t[1], t[2])]
                        eng.wait_ge(sems[t[1]][ep], val)
                    else:
                        eng.wait_ge(dsems[(t[1], t[2])], t[3])
                if o["fn"] is None:
                    continue
                inst = o["fn"](eng)
                if o["dma"] is not None:
                    inst.then_inc(dsems[(e, o["dma"][0])], 16)
                elif o["marked"]:
                    inst.then_inc(sems[e][semval[(e, i)][0]], 1)

        with nc.Block() as block:
            @block.tensor
            def _(eng):
                emit("pe", eng)

            @block.scalar
            def _(eng):
                emit("act", eng)

            @block.vector
            def _(eng):
                emit("dve", eng)

            @block.gpsimd
            def _(eng):
                emit("pool", eng)

            @block.sync
            def _(eng):
                emit("sp", eng)
        self.es.close()
        return nc


def chan_cols(v):
    v = np.asarray(v, np.float32)
    return np.ascontiguousarray(v.reshape(-1, 128).T)


class VecLayout:
    def __init__(self):
        self.off = {}
        self.cols = []
        self.n = 0

    def add(self, name, arr):
        arr = np.asarray(arr, np.float32)
        assert arr.shape[0] == 128
        arr = arr.reshape(128, -1)
        self.off[name] = self.n
        self.cols.append(arr)
        self.n += arr.shape[1]

    def pack(self):
        return np.ascontiguousarray(np.concatenate(self.cols, axis=1))


def prep_vecs(inp):
    V = VecLayout()
    V.add("norm_mix_g", np.concatenate([chan_cols(inp["norm_mix_g"][l]) for l in range(4)], axis=1))
    V.add("norm_ffn_g", np.concatenate([chan_cols(inp["norm_ffn_g"][l]) for l in range(4)], axis=1))
    V.add("pool_b", chan_cols(inp["pool_b"][0]))
    V.add("pool_scale", chan_cols(inp["pool_scale"][0]))
    V.add("ffn_conv_w", np.concatenate([chan_cols(inp["ffn_conv_w"][l, k]) for l in range(4) for k in range(3)], axis=1))
    V.add("ffn_conv_b", np.concatenate([chan_cols(inp["ffn_conv_b"][l]) for l in range(4)], axis=1))
    V.add("lru_conv_w", np.concatenate([chan_cols(inp["lru_conv_w"][0, k]) for k in range(4)], axis=1))
    for nm in ("lru_conv_b", "lru_b_a", "lru_b_x", "lru_lam"):
        V.add(nm, chan_cols(inp[nm][0]))
    lre, lim, ldt = (np.asarray(inp[k][0], np.float32) for k in ("s5_lam_re", "s5_lam_im", "s5_log_dt"))
    V.add("s5_lre", np.tile(lre.T, (2, 1)))
    V.add("s5_lim", np.tile(lim.T, (2, 1)))
    V.add("s5_ldt", np.broadcast_to(ldt.reshape(1, 64), (128, 64)))
    V.add("s5_d", chan_cols(inp["s5_d"][0]))
    V.add("s5_b_out", chan_cols(inp["s5_b_out"][0]))
    V.add("sb_q_g", np.tile(np.asarray(inp["sb_q_g"][0], np.float32), 2).reshape(128, 1))
    V.add("sb_k_g", np.tile(np.asarray(inp["sb_k_g"][0], np.float32), 2).reshape(128, 1))
    return V


def prep_consts():
    C = VecLayout()
    C.add("ident", np.eye(128, dtype=np.float32))
    rc = np.zeros((4, 16), np.float32)
    for wi, w in enumerate(POOL_WINDOWS):
        for t in range(16):
            rc[wi, t] = 1.0 / min(t + 1, w)
    C.add("pool_rc", np.broadcast_to(rc.reshape(1, 64), (128, 64)))
    jt = np.zeros((128, 128), np.float32)
    for k in range(64):
        jt[k, k + 64] = 1.0
        jt[k + 64, k] = -1.0
    C.add("jt", jt)
    return C


def prep_cmask():
    p = np.arange(128)[:, None]
    j = np.arange(512)[None, :]
    m01 = np.concatenate([(j - p > off).astype(np.float32) for off in (0, 128, 256, 384)], axis=1)
    neg = (1.0 - m01) * -30000.0
    jj = np.arange(128)[:, None]
    ss = np.arange(128)[None, :]
    negtri = -(jj >= ss).astype(np.float32)
    ident = np.eye(128, dtype=np.float32)
    return np.ascontiguousarray(np.concatenate([m01, neg, negtri, ident], axis=1))


def prep_s5(inp):
    lre, lim, ldt = (np.asarray(inp[k][0], np.float32) for k in ("s5_lam_re", "s5_lam_im", "s5_log_dt"))
    row = np.stack([lre.reshape(-1), lim.reshape(-1), np.repeat(ldt, 64)], axis=0)
    s5row = np.ascontiguousarray(np.broadcast_to(row.reshape(1, 3 * 4096), (128, 3 * 4096)))
    bre, bim = np.asarray(inp["s5_b_re"][0], np.float32), np.asarray(inp["s5_b_im"][0], np.float32)
    bt = np.zeros((8, 128, 2, 8, 64), np.float32)
    for g in range(64):
        c, gl = g // 8, g % 8
        bt[c, 16 * gl:16 * gl + 16, 0, gl, :] = bre[g].T
        bt[c, 16 * gl:16 * gl + 16, 1, gl, :] = bim[g].T
    cre, cim = np.asarray(inp["s5_c_re"][0], np.float32), np.asarray(inp["s5_c_im"][0], np.float32)
    cc = np.concatenate([cre.transpose(2, 0, 1), cim.transpose(2, 0, 1)], axis=0)
    return s5row, np.ascontiguousarray(bt.reshape(8, 128, 1024)), np.ascontiguousarray(cc.reshape(128, 1024))


def wblock_v(W, qq, half):
    blk = W[half * 512:(half + 1) * 512, 2 * D + qq * 256:2 * D + (qq + 1) * 256]
    return np.ascontiguousarray(blk.reshape(4, 128, 256).transpose(1, 0, 2).reshape(128, 1024))


def wblock_in(W, col0, kc=8):
    blk = W[:kc * 128, col0:col0 + 128].reshape(kc, 128, 128).transpose(1, 0, 2).reshape(128, kc * 128)
    out = np.zeros((128, 1024), np.float32)
    out[:, :kc * 128] = blk
    return out


def prep_weights(inp, layers):
    blocks = []
    for l in layers:
        m = l % 4
        if m == 0:
            pw = inp["pool_w"][0]
            for gi in range(4):
                for mo in range(2):
                    blocks.append(wblock_in(pw[gi], mo * 128, kc=2))
        if m == 2:
            win, wa, wx, wout = inp["lru_w_in"][0], inp["lru_w_a"][0], inp["lru_w_x"][0], inp["lru_w_out"][0]
            for nb in range(4):
                for cc in range(2):
                    blocks.append(wblock_in(win, D + (2 * nb + cc) * 128))
                for mo in range(2):
                    blocks.append(wblock_in(win, (2 * nb + mo) * 128))
                    blocks.append(wblock_in(wa[nb], mo * 128, kc=2))
                    blocks.append(wblock_in(wx[nb], mo * 128, kc=2))
            for mo in range(NCH):
                blocks.append(wblock_in(wout, mo * 128))
        if m == 1:
            w5 = inp["s5_w_out"][0]
            for mo in range(NCH):
                blocks.append(wblock_in(w5, mo * 128))
                blocks.append(wblock_in(w5, D + mo * 128))
        if m == 3:
            wqkv, wo_ = inp["sb_w_qkv"][0], inp["sb_w_o"][0]
            for qq in range(4):
                for c in (2 * qq, 2 * qq + 1):
                    blocks.append(wblock_in(wqkv, c * 128))
                for c in (2 * qq, 2 * qq + 1):
                    blocks.append(wblock_in(wqkv, D + c * 128))
                blocks.append(wblock_v(wqkv, qq, 0))
                blocks.append(wblock_v(wqkv, qq, 1))
                for c in (2 * qq, 2 * qq + 1):
                    blocks.append(np.ascontiguousarray(wo_[c * 128:(c + 1) * 128, :]))
        wi, wo = inp["ffn_w_in"][l], inp["ffn_w_out"][l]
        for f in range(NF):
            blocks.append(wblock_in(wi, f * 128))
            blocks.append(wblock_in(wi, FH + f * 128))
            blocks.append(np.ascontiguousarray(wo[f * 128:(f + 1) * 128, :]))
    return np.ascontiguousarray(np.stack(blocks, axis=0))


class Builder:
    def __init__(self, layers, nseq, nblocks, voff, nv, coff, ncst):
        self.layers = layers
        self.nseq = nseq
        self.voff, self.coff = voff, coff
        P = self.P = Prog()
        self.x_d = P.dram("xT", [nseq, 128, NCH, S], F32, "ExternalInput")
        self.w_d = P.dram("wts", [nblocks, 128, 1024], F32, "ExternalInput")
        self.v_d = P.dram("vecs", [128, nv], F32, "ExternalInput")
        self.c_d = P.dram("consts", [128, ncst], F32, "ExternalInput")
        self.o_d = P.dram("outT", [nseq, 128, NCH, S], F32, "ExternalOutput")
        self.m_d = P.dram("cmask", [128, 4352], F32, "ExternalInput")
        self.s5row_d = P.dram("s5row", [128, 3 * 4096], F32, "ExternalInput")
        self.s5bt_d = P.dram("s5bt", [8, 128, 1024], F32, "ExternalInput")
        self.s5c_d = P.dram("s5c", [128, 1024], F32, "ExternalInput")
        self.bank_mod = 8
        self.nblocks = nblocks
        self.xT = P.sb([128, NCH, S], F32, "xT_sb")
        self.vecs = P.sb([128, nv], F32, "vecs_sb")
        self.cst = P.sb([128, ncst], F32, "cst_sb")
        self.ring = [P.sb([128, 1024], BF16, f"ring{i}") for i in range(RING)]
        self.ones = P.sb([128, 128], BF16, "ones")
        self.bsv = P.sb([128, 8], F32, "bsv")
        self.lsc = P.sb([128, 8], F32, "lsc")
        self.gqs = P.sb([128, 1], F32, "gqs")
        self.arena = P.sb([128, ARENA], F32, "arena")
        self.psb = [P.ps([128, TW], F32, f"psb{i}") for i in range(8)]
        self.bank_i = 0
        self.blk_i = 0
        self.blk_issued = 0
        self.blk_total = nblocks * nseq
        self.arena_off = 0
        self.slot_free = [True] * RING

    def bank(self):
        b = self.bank_i
        self.bank_i = (self.bank_i + 1) % self.bank_mod
        return b

    def v(self, name, col, n=1):
        o = self.voff[name] + col
        return self.vecs[:, o:o + n]

    def c(self, name, col, n=1):
        o = self.coff[name] + col
        return self.cst[:, o:o + n]

    def carve(self, n_f32, dt=F32):
        a = self.arena_off
        self.arena_off += n_f32
        assert self.arena_off <= ARENA, self.arena_off
        ap = self.arena[:, a:a + n_f32]
        if dt == BF16:
            ap = ap.bitcast(BF16)
        return ap

    def reset_arena(self, off=0):
        self.P.fence()
        self.arena_off = off

    def norm_carve(self):
        self.rstd = self.carve(S)
        self.sq = [self.carve(NCH * TW // 2, BF16).rearrange("p (c s) -> p c s", c=NCH) for _ in range(2)]
        self.nrm_tmp = self.carve(TW)

    def try_issue(self):
        while self.blk_issued < self.blk_total and self.slot_free[self.blk_issued % RING]:
            g = self.blk_issued
            self.blk_issued += 1
            slot = g % RING
            self.slot_free[slot] = False
            src = self.w_d[g % self.nblocks]
            dst = self.ring[slot]
            self.P.dma("pool", lambda e, dst=dst, src=src: e.dma_start(out=dst[:], in_=src), writes=[("ring", slot)])

    def next_block(self):
        g = self.blk_i
        self.blk_i += 1
        assert g < self.blk_issued, (g, self.blk_issued)
        return g % RING

    def done_block(self, slot):
        self.slot_free[slot] = True
        self.try_issue()

    def setup(self):
        P = self.P
        P.dma("sp", lambda e: e.dma_start(out=self.vecs[:], in_=self.v_d[:, :]), writes=["vecs"])
        P.dma("sp", lambda e: e.dma_start(out=self.cst[:], in_=self.c_d[:, :]), writes=["cst"])
        P.op("dve", lambda e: e.memset(self.ones[:], 1.0), writes=["ones"])
        if 0 in self.layers:
            P.op("dve", lambda e: e.tensor_tensor(out=self.bsv[:], in0=self.v("pool_b", 0, 8), in1=self.v("pool_scale", 0, 8),
                                                  op=ALU.mult), reads=["vecs"], writes=["bsv"])
        if 3 in self.layers:
            P.op("dve", lambda e: e.tensor_scalar(out=self.gqs[:], in0=self.v("sb_q_g", 0), scalar1=0.125, scalar2=None, op0=ALU.mult),
                 reads=["vecs"], writes=["gqs"])
        if 2 in self.layers:
            P.op("act", lambda e: e.activation(out=self.lsc[:], in_=self.v("lru_lam", 0, 8), func=AF.Exp, scale=-1.0),
                 reads=["vecs"], writes=["lsc"])
            P.op("act", lambda e: e.activation(out=self.lsc[:], in_=self.lsc[:], func=AF.Ln, bias=1.0), reads=["lsc"], writes=["lsc"])
            P.op("dve", lambda e: e.tensor_scalar(out=self.lsc[:], in0=self.lsc[:], scalar1=-8.0, scalar2=None, op0=ALU.mult),
                 reads=["lsc"], writes=["lsc"])
        self.try_issue()

    def load_x(self, s):
        P = self.P
        for c in range(NCH):
            P.dma("sp", lambda e, c=c: e.dma_start(out=self.xT[:, c, :], in_=self.x_d[s, :, c, :]),
                  writes=[("xT", c, n) for n in range(NT)])

    def store_x(self, s):
        P = self.P
        for c in range(NCH):
            P.dma("sp", lambda e, c=c: e.dma_start(out=self.o_d[s, :, c, :], in_=self.xT[:, c, :]),
                  reads=[("xT", c, n) for n in range(NT)])

    def rms_rstd(self):
        P = self.P
        rstd, nrm_tmp = self.rstd, self.nrm_tmp
        for n in range(NT):
            sl = slice(n * TW, (n + 1) * TW)
            sq = self.sq[n % 2]
            for c in range(NCH):
                P.op("act", lambda e, c=c, sq=sq, sl=sl: e.activation(out=sq[:, c, :], in_=self.xT[:, c, sl], func=AF.Square),
                     reads=[("xT", c, n)], writes=[("sq", n % 2, c)])
            b = self.bank()
            for c in range(NCH):
                P.op("pe", lambda e, c=c, sq=sq, b=b: e.matmul(self.psb[b][:], lhsT=self.ones[:], rhs=sq[:, c, :],
                                                               start=(c == 0), stop=(c == NCH - 1)),
                     reads=["ones", ("sq", n % 2, c)], writes=[("ps", b)])
            P.op("act", lambda e, b=b: e.activation(out=nrm_tmp[:], in_=self.psb[b][:], func=AF.Sqrt, scale=1.0 / D, bias=EPS),
                 reads=[("ps", b)], writes=["nrm_tmp"])
            P.op("dve", lambda e, sl=sl: e.reciprocal(out=rstd[:, sl], in_=nrm_tmp[:]),
                 reads=["nrm_tmp"], writes=[("rstd", n)])

    def make_xn(self, xn, gname, l):
        P = self.P
        rstd = self.rstd
        for c in range(NCH):
            g = self.v(gname, l * 8 + c)
            P.op("dve", lambda e, c=c, g=g: e.scalar_tensor_tensor(out=xn[:, c, :], in0=self.xT[:, c, :], scalar=g, in1=rstd[:],
                                                                   op0=ALU.mult, op1=ALU.mult),
                 reads=[("xT", c, n) for n in range(NT)] + [("rstd", n) for n in range(NT)] + ["vecs"],
                 writes=[("xn", c)])

    def norm_xn(self, gname, l):
        self.reset_arena()
        xn = self.carve(NCH * S // 2, BF16).rearrange("p (c s) -> p c s", c=NCH)
        mark = self.arena_off
        self.norm_carve()
        self.rms_rstd()
        self.make_xn(xn, gname, l)
        self.reset_arena(mark)
        return xn

    def pool_mixer(self, l):
        P = self.P
        self.reset_arena()
        self.norm_carve()
        self.rms_rstd()
        rstd = self.rstd
        d = self.carve(NCH * S // 2, BF16)
        H = [self.carve(16 + S) for _ in range(2)]
        SA = [self.carve(16 + S) for _ in range(2)]
        SB = [self.carve(16 + S) for _ in range(2)]
        t16 = self.carve(16)
        for buf in H + SA + SB:
            P.op("dve", lambda e, buf=buf: e.memset(buf[:, 0:16], 0.0), writes=[("pm", id(buf))])
        for c in range(NCH):
            gi = c // 2
            w = POOL_WINDOWS[gi]
            h, sa, sb_ = H[c % 2], SA[c % 2], SB[c % 2]
            g = self.v("norm_mix_g", l * 8 + c)
            P.op("dve", lambda e, c=c, g=g, h=h: e.scalar_tensor_tensor(out=h[:, 16:16 + S], in0=self.xT[:, c, :], scalar=g, in1=rstd[:],
                                                                        op0=ALU.mult, op1=ALU.mult),
                 reads=[("xT", c, n) for n in range(NT)] + [("rstd", n) for n in range(NT)] + ["vecs"], writes=[("pm", id(h))])
            src = h
            sh = 1
            pp = [sa, sb_]
            j = 0
            while sh < w:
                dst = pp[j % 2]
                P.op("dve", lambda e, src=src, dst=dst, sh=sh: e.tensor_tensor(out=dst[:, 16:16 + S], in0=src[:, 16:16 + S],
                                                                               in1=src[:, 16 - sh:16 + S - sh], op=ALU.add),
                     reads=[("pm", id(src))], writes=[("pm", id(dst))])
                src = dst
                sh *= 2
                j += 1
            dc = d[:, c * S:(c + 1) * S]
            P.op("dve", lambda e, src=src, h=h, dc=dc, w=w: e.scalar_tensor_tensor(out=dc, in0=src[:, 16:16 + S], scalar=1.0 / w,
                                                                                   in1=h[:, 16:16 + S], op0=ALU.mult, op1=ALU.subtract),
                 reads=[("pm", id(src)), ("pm", id(h))], writes=[("pd", c)])
            rc = self.c("pool_rc", gi * 16, 16)
            P.op("dve", lambda e, src=src, rc=rc: e.tensor_tensor(out=t16[:], in0=src[:, 16:32], in1=rc, op=ALU.mult),
                 reads=[("pm", id(src)), "cst"], writes=["t16"])
            P.op("dve", lambda e, h=h, dc=dc: e.tensor_tensor(out=dc[:, 0:16], in0=t16[:], in1=h[:, 16:32], op=ALU.subtract),
                 reads=["t16", ("pm", id(h))], writes=[("pd", c)])
        tmp = [self.carve(TW) for _ in range(2)]
        ti = 0
        for gi in range(4):
            for mo in range(2):
                c = 2 * gi + mo
                slot = self.next_block()
                blk = self.ring[slot]
                for n in range(NT):
                    sl = slice(n * TW, (n + 1) * TW)
                    b = self.bank()
                    for k in range(2):
                        kc = 2 * gi + k
                        P.op("pe", lambda e, b=b, blk=blk, k=k, kc=kc, sl=sl: e.matmul(
                            self.psb[b][:], lhsT=blk[:, k * 128:(k + 1) * 128], rhs=d[:, kc * S + sl.start:kc * S + sl.stop],
                            start=(k == 0), stop=(k == 1)),
                            reads=[("ring", slot), ("pd", kc)], writes=[("ps", b)])
                    t = tmp[ti % 2]
                    ti += 1
                    P.op("act", lambda e, b=b, t=t, c=c: e.activation(out=t[:], in_=self.psb[b][:], func=AF.Identity,
                                                                      scale=self.v("pool_scale", c), bias=self.bsv[:, c:c + 1]),
                         reads=[("ps", b), "vecs", "bsv"], writes=[("ptmp", id(t))])
                    P.op("dve", lambda e, t=t, c=c, sl=sl: e.tensor_tensor(out=self.xT[:, c, sl], in0=self.xT[:, c, sl], in1=t[:], op=ALU.add),
                         reads=[("ptmp", id(t)), ("xT", c, n)], writes=[("xT", c, n)])
                self.done_block(slot)

    def linear_tile(self, slot, rhs_fn, nk, n, evac):
        P = self.P
        b = self.bank()
        blk = self.ring[slot]
        for k in range(nk):
            rhs, rkeys = rhs_fn(k, n)
            P.op("pe", lambda e, b=b, blk=blk, k=k, rhs=rhs: e.matmul(self.psb[b][:], lhsT=blk[:, k * 128:(k + 1) * 128], rhs=rhs,
                                                                     start=(k == 0), stop=(k == nk - 1)),
                 reads=[("ring", slot)] + rkeys, writes=[("ps", b)])
        evac(b, n)

    def resid_add(self, m):
        def evac(b, n):
            sl = slice(n * TW, (n + 1) * TW)
            self.P.op("dve", lambda e: e.tensor_tensor(out=self.xT[:, m, sl], in0=self.xT[:, m, sl], in1=self.psb[b][:], op=ALU.add),
                      reads=[("ps", b), ("xT", m, n)], writes=[("xT", m, n)])
        return evac

    def lru_mixer(self, l):
        P = self.P
        xn = self.norm_xn("norm_mix_g", l)
        yb = self.carve(NCH * S // 2, BF16).rearrange("p (c s) -> p c s", c=NCH)
        hc = self.carve(1028, BF16)
        cacc = self.carve(S)
        recb = [self.carve(S // 2, BF16) for _ in range(2)]
        A, B, C = self.carve(S), self.carve(S), self.carve(S)
        Dg = self.carve(S // 2, BF16)
        P.op("dve", lambda e: e.memset(hc[:, 0:3], 0.0), writes=["hc"])
        xn_rhs = lambda k, n: (xn[:, k, n * TW:(n + 1) * TW], [("xn", k)])
        cw = lambda k, c: self.v("lru_conv_w", k * 8 + c)
        for nb in range(4):
            for cc in range(2):
                c = 2 * nb + cc
                slot = self.next_block()

                def evac_rec(b, n, c=c):
                    sl = slice(n * TW, (n + 1) * TW)
                    P.op("act", lambda e: e.activation(out=hc[:, 3 + n * TW:3 + (n + 1) * TW], in_=self.psb[b][:], func=AF.Copy),
                         reads=[("ps", b)], writes=["hc"])
                    P.op("act", lambda e: e.activation(out=cacc[:, sl], in_=self.psb[b][:], func=AF.Identity,
                                                       scale=cw(3, c), bias=self.v("lru_conv_b", c)),
                         reads=[("ps", b), "vecs"], writes=["cacc"])
                for n in range(NT):
                    self.linear_tile(slot, xn_rhs, NCH, n, evac_rec)
                self.done_block(slot)
                for k in (2, 1):
                    P.op("dve", lambda e, k=k, c=c: e.scalar_tensor_tensor(out=cacc[:], in0=hc[:, k:k + S], scalar=cw(k, c), in1=cacc[:],
                                                                           op0=ALU.mult, op1=ALU.add),
                         reads=["hc", "cacc", "vecs"], writes=["cacc"])
                rb = recb[cc]
                P.op("dve", lambda e, c=c, rb=rb: e.scalar_tensor_tensor(out=rb[:], in0=hc[:, 0:S], scalar=cw(0, c), in1=cacc[:],
                                                                         op0=ALU.mult, op1=ALU.add),
                     reads=["hc", "cacc", "vecs"], writes=[("recb", cc)])
            rec_rhs = lambda k, n: (recb[k][:, n * TW:(n + 1) * TW], [("recb", k)])
            for mo in range(2):
                c = 2 * nb + mo
                slot = self.next_block()

                def evac_g(b, n):
                    sl = slice(n * TW, (n + 1) * TW)
                    P.op("act", lambda e: e.activation(out=Dg[:, sl], in_=self.psb[b][:], func=AF.Gelu), reads=[("ps", b)], writes=["Dg"])
                for n in range(NT):
                    self.linear_tile(slot, xn_rhs, NCH, n, evac_g)
                self.done_block(slot)
                for (dst, dkey, bname) in ((A, "lA", "lru_b_a"), (C, "lC", "lru_b_x")):
                    slot = self.next_block()

                    def evac_s(b, n, dst=dst, dkey=dkey, bname=bname, c=c):
                        sl = slice(n * TW, (n + 1) * TW)
                        P.op("act", lambda e: e.activation(out=dst[:, sl], in_=self.psb[b][:], func=AF.Sigmoid, bias=self.v(bname, c)),
                             reads=[("ps", b), "vecs"], writes=[dkey])
                    for n in range(NT):
                        self.linear_tile(slot, rec_rhs, 2, n, evac_s)
                    self.done_block(slot)
                P.op("act", lambda e, c=c: e.activation(out=A[:], in_=A[:], func=AF.Exp, scale=self.lsc[:, c:c + 1]),
                     reads=["lA", "lsc"], writes=["lA"])
                P.op("dve", lambda e: e.tensor_tensor(out=B[:], in0=A[:], in1=A[:], op=ALU.mult), reads=["lA"], writes=["lB"])
                P.op("act", lambda e: e.activation(out=B[:], in_=B[:], func=AF.Sqrt, scale=-1.0, bias=1.0), reads=["lB"], writes=["lB"])
                rb = recb[mo]
                P.op("dve", lambda e, rb=rb: e.tensor_tensor(out=C[:], in0=C[:], in1=rb[:], op=ALU.mult),
                     reads=["lC", ("recb", mo)], writes=["lC"])
                P.op("dve", lambda e: e.tensor_tensor(out=C[:], in0=C[:], in1=B[:], op=ALU.mult), reads=["lC", "lB"], writes=["lC"])
                P.op("dve", lambda e: e.tensor_tensor_scan(out=B[:], data0=A[:], data1=C[:], initial=0.0, op0=ALU.mult, op1=ALU.add),
                     reads=["lA", "lC", "lB"], writes=["lB"])
                P.op("dve", lambda e, c=c: e.tensor_tensor(out=yb[:, c, :], in0=Dg[:], in1=B[:], op=ALU.mult),
                     reads=["Dg", "lB"], writes=[("yb", c)])
        yb_rhs = lambda k, n: (yb[:, k, n * TW:(n + 1) * TW], [("yb", k)])
        for m in range(NCH):
            slot = self.next_block()
            for n in range(NT):
                self.linear_tile(slot, yb_rhs, NCH, n, self.resid_add(m))
            self.done_block(slot)

    def cexp(self, n, lre_dt, lim_dt, out_r, out_i, tmps, itmp, tkey):
        P = self.P
        er, y, f, m = tmps
        kk = ("cx", tkey)
        P.op("act", lambda e: e.activation(out=er, in_=lre_dt, func=AF.Exp, scale=float(n)), reads=["s5in"], writes=[kk + ("er",)])
        for (dst, add) in ((out_i, 0.5), (out_r, 0.75)):
            P.op("dve", lambda e, add=add: e.tensor_scalar(out=y, in0=lim_dt, scalar1=float(n) / (2 * math.pi), scalar2=add,
                                                           op0=ALU.mult, op1=ALU.add), reads=["s5in"], writes=[kk + ("y",)])
            P.op("dve", lambda e: e.tensor_copy(out=itmp, in_=y), reads=[kk + ("y",)], writes=[kk + ("i",)])
            P.op("dve", lambda e: e.tensor_copy(out=f, in_=itmp), reads=[kk + ("i",)], writes=[kk + ("f",)])
            P.op("dve", lambda e: e.tensor_tensor(out=f, in0=y, in1=f, op=ALU.subtract), reads=[kk + ("y",), kk + ("f",)], writes=[kk + ("f",)])
            P.op("dve", lambda e: e.tensor_scalar(out=m, in0=f, scalar1=0.0, scalar2=None, op0=ALU.is_lt), reads=[kk + ("f",)], writes=[kk + ("m",)])
            P.op("dve", lambda e: e.tensor_tensor(out=f, in0=f, in1=m, op=ALU.add), reads=[kk + ("f",), kk + ("m",)], writes=[kk + ("f",)])
            P.op("dve", lambda e: e.tensor_scalar(out=m, in0=f, scalar1=1.0, scalar2=None, op0=ALU.is_ge), reads=[kk + ("f",)], writes=[kk + ("m",)])
            P.op("dve", lambda e: e.tensor_tensor(out=f, in0=f, in1=m, op=ALU.subtract), reads=[kk + ("f",), kk + ("m",)], writes=[kk + ("f",)])
            P.op("dve", lambda e: e.tensor_scalar(out=f, in0=f, scalar1=2 * math.pi, scalar2=-math.pi, op0=ALU.mult, op1=ALU.add),
                 reads=[kk + ("f",)], writes=[kk + ("f",)])
            P.op("dve", lambda e: e.tensor_scalar(out=f, in0=f, scalar1=3.14159, scalar2=-3.14159, op0=ALU.min, op1=ALU.max),
                 reads=[kk + ("f",)], writes=[kk + ("f",)])
            P.op("act", lambda e, dst=dst: e.activation(out=dst, in_=f, func=AF.Sin), reads=[kk + ("f",)], writes=[kk + ("o", add)])
        P.op("dve", lambda e: e.tensor_tensor(out=out_r, in0=out_r, in1=er, op=ALU.mult),
             reads=[kk + ("o", 0.75), kk + ("er",)], writes=[kk + ("o", 0.75), "s5tab"])
        P.op("dve", lambda e: e.tensor_tensor(out=out_i, in0=out_i, in1=er, op=ALU.mult),
             reads=[kk + ("o", 0.5), kk + ("er",)], writes=[kk + ("o", 0.5), "s5tab"])

    def s5_mixer(self, l):
        P = self.P
        xn = self.norm_xn("norm_mix_g", l)
        POWS = [1, 2, 3, 4, 5, 6, 7, 8, 16, 32, 64, 128, 256, 512, 1024]
        NPW = len(POWS)
        AR = self.carve(NPW * 64).rearrange("p (w g) -> p w g", w=NPW)
        AI = self.carve(NPW * 64).rearrange("p (w g) -> p w g", w=NPW)
        BTs = self.carve(64 * 128 // 2, BF16).rearrange("p (g z) -> p g z", g=64)
        Cc = self.carve(1024 // 2, BF16).rearrange("p (g h) -> p g h", g=64)
        Cpad = self.carve(8 * 128 // 2, BF16).rearrange("p (g c) -> p g c", g=8)
        mark = self.arena_off
        ident = self.c("ident", 0, 128)
        jt = self.c("jt", 0, 128)
        col_ld = self.carve(64)
        col_li = self.carve(64)
        ctm = [self.carve(64) for _ in range(4)]
        cit = self.carve(64).bitcast(I32)
        lre, lim, ldt = self.v("s5_lre", 0, 64), self.v("s5_lim", 0, 64), self.v("s5_ldt", 0, 64)

        def prep_in(dst_ld, dst_li, lre, lim, ldt, dtt):
            P.op("act", lambda e: e.activation(out=dtt, in_=ldt, func=AF.Exp), reads=["vecs", "s5row"], writes=["s5dtt"])
            P.op("dve", lambda e: e.tensor_scalar(out=dst_ld, in0=lre, scalar1=-1e-4, scalar2=None, op0=ALU.min),
                 reads=["vecs", "s5row"], writes=["s5in"])
            P.op("dve", lambda e: e.tensor_tensor(out=dst_ld, in0=dst_ld, in1=dtt, op=ALU.mult), reads=["s5in", "s5dtt"], writes=["s5in"])
            P.op("dve", lambda e: e.tensor_tensor(out=dst_li, in0=lim, in1=dtt, op=ALU.mult), reads=["s5dtt", "vecs", "s5row"], writes=["s5in"])
        prep_in(col_ld, col_li, lre, lim, ldt, ctm[0])
        for w, n in enumerate(POWS):
            self.cexp(n, col_ld, col_li, AR[:, w, :], AI[:, w, :], ctm, cit, "col")
        RW = 512
        R = [self.carve(RW) for _ in range(16)]
        rit = self.carve(RW).bitcast(I32)
        braw = self.carve(1024)
        bt1 = [self.carve(64) for _ in range(2)]
        bt2 = [self.carve(64) for _ in range(2)]
        r_lre, r_lim, r_ldt, r_ld, r_li, r_dt, lr, li_, sre, sim, den, t_ = R[:12]
        for c in range(NCH):
            for i, dstt in enumerate((r_lre, r_lim, r_ldt)):
                P.dma("sp", lambda e, i=i, c=c, dstt=dstt: e.dma_start(out=dstt[:], in_=self.s5row_d[:, i * 4096 + c * RW:i * 4096 + (c + 1) * RW]),
                      reads=["s5in", "s5dtt", "s5sre", "s5sim", "s5den", "s5t"], writes=["s5row"])
            P.dma("sp", lambda e, c=c: e.dma_start(out=braw[:], in_=self.s5bt_d[c]), reads=[("bt1", 0), ("bt1", 1), ("bt2", 0), ("bt2", 1)],
                  writes=["s5bt"])
            P.op("dve", lambda e: e.tensor_scalar(out=r_lre[:], in0=r_lre[:], scalar1=-1e-4, scalar2=None, op0=ALU.min), reads=["s5row"], writes=["s5row"])
            prep_in(r_ld[:], r_li[:], r_lre[:], r_lim[:], r_ldt[:], r_dt[:])
            self.cexp(1, r_ld[:], r_li[:], lr[:], li_[:], [R[12][:], R[13][:], R[14][:], R[15][:]], rit, "row")
            P.op("dve", lambda e: e.tensor_scalar(out=lr[:], in0=lr[:], scalar1=-1.0, scalar2=None, op0=ALU.add), reads=["s5tab"], writes=["s5tab"])
            P.op("dve", lambda e: e.tensor_tensor(out=den[:], in0=r_lre[:], in1=r_lre[:], op=ALU.mult), reads=["s5row"], writes=["s5den"])
            P.op("dve", lambda e: e.tensor_tensor(out=t_[:], in0=r_lim[:], in1=r_lim[:], op=ALU.mult), reads=["s5row"], writes=["s5t"])
            P.op("dve", lambda e: e.tensor_tensor(out=den[:], in0=den[:], in1=t_[:], op=ALU.add), reads=["s5den", "s5t"], writes=["s5den"])
            P.op("dve", lambda e: e.reciprocal(out=den[:], in_=den[:]), reads=["s5den"], writes=["s5den"])
            P.op("dve", lambda e: e.tensor_tensor(out=sre[:], in0=lr[:], in1=r_lre[:], op=ALU.mult), reads=["s5tab", "s5row"], writes=["s5sre"])
            P.op("dve", lambda e: e.tensor_tensor(out=t_[:], in0=li_[:], in1=r_lim[:], op=ALU.mult), reads=["s5tab", "s5row", "s5den"], writes=["s5t"])
            P.op("dve", lambda e: e.tensor_tensor(out=sre[:], in0=sre[:], in1=t_[:], op=ALU.add), reads=["s5sre", "s5t"], writes=["s5sre"])
            P.op("dve", lambda e: e.tensor_tensor(out=sre[:], in0=sre[:], in1=den[:], op=ALU.mult), reads=["s5sre", "s5den"], writes=["s5sre"])
            P.op("dve", lambda e: e.tensor_tensor(out=sim[:], in0=li_[:], in1=r_lre[:], op=ALU.mult), reads=["s5tab", "s5row"], writes=["s5sim"])
            P.op("dve", lambda e: e.tensor_tensor(out=t_[:], in0=lr[:], in1=r_lim[:], op=ALU.mult), reads=["s5tab", "s5row", "s5sre"], writes=["s5t"])
            P.op("dve", lambda e: e.tensor_tensor(out=sim[:], in0=sim[:], in1=t_[:], op=ALU.subtract), reads=["s5sim", "s5t"], writes=["s5sim"])
            P.op("dve", lambda e: e.tensor_tensor(out=sim[:], in0=sim[:], in1=den[:], op=ALU.mult), reads=["s5sim", "s5den"], writes=["s5sim"])
            for gl in range(8):
                g = c * 8 + gl
                bre = braw[:, gl * 64:(gl + 1) * 64]
                bim = braw[:, 512 + gl * 64:512 + (gl + 1) * 64]
                sr = sre[:, gl * 64:(gl + 1) * 64]
                si = sim[:, gl * 64:(gl + 1) * 64]
                i2 = g % 2
                eng2 = "dve" if g % 2 == 0 else "pool"
                for (o0, a_, b_, op2) in ((0, sr, si, ALU.subtract), (64, si, sr, ALU.add)):
                    P.op(eng2, lambda e, a_=a_, bre=bre, i2=i2: e.tensor_tensor(out=bt1[i2][:], in0=bre, in1=a_, op=ALU.mult),
                         reads=["s5bt", "s5sre", "s5sim"], writes=[("bt1", i2)])
                    P.op(eng2, lambda e, b_=b_, bim=bim, i2=i2: e.tensor_tensor(out=bt2[i2][:], in0=bim, in1=b_, op=ALU.mult),
                         reads=["s5bt", "s5sre", "s5sim"], writes=[("bt2", i2)])
                    P.op(eng2, lambda e, g=g, o0=o0, op2=op2, i2=i2: e.tensor_tensor(out=BTs[:, g, o0:o0 + 64], in0=bt1[i2][:], in1=bt2[i2][:], op=op2),
                         reads=[("bt1", i2), ("bt2", i2)], writes=[("BTs", g)])
        P.dma("pool", lambda e: e.dma_start(out=Cc.rearrange("p g h -> p (g h)"), in_=self.s5c_d[:, :]), writes=["Cc"])
        P.op("dve", lambda e: e.tensor_scalar(out=Cc[64:128, :, :], in0=Cc[64:128, :, :], scalar1=-1.0, scalar2=None, op0=ALU.mult),
             reads=["Cc"], writes=["Cc"])
        P.op("dve", lambda e: e.memset(Cpad.rearrange("p g c -> p (g c)"), 0.0), writes=[("Cpad", i) for i in range(8)])
        self.reset_arena(mark)
        NB = 4
        identb = self.carve(64, BF16)
        P.op("act", lambda e: e.activation(out=identb[:], in_=ident, func=AF.Copy), reads=["cst"], writes=["identb"])
        Zb = [self.carve(S // 2, BF16) for _ in range(NB)]
        Eb2 = [[self.carve(128, BF16) for _ in range(2)] for _ in range(NB)]
        MT = [[[self.carve(64, BF16) for _ in range(NPW)] for _ in range(NB)] for _ in range(2)]
        T1 = [self.carve(128) for _ in range(3)]
        ytmp = self.carve(TW)
        self.bank_mod = 4
        self.bank_i = 0
        pw = {n: i for i, n in enumerate(POWS)}
        t1c = [0]

        def mtgen(bi):
            for q, g in enumerate(range(bi * NB, (bi + 1) * NB)):
                for w in range(NPW):
                    ti = t1c[0] % 3
                    t1c[0] += 1
                    t1 = T1[ti]
                    P.op("dve", lambda e, t1=t1, w=w, g=g: e.tensor_scalar(out=t1[:], in0=ident, scalar1=AR[:, w, g:g + 1], scalar2=None, op0=ALU.mult),
                         reads=["cst", "s5tab"], writes=[("T1", ti)])
                    P.op("dve", lambda e, t1=t1, w=w, g=g, m_=MT[bi % 2][q][w]: e.scalar_tensor_tensor(out=m_[:], in0=jt, scalar=AI[:, w, g:g + 1], in1=t1[:],
                                                                                                   op0=ALU.mult, op1=ALU.add),
                         reads=["cst", "s5tab", ("T1", ti)], writes=[("MT", bi % 2, q)])
        mtgen(0)
        for bi in range(64 // NB):
            groups = list(range(bi * NB, (bi + 1) * NB))
            c = groups[0] // 8
            mtb = MT[bi % 2]
            if bi + 1 < 64 // NB:
                mtgen(bi + 1)
            for q, g in enumerate(groups):
                gl = g % 8
                P.op("pool", lambda e, gl=gl, g=g: e.tensor_copy(out=Cpad[:, gl, 16 * gl:16 * gl + 16], in_=Cc[:, g, :]),
                     reads=["Cc"], writes=[("Cpad", gl)])
            for n in range(NT):
                sl = slice(n * TW, (n + 1) * TW)
                for q, g in enumerate(groups):
                    b = self.bank()
                    P.op("pe", lambda e, b=b, g=g, sl=sl, c=c: e.matmul(self.psb[b][:], lhsT=BTs[:, g, :], rhs=xn[:, c, sl], start=True, stop=True),
                         reads=[("BTs", g), ("xn", c)], writes=[("ps", b)])
                    P.op("act", lambda e, b=b, q=q, sl=sl: e.activation(out=Zb[q][:, sl], in_=self.psb[b][:], func=AF.Copy),
                         reads=[("ps", b)], writes=[("Zb", q)])
            L1 = [(4, 1, 0, 1), (4, 3, 2, 1), (4, 2, 1, 1), (4, 3, 1, 2), (8, 4, 3, 1), (8, 5, 3, 2), (8, 6, 3, 3), (8, 7, 3, 4)]
            for (per, dcl, scl, pwr) in (L1 if S5_DBG[0] else ()):
                ncol = S // per
                for t0 in range(0, ncol, TW):
                    tn = min(TW, ncol - t0)
                    for q, g in enumerate(groups):
                        zv = Zb[q].rearrange("p (a r) -> p a r", r=per)
                        dst = zv[:, t0:t0 + tn, dcl]
                        src = zv[:, t0:t0 + tn, scl]
                        b = self.bank()
                        ps = self.psb[b]
                        P.op("pe", lambda e, ps=ps, dst=dst, tn=tn: e.matmul(ps[:, 0:tn], lhsT=identb[:], rhs=dst, start=True, stop=False),
                             reads=["identb", ("Zb", q)], writes=[("ps", b)])
                        P.op("pe", lambda e, ps=ps, src=src, tn=tn, m_=mtb[q][pw[pwr]]: e.matmul(ps[:, 0:tn], lhsT=m_[:], rhs=src, start=False, stop=True),
                             reads=[("MT", bi % 2, q), ("Zb", q)], writes=[("ps", b)])
                        P.op("act", lambda e, ps=ps, dst=dst, tn=tn: e.activation(out=dst, in_=ps[:, 0:tn], func=AF.Copy),
                             reads=[("ps", b)], writes=[("Zb", q)])
            for j in (range(8) if S5_DBG[1] else ()):
                k = 2 ** j
                for q, g in enumerate(groups):
                    eold = Zb[q].rearrange("p (a r) -> p a r", r=8)[:, :, 7] if j == 0 else Eb2[q][(j - 1) % 2][:, 0:256]
                    eold_sh = Zb[q].rearrange("p (a r) -> p a r", r=8)[:, 0:256 - k, 7] if j == 0 else Eb2[q][(j - 1) % 2][:, 0:256 - k]
                    enew = Eb2[q][j % 2]
                    b = self.bank()
                    ps = self.psb[b]
                    P.op("pe", lambda e, ps=ps, eold=eold: e.matmul(ps[:, 0:256], lhsT=identb[:], rhs=eold, start=True, stop=False),
                         reads=["identb", ("Zb", q), ("E", q, (j - 1) % 2)], writes=[("ps", b)])
                    P.op("pe", lambda e, ps=ps, eold_sh=eold_sh, k=k, m_=mtb[q][pw[8 * k]]: e.matmul(ps[:, k:256], lhsT=m_[:], rhs=eold_sh, start=False, stop=True),
                         reads=[("MT", bi % 2, q), ("Zb", q), ("E", q, (j - 1) % 2)], writes=[("ps", b)])
                    P.op("dve", lambda e, ps=ps, enew=enew: e.tensor_copy(out=enew[:, 0:256], in_=ps[:, 0:256]),
                         reads=[("ps", b)], writes=[("E", q, j % 2)])
            for ip in (range(4) if S5_DBG[2] else ()):
                for q, g in enumerate(groups):
                    efin = Eb2[q][7 % 2]
                    z8 = Zb[q].rearrange("p (a r) -> p a r", r=8)
                    b = self.bank()
                    ps = self.psb[b]
                    for t in range(2):
                        i = 2 * ip + t
                        P.op("pe", lambda e, ps=ps, t=t, i=i, z8=z8: e.matmul(ps[:, t * 256:(t + 1) * 256], lhsT=identb[:], rhs=z8[:, :, i],
                                                                            start=True, stop=False),
                             reads=["identb", ("Zb", q)], writes=[("ps", b)])
                        P.op("pe", lambda e, ps=ps, t=t, i=i, efin=efin, m_=mtb[q][pw[i + 1]]: e.matmul(
                            ps[:, t * 256 + 1:(t + 1) * 256], lhsT=m_[:], rhs=efin[:, 0:255], start=False, stop=True),
                            reads=[("MT", bi % 2, q), ("E", q, 1)], writes=[("ps", b)])
                    for t in range(2):
                        P.op("dve", lambda e, ps=ps, z8=z8, ip=ip, t=t: e.tensor_copy(out=z8[:, :, 2 * ip + t], in_=ps[:, t * 256:(t + 1) * 256]),
                             reads=[("ps", b)], writes=[("Zb", q)])
            for q, g in enumerate(groups):
                gl = g % 8
                for n in range(NT):
                    sl = slice(n * TW, (n + 1) * TW)
                    P.op("pe", lambda e, n=n, gl=gl, q=q, sl=sl: e.matmul(self.psb[4 + n][:], lhsT=Cpad[:, gl, :], rhs=Zb[q][:, sl],
                                                                        start=(gl == 0), stop=(gl == 7)),
                         reads=[("Cpad", gl), ("Zb", q)], writes=[("ps", 4 + n)])
            if groups[-1] % 8 == 7:
                for n in range(NT):
                    sl = slice(n * TW, (n + 1) * TW)
                    P.op("dve", lambda e, n=n, c=c, sl=sl: e.scalar_tensor_tensor(out=ytmp[:], in0=xn[:, c, sl], scalar=self.v("s5_d", c),
                                                                                  in1=self.psb[4 + n][:], op0=ALU.mult, op1=ALU.add),
                         reads=[("ps", 4 + n), ("xn", c), "vecs"], writes=["ytmp"])
                    P.op("act", lambda e, c=c, sl=sl: e.activation(out=xn[:, c, sl], in_=ytmp[:], func=AF.Gelu),
                         reads=["ytmp"], writes=[("xn", c)])
        self.reset_arena(mark)
        self.bank_mod = 8
        vt = [self.carve(TW) for _ in range(2)]
        gt = [self.carve(TW) for _ in range(2)]
        xn_rhs = lambda k, n: (xn[:, k, n * TW:(n + 1) * TW], [("xn", k)])
        cnt = [0]
        for m in range(NCH):
            sv, sg = self.next_block(), self.next_block()
            for n in range(NT):
                i = cnt[0] % 2
                cnt[0] += 1
                sl = slice(n * TW, (n + 1) * TW)

                def evac_v(b, n, i=i, m=m):
                    P.op("act", lambda e: e.activation(out=vt[i][:], in_=self.psb[b][:], func=AF.Identity, bias=self.v("s5_b_out", m)),
                         reads=[("ps", b), "vecs"], writes=[("vt", i)])

                def evac_g(b, n, i=i, m=m):
                    P.op("act", lambda e: e.activation(out=gt[i][:], in_=self.psb[b][:], func=AF.Sigmoid, bias=self.v("s5_b_out", 8 + m)),
                         reads=[("ps", b), "vecs"], writes=[("gt", i)])
                self.linear_tile(sv, xn_rhs, NCH, n, evac_v)
                self.linear_tile(sg, xn_rhs, NCH, n, evac_g)
                P.op("dve", lambda e, i=i: e.tensor_tensor(out=vt[i][:], in0=vt[i][:], in1=gt[i][:], op=ALU.mult),
                     reads=[("vt", i), ("gt", i)], writes=[("vt", i)])
                P.op("dve", lambda e, i=i, m=m, sl=sl: e.tensor_tensor(out=self.xT[:, m, sl], in0=self.xT[:, m, sl], in1=vt[i][:], op=ALU.add),
                     reads=[("vt", i), ("xT", m, n)], writes=[("xT", m, n)])
            self.done_block(sv)
            self.done_block(sg)

    def sb_mixer(self, l):
        P = self.P
        xn = self.norm_xn("norm_mix_g", l)
        cm = self.carve(4352 // 2, BF16)
        m01 = [cm[:, i * 512:(i + 1) * 512] for i in range(4)]
        negm = [cm[:, 2048 + i * 512:2048 + (i + 1) * 512] for i in range(4)]
        negtri = cm[:, 4096:4224]
        identb = cm[:, 4224:4352]
        for (a, b_) in ((0, 2048), (2048, 4096), (4096, 4352)):
            P.dma("pool", lambda e, a=a, b_=b_: e.dma_start(out=cm[:, a:b_], in_=self.m_d[:, a:b_]), writes=["cm"])
        ones2 = self.carve(64, BF16)
        negones = self.carve(64, BF16)
        P.op("dve", lambda e: e.memset(ones2[:], 0.0), writes=["ones2"])
        P.op("dve", lambda e: e.memset(ones2[0:64, 0:64], 1.0), reads=["ones2"], writes=["ones2"])
        P.op("dve", lambda e: e.memset(ones2[64:128, 64:128], 1.0), reads=["ones2"], writes=["ones2"])
        P.op("dve", lambda e: e.memset(negones[:], -1.0), writes=["negones"])
        qT = self.carve(2 * S // 2, BF16).rearrange("p (c s) -> p c s", c=2)
        kT = self.carve(2 * S // 2, BF16).rearrange("p (c s) -> p c s", c=2)
        vtm = self.carve(16 * 256 // 2, BF16).rearrange("p (t j) -> p t j", t=16)
        oT = self.carve(2 * S // 2, BF16).rearrange("p (c s) -> p c s", c=2)
        qraw = [self.carve(TW) for _ in range(2)]
        sqb = [self.carve(TW // 2, BF16) for _ in range(2)]
        rq = [self.carve(TW) for _ in range(2)]
        Eb = [self.carve(TW) for _ in range(2)]
        Lb = [self.carve(TW // 2, BF16) for _ in range(3)]
        att = [self.carve(TW // 2, BF16) for _ in range(2)]
        Lacc = self.carve(TW // 2, BF16)
        xn_rhs = lambda k, n: (xn[:, k, n * TW:(n + 1) * TW], [("xn", k)])
        cnt = [0]
        for qq in range(4):
            for (dstT, gap, dname) in ((qT, self.gqs[:, 0:1], "qT"), (kT, self.v("sb_k_g", 0), "kT")):
                for cc in range(2):
                    slot = self.next_block()

                    def evac_qk(b, n, cc=cc, dstT=dstT, gap=gap, dname=dname):
                        i = cnt[0] % 2
                        cnt[0] += 1
                        sl = slice(n * TW, (n + 1) * TW)
                        P.op("act", lambda e: e.activation(out=qraw[i][:], in_=self.psb[b][:], func=AF.Copy),
                             reads=[("ps", b)], writes=[("qraw", i)])
                        P.op("act", lambda e: e.activation(out=sqb[i][:], in_=self.psb[b][:], func=AF.Square),
                             reads=[("ps", b)], writes=[("sqb", i)])
                        b2 = self.bank()
                        P.op("pe", lambda e: e.matmul(self.psb[b2][:], lhsT=ones2[:], rhs=sqb[i][:], start=True, stop=True),
                             reads=["ones2", ("sqb", i)], writes=[("ps", b2)])
                        P.op("act", lambda e: e.activation(out=rq[i][:], in_=self.psb[b2][:], func=AF.Sqrt, scale=1.0 / 64, bias=EPS),
                             reads=[("ps", b2)], writes=[("rq", i)])
                        P.op("dve", lambda e: e.reciprocal(out=rq[i][:], in_=rq[i][:]), reads=[("rq", i)], writes=[("rq", i)])
                        P.op("dve", lambda e: e.scalar_tensor_tensor(out=dstT[:, cc, sl], in0=qraw[i][:], scalar=gap, in1=rq[i][:],
                                                                     op0=ALU.mult, op1=ALU.mult),
                             reads=[("qraw", i), ("rq", i), "vecs", "gqs"], writes=[(dname, cc, n)])
                    for n in range(NT):
                        self.linear_tile(slot, xn_rhs, NCH, n, evac_qk)
                    self.done_block(slot)
            sva, svb = self.next_block(), self.next_block()
            for tb in range(16):
                b = self.bank()
                for k in range(NCH):
                    slot = sva if k < 4 else svb
                    P.op("pe", lambda e, b=b, k=k, slot=slot, tb=tb: e.matmul(
                        self.psb[b][:, 0:256], lhsT=xn[:, k, tb * 128:(tb + 1) * 128], rhs=self.ring[slot][:, (k % 4) * 256:(k % 4 + 1) * 256],
                        start=(k == 0), stop=(k == NCH - 1)), reads=[("ring", slot), ("xn", k)], writes=[("ps", b)])
                P.op("act", lambda e, b=b, tb=tb: e.activation(out=vtm[:, tb, :], in_=self.psb[b][:, 0:256], func=AF.Copy),
                     reads=[("ps", b)], writes=[("vtm", tb)])
            self.done_block(sva)
            self.done_block(svb)
            self.bank_mod = 6
            self.bank_i = 0
            units = []
            for hq in range(SB_HEADS_DBG):
                for qb in range(4):
                    kbs = list(range(4 * qb + 3, -1, -1))
                    for ii, kb in enumerate(kbs):
                        units.append(dict(hq=hq, qb=qb, kb=kb, first=(ii == 0), last=(kb == 0), ui=len(units)))

            def front(u):
                hq, qb, kb, ui = u["hq"], u["qb"], u["kb"], u["ui"]
                cq, pb = hq // 2, 64 * (hq % 2)
                diag = kb >= 4 * qb
                oi = kb - 4 * qb
                b = self.bank()
                u["b"] = b
                ps = self.psb[b]
                e_i, l_i = ui % 2, ui % 3
                P.op("pe", lambda e: e.matmul(ps[:], lhsT=kT[pb:pb + 64, cq, kb * 128:(kb + 1) * 128], rhs=qT[pb:pb + 64, cq, qb * TW:(qb + 1) * TW],
                                              start=True, stop=False),
                     reads=[("kT", cq, kb // 4), ("qT", cq, qb)], writes=[("ps", b)])
                P.op("act", lambda e: e.activation(out=Eb[e_i][:], in_=ps[:], func=AF.Exp), reads=[("ps", b)], writes=[("Eb", e_i)])
                P.op("act", lambda e: e.activation(out=Lb[l_i][:], in_=Eb[e_i][:], func=AF.Ln, bias=1.0), reads=[("Eb", e_i)], writes=[("Lb", l_i)])
                if diag:
                    P.op("dve", lambda e: e.tensor_tensor(out=Lb[l_i][:], in0=Lb[l_i][:], in1=m01[oi], op=ALU.mult),
                         reads=[("Lb", l_i), "cm"], writes=[("Lb", l_i)])

            def back(u):
                hq, qb, kb, ui, first, last, b = u["hq"], u["qb"], u["kb"], u["ui"], u["first"], u["last"], u["b"]
                cq, pb = hq // 2, 64 * (hq % 2)
                diag = kb >= 4 * qb
                oi = kb - 4 * qb
                ps = self.psb[b]
                l_i, a_i = ui % 3, ui % 2
                pob = 6 + (hq * 4 + qb) % 2
                po = self.psb[pob]
                more = (not first) or diag
                P.op("pe", lambda e: e.matmul(ps[:], lhsT=negtri, rhs=Lb[l_i][:], start=False, stop=not more),
                     reads=["cm", ("Lb", l_i)], writes=[("ps", b)])
                if not first:
                    P.op("pe", lambda e: e.matmul(ps[:], lhsT=negones[:], rhs=Lacc[:], start=False, stop=not diag),
                         reads=["negones", "Lacc"], writes=[("ps", b)])
                if diag:
                    P.op("pe", lambda e: e.matmul(ps[:], lhsT=identb, rhs=negm[oi], start=False, stop=True), reads=["cm"], writes=[("ps", b)])
                P.op("act", lambda e: e.activation(out=att[a_i][:], in_=ps[:], func=AF.Exp), reads=[("ps", b)], writes=[("att", a_i)])
                P.op("pe", lambda e: e.matmul(po[pb:pb + 64, :], lhsT=vtm[:, kb, hq * 64:(hq + 1) * 64], rhs=att[a_i][:], start=first, stop=last),
                     reads=[("vtm", kb), ("att", a_i)], writes=[("ps", pob)])
                if not last:
                    if first:
                        P.op("dve", lambda e: e.tensor_copy(out=Lacc[:], in_=Lb[l_i][:]), reads=[("Lb", l_i)], writes=["Lacc"])
                    else:
                        P.op("dve", lambda e: e.tensor_tensor(out=Lacc[:], in0=Lacc[:], in1=Lb[l_i][:], op=ALU.add),
                             reads=[("Lb", l_i), "Lacc"], writes=["Lacc"])
                else:
                    P.op("act", lambda e: e.activation(out=oT[pb:pb + 64, cq, qb * TW:(qb + 1) * TW], in_=po[pb:pb + 64, :], func=AF.Copy),
                         reads=[("ps", pob)], writes=[("oT", cq, qb)])
            for i in range(len(units) + 1):
                if i < len(units):
                    front(units[i])
                if i >= 1:
                    back(units[i - 1])
            self.bank_mod = 8
            so0, so1 = self.next_block(), self.next_block()
            for m in range(NCH):
                for n in range(NT):
                    sl = slice(n * TW, (n + 1) * TW)
                    b = self.bank()
                    for k, so in enumerate((so0, so1)):
                        P.op("pe", lambda e, b=b, k=k, so=so, m=m, sl=sl: e.matmul(
                            self.psb[b][:], lhsT=self.ring[so][:, m * 128:(m + 1) * 128], rhs=oT[:, k, sl], start=(k == 0), stop=(k == 1)),
                            reads=[("ring", so), ("oT", k, n)], writes=[("ps", b)])
                    self.resid_add(m)(b, n)
            self.done_block(so0)
            self.done_block(so1)

    def ffn(self, l):
        P = self.P
        xn = self.norm_xn("norm_ffn_g", l)
        hv = [self.carve((2 + S + 2) // 2, BF16) for _ in range(2)]
        hg = [self.carve((2 + S + 2) // 2, BF16) for _ in range(2)]
        cv = [self.carve(S)] * 2
        cg = [self.carve(S)] * 2
        gb = [[self.carve(S // 2, BF16) for _ in range(2)] for _ in range(2)]
        for buf in hv + hg:
            P.op("dve", lambda e, buf=buf: e.memset(buf[:, 0:2], 0.0), writes=[("fh", id(buf), n) for n in range(NT)])
        cwo = self.voff["ffn_conv_w"] + l * 3 * 44
        cbo = self.voff["ffn_conv_b"] + l * 44
        prev = None
        for f in range(NF):
            par = f % 2
            sv, sg, so = self.next_block(), self.next_block(), self.next_block()
            for (slot, h, cc, ch) in ((sv, hv[par], cv[par], f), (sg, hg[par], cg[par], NF + f)):
                blk = self.ring[slot]
                w0 = self.vecs[:, cwo + 0 * 44 + ch:cwo + 0 * 44 + ch + 1]
                w1 = self.vecs[:, cwo + 1 * 44 + ch:cwo + 1 * 44 + ch + 1]
                w2 = self.vecs[:, cwo + 2 * 44 + ch:cwo + 2 * 44 + ch + 1]
                bb = self.vecs[:, cbo + ch:cbo + ch + 1]
                for n in range(NT):
                    sl = slice(n * TW, (n + 1) * TW)
                    b = self.bank()
                    for k in range(NCH):
                        P.op("pe", lambda e, b=b, blk=blk, k=k, sl=sl: e.matmul(
                            self.psb[b][:], lhsT=blk[:, k * 128:(k + 1) * 128], rhs=xn[:, k, sl], start=(k == 0), stop=(k == NCH - 1)),
                            reads=[("ring", slot), ("xn", k)], writes=[("ps", b)])
                    P.op("act", lambda e, b=b, h=h, n=n: e.activation(out=h[:, 2 + n * TW:2 + (n + 1) * TW], in_=self.psb[b][:], func=AF.Copy),
                         reads=[("ps", b)], writes=[("fh", id(h), n)])
                    P.op("act", lambda e, b=b, cc=cc, sl=sl, w2=w2, bb=bb: e.activation(out=cc[:, sl], in_=self.psb[b][:], func=AF.Identity,
                                                                                         scale=w2, bias=bb),
                         reads=[("ps", b), "vecs"], writes=[("fc", id(cc))])
                P.op("dve", lambda e, h=h, cc=cc, w1=w1: e.scalar_tensor_tensor(out=cc[:], in0=h[:, 1:1 + S], scalar=w1, in1=cc[:],
                                                                                op0=ALU.mult, op1=ALU.add),
                     reads=[("fh", id(h), n) for n in range(NT)] + [("fc", id(cc)), "vecs"], writes=[("fc", id(cc))])
                P.op("dve", lambda e, h=h, cc=cc, w0=w0: e.scalar_tensor_tensor(out=cc[:], in0=h[:, 0:S], scalar=w0, in1=cc[:],
                                                                                op0=ALU.mult, op1=ALU.add),
                     reads=[("fh", id(h), n) for n in range(NT)] + [("fc", id(cc)), "vecs"], writes=[("fc", id(cc))])
            self.done_block(sv)
            self.done_block(sg)
            cgp, cvp = cg[par], cv[par]
            P.op("act", lambda e, cgp=cgp: e.activation(out=cgp[:], in_=cgp[:], func=AF.Silu),
                 reads=[("fc", id(cgp))], writes=[("fc", id(cgp))])
            g = gb[(f // 2) % 2][par]
            P.op("dve", lambda e, g=g, cgp=cgp, cvp=cvp: e.tensor_tensor(out=g[:], in0=cgp[:], in1=cvp[:], op=ALU.mult),
                 reads=[("fc", id(cgp)), ("fc", id(cvp))], writes=[("fg", id(g))])
            if par == 0:
                prev = (so, g)
            else:
                so0, g0 = prev
                for m in range(NCH):
                    for n in range(NT):
                        sl = slice(n * TW, (n + 1) * TW)
                        b = self.bank()
                        P.op("pe", lambda e, b=b, m=m, sl=sl, so0=so0, g0=g0: e.matmul(
                            self.psb[b][:], lhsT=self.ring[so0][:, m * 128:(m + 1) * 128], rhs=g0[:, sl], start=True, stop=False),
                            reads=[("ring", so0), ("fg", id(g0))], writes=[("ps", b)])
                        P.op("pe", lambda e, b=b, m=m, sl=sl, so=so, g=g: e.matmul(
                            self.psb[b][:], lhsT=self.ring[so][:, m * 128:(m + 1) * 128], rhs=g[:, sl], start=False, stop=True),
                            reads=[("ring", so), ("fg", id(g))], writes=[("ps", b)])
                        P.op("dve", lambda e, b=b, m=m, sl=sl: e.tensor_tensor(out=self.xT[:, m, sl], in0=self.xT[:, m, sl], in1=self.psb[b][:],
                                                                               op=ALU.add),
                             reads=[("ps", b), ("xT", m, n)], writes=[("xT", m, n)])
                self.done_block(so0)
                self.done_block(so)

    def build(self):
        self.setup()
        for s in range(self.nseq):
            self.load_x(s)
            for l in self.layers:
                m = l % 4
                if m == 0:
                    self.pool_mixer(l)
                elif m == 1:
                    self.s5_mixer(l)
                elif m == 2:
                    self.lru_mixer(l)
                elif m == 3:
                    self.sb_mixer(l)
                self.ffn(l)
            self.store_x(s)
        assert self.blk_i == self.blk_total, (self.blk_i, self.blk_total)
        return self.P.finalize()


LAYERS = [0, 1, 2, 3]
NSEQ = 2
NCORES = 8


def run(inputs, layers=LAYERS, nseq=NSEQ, ncores=NCORES, trace=False):
    x = np.asarray(inputs["x"], np.float32)
    V = prep_vecs(inputs)
    C = prep_consts()
    vecs = V.pack()
    consts = C.pack()
    wts = prep_weights(inputs, layers)
    cmask = prep_cmask()
    s5row, s5bt, s5c = prep_s5(inputs)
    bld = Builder(layers, nseq, wts.shape[0], V.off, V.n, C.off, C.n)
    nc = bld.build()
    in_maps = []
    for core in range(ncores):
        xs = x[core * nseq:(core + 1) * nseq]
        xt = np.ascontiguousarray(xs.reshape(nseq, S, NCH, 128).transpose(0, 3, 2, 1))
        in_maps.append({"xT": xt, "wts": wts, "vecs": vecs, "consts": consts, "cmask": cmask,
                        "s5row": s5row, "s5bt": s5bt, "s5c": s5c})
    res = run_bass_kernel_spmd(nc, in_maps, core_ids=list(range(ncores)), trace=trace)
    outs = []
    for core in range(ncores):
        o = res.results[core]["outT"]
        outs.append(np.ascontiguousarray(o.transpose(0, 3, 2, 1)).reshape(nseq, S, D))
    return np.concatenate(outs, axis=0).astype(np.float32), res, bld


def kernel(**inputs):
    out, _, _ = run(inputs)
    return out
```

```python
import contextlib
import math
import numpy as np
import concourse.bass as bass
import concourse.mybir as mybir
from concourse.bass_utils import run_bass_kernel_spmd

F32 = mybir.dt.float32
BF16 = mybir.dt.bfloat16
I32 = mybir.dt.int32
AF = mybir.ActivationFunctionType
ALU = mybir.AluOpType

ENGS = ["pe", "act", "dve", "pool", "sp"]
NDMASEM = 8
SAME_ENGINE_WAITS = True

D = 1024
S = 2048
NCH = 8
FH = 2816
NF = 22
EPS = 1e-6
NT = 4
TW = 512
RING = 12
ARENA = 29200
POOL_WINDOWS = (2, 4, 8, 16)
SB_HEADS_DBG = 4
S5_DBG = [1, 1, 1]


class Prog:
    def __init__(self):
        self.nc = bass.Bass("TRN2", target_bir_lowering=False)
        self.es = contextlib.ExitStack()
        self.ops = {e: [] for e in ENGS}
        self.lastw = {}
        self.readers = {}
        self.known = {e: {} for e in ENGS}
        self.ndma = {e: 0 for e in ENGS}
        self.dma_last = {}
        self.last_compute = {}
        self.fence_toks = []
        self.n_alloc = 0

    def sb(self, shape, dt, name=None):
        self.n_alloc += 1
        return self.es.enter_context(self.nc.sbuf_tensor(name or f"sb{self.n_alloc}", list(shape), dt))

    def ps(self, shape, dt=F32, name=None):
        self.n_alloc += 1
        return self.es.enter_context(self.nc.psum_tensor(name or f"ps{self.n_alloc}", list(shape), dt))

    def dram(self, name, shape, dt, kind):
        return self.nc.dram_tensor(name, list(shape), dt, kind=kind).ap()

    def _deps(self, reads, writes):
        deps = []
        for k in reads:
            t = self.lastw.get(k)
            if t is not None:
                deps.append(t)
        for k in writes:
            t = self.lastw.get(k)
            if t is not None:
                deps.append(t)
            deps.extend(self.readers.get(k, ()))
        deps.extend(self.fence_toks)
        return deps

    def _commit(self, tok, reads, writes):
        for k in reads:
            self.readers.setdefault(k, []).append(tok)
        for k in writes:
            self.lastw[k] = tok
            self.readers[k] = []

    def _waits(self, eng, deps):
        waits = []
        kn = self.known[eng]
        for t in deps:
            if t[0] == "eng":
                _, e, idx = t
                if e == eng and (eng == "pe" or not SAME_ENGINE_WAITS):
                    continue
                if kn.get(e, -1) >= idx:
                    continue
                kn[e] = idx
                self.ops[e][idx]["marked"] = True
                waits.append(t)
            else:
                _, q, slot, val = t
                key = ("dma", q, slot)
                if kn.get(key, 0) >= val:
                    continue
                kn[key] = val
                waits.append(t)
        return waits

    def op(self, eng, fn, reads=(), writes=()):
        deps = self._deps(reads, writes)
        waits = self._waits(eng, deps)
        idx = len(self.ops[eng])
        self.ops[eng].append(dict(fn=fn, waits=waits, marked=False, dma=None))
        self.last_compute[eng] = idx
        self._commit(("eng", eng, idx), reads, writes)

    def dma(self, eng, fn, reads=(), writes=()):
        deps = self._deps(reads, writes)
        k = self.ndma[eng]
        self.ndma[eng] += 1
        slot = k % NDMASEM
        val = 16 * (k // NDMASEM + 1)
        if k >= NDMASEM:
            deps.append(("dma", eng, slot, val - 16))
        waits = self._waits(eng, deps)
        tok = ("dma", eng, slot, val)
        self.ops[eng].append(dict(fn=fn, waits=waits, marked=False, dma=(slot, val)))
        self.dma_last[(eng, slot)] = val
        self._commit(tok, reads, writes)

    def fence(self):
        self.fence_toks = [("eng", e, i) for e, i in self.last_compute.items()]

    def finalize(self):
        nc = self.nc
        deps = [("dma", q, s, v) for (q, s), v in self.dma_last.items()]
        waits = self._waits("sp", deps)
        self.ops["sp"].append(dict(fn=None, waits=waits, marked=False, dma=None))
        EPOCH = 1000
        nmark = {e: sum(o["marked"] for o in self.ops[e]) for e in ENGS}
        sems = {e: [self.es.enter_context(nc.semaphore(f"s_{e}{j}")) for j in range(nmark[e] // EPOCH + 1)] for e in ENGS}
        dsems = {}
        for e in ENGS:
            for s in range(min(NDMASEM, self.ndma[e])):
                dsems[(e, s)] = self.es.enter_context(nc.semaphore(f"d_{e}{s}"))
        semval = {}
        for e in ENGS:
            c = 0
            for i, o in enumerate(self.ops[e]):
                if o["marked"]:
                    semval[(e, i)] = (c // EPOCH, c % EPOCH + 1)
                    c += 1
        self.stats = {e: (len(self.ops[e]), sum(len(o["waits"]) for o in self.ops[e])) for e in ENGS}

        def emit(e, eng):
            for i, o in enumerate(self.ops[e]):
                for t in o["waits"]:
                    if t[0] == "eng":
                        ep, val = semval[(t[1], t[2])]
                        eng.wait_ge(sems[t[1]][ep], val)
                    else:
                        eng.wait_ge(dsems[(t[1], t[2])], t[3])
                if o["fn"] is None:
                    continue
                inst = o["fn"](eng)
                if o["dma"] is not None:
                    inst.then_inc(dsems[(e, o["dma"][0])], 16)
                elif o["marked"]:
                    inst.then_inc(sems[e][semval[(e, i)][0]], 1)

        with nc.Block() as block:
            @block.tensor
            def _(eng):
                emit("pe", eng)

            @block.scalar
            def _(eng):
                emit("act", eng)

            @block.vector
            def _(eng):
                emit("dve", eng)

            @block.gpsimd
            def _(eng):
                emit("pool", eng)

            @block.sync
            def _(eng):
                emit("sp", eng)
        self.es.close()
        return nc


def chan_cols(v):
    v = np.asarray(v, np.float32)
    return np.ascontiguousarray(v.reshape(-1, 128).T)


class VecLayout:
    def __init__(self):
        self.off = {}
        self.cols = []
        self.n = 0

    def add(self, name, arr):
        arr = np.asarray(arr, np.float32)
        assert arr.shape[0] == 128
        arr = arr.reshape(128, -1)
        self.off[name] = self.n
        self.cols.append(arr)
        self.n += arr.shape[1]

    def pack(self):
        return np.ascontiguousarray(np.concatenate(self.cols, axis=1))


def prep_vecs(inp):
    V = VecLayout()
    V.add("norm_mix_g", np.concatenate([chan_cols(inp["norm_mix_g"][l]) for l in range(4)], axis=1))
    V.add("norm_ffn_g", np.concatenate([chan_cols(inp["norm_ffn_g"][l]) for l in range(4)], axis=1))
    V.add("pool_b", chan_cols(inp["pool_b"][0]))
    V.add("pool_scale", chan_cols(inp["pool_scale"][0]))
    V.add("ffn_conv_w", np.concatenate([chan_cols(inp["ffn_conv_w"][l, k]) for l in range(4) for k in range(3)], axis=1))
    V.add("ffn_conv_b", np.concatenate([chan_cols(inp["ffn_conv_b"][l]) for l in range(4)], axis=1))
    V.add("lru_conv_w", np.concatenate([chan_cols(inp["lru_conv_w"][0, k]) for k in range(4)], axis=1))
    for nm in ("lru_conv_b", "lru_b_a", "lru_b_x", "lru_lam"):
        V.add(nm, chan_cols(inp[nm][0]))
    lre, lim, ldt = (np.asarray(inp[k][0], np.float32) for k in ("s5_lam_re", "s5_lam_im", "s5_log_dt"))
    V.add("s5_lre", np.tile(lre.T, (2, 1)))
    V.add("s5_lim", np.tile(lim.T, (2, 1)))
    V.add("s5_ldt", np.broadcast_to(ldt.reshape(1, 64), (128, 64)))
    V.add("s5_d", chan_cols(inp["s5_d"][0]))
    V.add("s5_b_out", chan_cols(inp["s5_b_out"][0]))
    V.add("sb_q_g", np.tile(np.asarray(inp["sb_q_g"][0], np.float32), 2).reshape(128, 1))
    V.add("sb_k_g", np.tile(np.asarray(inp["sb_k_g"][0], np.float32), 2).reshape(128, 1))
    return V


def prep_consts():
    C = VecLayout()
    C.add("ident", np.eye(128, dtype=np.float32))
    rc = np.zeros((4, 16), np.float32)
    for wi, w in enumerate(POOL_WINDOWS):
        for t in range(16):
            rc[wi, t] = 1.0 / min(t + 1, w)
    C.add("pool_rc", np.broadcast_to(rc.reshape(1, 64), (128, 64)))
    jt = np.zeros((128, 128), np.float32)
    for k in range(64):
        jt[k, k + 64] = 1.0
        jt[k + 64, k] = -1.0
    C.add("jt", jt)
    return C


def prep_cmask():
    p = np.arange(128)[:, None]
    j = np.arange(512)[None, :]
    m01 = np.concatenate([(j - p > off).astype(np.float32) for off in (0, 128, 256, 384)], axis=1)
    neg = (1.0 - m01) * -30000.0
    jj = np.arange(128)[:, None]
    ss = np.arange(128)[None, :]
    negtri = -(jj >= ss).astype(np.float32)
    ident = np.eye(128, dtype=np.float32)
    return np.ascontiguousarray(np.concatenate([m01, neg, negtri, ident], axis=1))


def prep_s5(inp):
    lre, lim, ldt = (np.asarray(inp[k][0], np.float32) for k in ("s5_lam_re", "s5_lam_im", "s5_log_dt"))
    row = np.stack([lre.reshape(-1), lim.reshape(-1), np.repeat(ldt, 64)], axis=0)
    s5row = np.ascontiguousarray(np.broadcast_to(row.reshape(1, 3 * 4096), (128, 3 * 4096)))
    bre, bim = np.asarray(inp["s5_b_re"][0], np.float32), np.asarray(inp["s5_b_im"][0], np.float32)
    bt = np.zeros((8, 128, 2, 8, 64), np.float32)
    for g in range(64):
        c, gl = g // 8, g % 8
        bt[c, 16 * gl:16 * gl + 16, 0, gl, :] = bre[g].T
        bt[c, 16 * gl:16 * gl + 16, 1, gl, :] = bim[g].T
    cre, cim = np.asarray(inp["s5_c_re"][0], np.float32), np.asarray(inp["s5_c_im"][0], np.float32)
    cc = np.concatenate([cre.transpose(2, 0, 1), cim.transpose(2, 0, 1)], axis=0)
    return s5row, np.ascontiguousarray(bt.reshape(8, 128, 1024)), np.ascontiguousarray(cc.reshape(128, 1024))


def wblock_v(W, qq, half):
    blk = W[half * 512:(half + 1) * 512, 2 * D + qq * 256:2 * D + (qq + 1) * 256]
    return np.ascontiguousarray(blk.reshape(4, 128, 256).transpose(1, 0, 2).reshape(128, 1024))


def wblock_in(W, col0, kc=8):
    blk = W[:kc * 128, col0:col0 + 128].reshape(kc, 128, 128).transpose(1, 0, 2).reshape(128, kc * 128)
    out = np.zeros((128, 1024), np.float32)
    out[:, :kc * 128] = blk
    return out


def prep_weights(inp, layers):
    blocks = []
    for l in layers:
        m = l % 4
        if m == 0:
            pw = inp["pool_w"][0]
            for gi in range(4):
                for mo in range(2):
                    blocks.append(wblock_in(pw[gi], mo * 128, kc=2))
        if m == 2:
            win, wa, wx, wout = inp["lru_w_in"][0], inp["lru_w_a"][0], inp["lru_w_x"][0], inp["lru_w_out"][0]
            for nb in range(4):
                for cc in range(2):
                    blocks.append(wblock_in(win, D + (2 * nb + cc) * 128))
                for mo in range(2):
                    blocks.append(wblock_in(win, (2 * nb + mo) * 128))
                    blocks.append(wblock_in(wa[nb], mo * 128, kc=2))
                    blocks.append(wblock_in(wx[nb], mo * 128, kc=2))
            for mo in range(NCH):
                blocks.append(wblock_in(wout, mo * 128))
        if m == 1:
            w5 = inp["s5_w_out"][0]
            for mo in range(NCH):
                blocks.append(wblock_in(w5, mo * 128))
                blocks.append(wblock_in(w5, D + mo * 128))
        if m == 3:
            wqkv, wo_ = inp["sb_w_qkv"][0], inp["sb_w_o"][0]
            for qq in range(4):
                for c in (2 * qq, 2 * qq + 1):
                    blocks.append(wblock_in(wqkv, c * 128))
                for c in (2 * qq, 2 * qq + 1):
                    blocks.append(wblock_in(wqkv, D + c * 128))
                blocks.append(wblock_v(wqkv, qq, 0))
                blocks.append(wblock_v(wqkv, qq, 1))
                for c in (2 * qq, 2 * qq + 1):
                    blocks.append(np.ascontiguousarray(wo_[c * 128:(c + 1) * 128, :]))
        wi, wo = inp["ffn_w_in"][l], inp["ffn_w_out"][l]
        for f in range(NF):
            blocks.append(wblock_in(wi, f * 128))
            blocks.append(wblock_in(wi, FH + f * 128))
            blocks.append(np.ascontiguousarray(wo[f * 128:(f + 1) * 128, :]))
    return np.ascontiguousarray(np.stack(blocks, axis=0))


class Builder:
    def __init__(self, layers, nseq, nblocks, voff, nv, coff, ncst):
        self.layers = layers
        self.nseq = nseq
        self.voff, self.coff = voff, coff
        P = self.P = Prog()
        self.x_d = P.dram("xT", [nseq, 128, NCH, S], F32, "ExternalInput")
        self.w_d = P.dram("wts", [nblocks, 128, 1024], F32, "ExternalInput")
        self.v_d = P.dram("vecs", [128, nv], F32, "ExternalInput")
        self.c_d = P.dram("consts", [128, ncst], F32, "ExternalInput")
        self.o_d = P.dram("outT", [nseq, 128, NCH, S], F32, "ExternalOutput")
        self.m_d = P.dram("cmask", [128, 4352], F32, "ExternalInput")
        self.s5row_d = P.dram("s5row", [128, 3 * 4096], F32, "ExternalInput")
        self.s5bt_d = P.dram("s5bt", [8, 128, 1024], F32, "ExternalInput")
        self.s5c_d = P.dram("s5c", [128, 1024], F32, "ExternalInput")
        self.bank_mod = 8
        self.nblocks = nblocks
        self.xT = P.sb([128, NCH, S], F32, "xT_sb")
        self.vecs = P.sb([128, nv], F32, "vecs_sb")
        self.cst = P.sb([128, ncst], F32, "cst_sb")
        self.ring = [P.sb([128, 1024], BF16, f"ring{i}") for i in range(RING)]
        self.ones = P.sb([128, 128], BF16, "ones")
        self.bsv = P.sb([128, 8], F32, "bsv")
        self.lsc = P.sb([128, 8], F32, "lsc")
        self.gqs = P.sb([128, 1], F32, "gqs")
        self.arena = P.sb([128, ARENA], F32, "arena")
        self.psb = [P.ps([128, TW], F32, f"psb{i}") for i in range(8)]
        self.bank_i = 0
        self.blk_i = 0
        self.blk_issued = 0
        self.blk_total = nblocks * nseq
        self.arena_off = 0
        self.slot_free = [True] * RING

    def bank(self):
        b = self.bank_i
        self.bank_i = (self.bank_i + 1) % self.bank_mod
        return b

    def v(self, name, col, n=1):
        o = self.voff[name] + col
        return self.vecs[:, o:o + n]

    def c(self, name, col, n=1):
        o = self.coff[name] + col
        return self.cst[:, o:o + n]

    def carve(self, n_f32, dt=F32):
        a = self.arena_off
        self.arena_off += n_f32
        assert self.arena_off <= ARENA, self.arena_off
        ap = self.arena[:, a:a + n_f32]
        if dt == BF16:
            ap = ap.bitcast(BF16)
        return ap

    def reset_arena(self, off=0):
        self.P.fence()
        self.arena_off = off

    def norm_carve(self):
        self.rstd = self.carve(S)
        self.sq = [self.carve(NCH * TW // 2, BF16).rearrange("p (c s) -> p c s", c=NCH) for _ in range(2)]
        self.nrm_tmp = self.carve(TW)

    def try_issue(self):
        while self.blk_issued < self.blk_total and self.slot_free[self.blk_issued % RING]:
            g = self.blk_issued
            self.blk_issued += 1
            slot = g % RING
            self.slot_free[slot] = False
            src = self.w_d[g % self.nblocks]
            dst = self.ring[slot]
            self.P.dma("pool", lambda e, dst=dst, src=src: e.dma_start(out=dst[:], in_=src), writes=[("ring", slot)])

    def next_block(self):
        g = self.blk_i
        self.blk_i += 1
        assert g < self.blk_issued, (g, self.blk_issued)
        return g % RING

    def done_block(self, slot):
        self.slot_free[slot] = True
        self.try_issue()

    def setup(self):
        P = self.P
        P.dma("sp", lambda e: e.dma_start(out=self.vecs[:], in_=self.v_d[:, :]), writes=["vecs"])
        P.dma("sp", lambda e: e.dma_start(out=self.cst[:], in_=self.c_d[:, :]), writes=["cst"])
        P.op("dve", lambda e: e.memset(self.ones[:], 1.0), writes=["ones"])
        if 0 in self.layers:
            P.op("dve", lambda e: e.tensor_tensor(out=self.bsv[:], in0=self.v("pool_b", 0, 8), in1=self.v("pool_scale", 0, 8),
                                                  op=ALU.mult), reads=["vecs"], writes=["bsv"])
        if 3 in self.layers:
            P.op("dve", lambda e: e.tensor_scalar(out=self.gqs[:], in0=self.v("sb_q_g", 0), scalar1=0.125, scalar2=None, op0=ALU.mult),
                 reads=["vecs"], writes=["gqs"])
        if 2 in self.layers:
            P.op("act", lambda e: e.activation(out=self.lsc[:], in_=self.v("lru_lam", 0, 8), func=AF.Exp, scale=-1.0),
                 reads=["vecs"], writes=["lsc"])
            P.op("act", lambda e: e.activation(out=self.lsc[:], in_=self.lsc[:], func=AF.Ln, bias=1.0), reads=["lsc"], writes=["lsc"])
            P.op("dve", lambda e: e.tensor_scalar(out=self.lsc[:], in0=self.lsc[:], scalar1=-8.0, scalar2=None, op0=ALU.mult),
                 reads=["lsc"], writes=["lsc"])
        self.try_issue()

    def load_x(self, s):
        P = self.P
        for c in range(NCH):
            P.dma("sp", lambda e, c=c: e.dma_start(out=self.xT[:, c, :], in_=self.x_d[s, :, c, :]),
                  writes=[("xT", c, n) for n in range(NT)])

    def store_x(self, s):
        P = self.P
        for c in range(NCH):
            P.dma("sp", lambda e, c=c: e.dma_start(out=self.o_d[s, :, c, :], in_=self.xT[:, c, :]),
                  reads=[("xT", c, n) for n in range(NT)])

    def rms_rstd(self):
        P = self.P
        rstd, nrm_tmp = self.rstd, self.nrm_tmp
        for n in range(NT):
            sl = slice(n * TW, (n + 1) * TW)
            sq = self.sq[n % 2]
            for c in range(NCH):
                P.op("act", lambda e, c=c, sq=sq, sl=sl: e.activation(out=sq[:, c, :], in_=self.xT[:, c, sl], func=AF.Square),
                     reads=[("xT", c, n)], writes=[("sq", n % 2, c)])
            b = self.bank()
            for c in range(NCH):
                P.op("pe", lambda e, c=c, sq=sq, b=b: e.matmul(self.psb[b][:], lhsT=self.ones[:], rhs=sq[:, c, :],
                                                               start=(c == 0), stop=(c == NCH - 1)),
                     reads=["ones", ("sq", n % 2, c)], writes=[("ps", b)])
            P.op("act", lambda e, b=b: e.activation(out=nrm_tmp[:], in_=self.psb[b][:], func=AF.Sqrt, scale=1.0 / D, bias=EPS),
                 reads=[("ps", b)], writes=["nrm_tmp"])
            P.op("dve", lambda e, sl=sl: e.reciprocal(out=rstd[:, sl], in_=nrm_tmp[:]),
                 reads=["nrm_tmp"], writes=[("rstd", n)])

    def make_xn(self, xn, gname, l):
        P = self.P
        rstd = self.rstd
        for c in range(NCH):
            g = self.v(gname, l * 8 + c)
            P.op("dve", lambda e, c=c, g=g: e.scalar_tensor_tensor(out=xn[:, c, :], in0=self.xT[:, c, :], scalar=g, in1=rstd[:],
                                                                   op0=ALU.mult, op1=ALU.mult),
                 reads=[("xT", c, n) for n in range(NT)] + [("rstd", n) for n in range(NT)] + ["vecs"],
                 writes=[("xn", c)])

    def norm_xn(self, gname, l):
        self.reset_arena()
        xn = self.carve(NCH * S // 2, BF16).rearrange("p (c s) -> p c s", c=NCH)
        mark = self.arena_off
        self.norm_carve()
        self.rms_rstd()
        self.make_xn(xn, gname, l)
        self.reset_arena(mark)
        return xn

    def pool_mixer(self, l):
        P = self.P
        self.reset_arena()
        self.norm_carve()
        self.rms_rstd()
        rstd = self.rstd
        d = self.carve(NCH * S // 2, BF16)
        H = [self.carve(16 + S) for _ in range(2)]
        SA = [self.carve(16 + S) for _ in range(2)]
        SB = [self.carve(16 + S) for _ in range(2)]
        t16 = self.carve(16)
        for buf in H + SA + SB:
            P.op("dve", lambda e, buf=buf: e.memset(buf[:, 0:16], 0.0), writes=[("pm", id(buf))])
        for c in range(NCH):
            gi = c // 2
            w = POOL_WINDOWS[gi]
            h, sa, sb_ = H[c % 2], SA[c % 2], SB[c % 2]
            g = self.v("norm_mix_g", l * 8 + c)
            P.op("dve", lambda e, c=c, g=g, h=h: e.scalar_tensor_tensor(out=h[:, 16:16 + S], in0=self.xT[:, c, :], scalar=g, in1=rstd[:],
                                                                        op0=ALU.mult, op1=ALU.mult),
                 reads=[("xT", c, n) for n in range(NT)] + [("rstd", n) for n in range(NT)] + ["vecs"], writes=[("pm", id(h))])
            src = h
            sh = 1
            pp = [sa, sb_]
            j = 0
            while sh < w:
                dst = pp[j % 2]
                P.op("dve", lambda e, src=src, dst=dst, sh=sh: e.tensor_tensor(out=dst[:, 16:16 + S], in0=src[:, 16:16 + S],
                                                                               in1=src[:, 16 - sh:16 + S - sh], op=ALU.add),
                     reads=[("pm", id(src))], writes=[("pm", id(dst))])
                src = dst
                sh *= 2
                j += 1
            dc = d[:, c * S:(c + 1) * S]
            P.op("dve", lambda e, src=src, h=h, dc=dc, w=w: e.scalar_tensor_tensor(out=dc, in0=src[:, 16:16 + S], scalar=1.0 / w,
                                                                                   in1=h[:, 16:16 + S], op0=ALU.mult, op1=ALU.subtract),
                 reads=[("pm", id(src)), ("pm", id(h))], writes=[("pd", c)])
            rc = self.c("pool_rc", gi * 16, 16)
            P.op("dve", lambda e, src=src, rc=rc: e.tensor_tensor(out=t16[:], in0=src[:, 16:32], in1=rc, op=ALU.mult),
                 reads=[("pm", id(src)), "cst"], writes=["t16"])
            P.op("dve", lambda e, h=h, dc=dc: e.tensor_tensor(out=dc[:, 0:16], in0=t16[:], in1=h[:, 16:32], op=ALU.subtract),
                 reads=["t16", ("pm", id(h))], writes=[("pd", c)])
        tmp = [self.carve(TW) for _ in range(2)]
        ti = 0
        for gi in range(4):
            for mo in range(2):
                c = 2 * gi + mo
                slot = self.next_block()
                blk = self.ring[slot]
                for n in range(NT):
                    sl = slice(n * TW, (n + 1) * TW)
                    b = self.bank()
                    for k in range(2):
                        kc = 2 * gi + k
                        P.op("pe", lambda e, b=b, blk=blk, k=k, kc=kc, sl=sl: e.matmul(
                            self.psb[b][:], lhsT=blk[:, k * 128:(k + 1) * 128], rhs=d[:, kc * S + sl.start:kc * S + sl.stop],
                            start=(k == 0), stop=(k == 1)),
                            reads=[("ring", slot), ("pd", kc)], writes=[("ps", b)])
                    t = tmp[ti % 2]
                    ti += 1
                    P.op("act", lambda e, b=b, t=t, c=c: e.activation(out=t[:], in_=self.psb[b][:], func=AF.Identity,
                                                                      scale=self.v("pool_scale", c), bias=self.bsv[:, c:c + 1]),
                         reads=[("ps", b), "vecs", "bsv"], writes=[("ptmp", id(t))])
                    P.op("dve", lambda e, t=t, c=c, sl=sl: e.tensor_tensor(out=self.xT[:, c, sl], in0=self.xT[:, c, sl], in1=t[:], op=ALU.add),
                         reads=[("ptmp", id(t)), ("xT", c, n)], writes=[("xT", c, n)])
                self.done_block(slot)

    def linear_tile(self, slot, rhs_fn, nk, n, evac):
        P = self.P
        b = self.bank()
        blk = self.ring[slot]
        for k in range(nk):
            rhs, rkeys = rhs_fn(k, n)
            P.op("pe", lambda e, b=b, blk=blk, k=k, rhs=rhs: e.matmul(self.psb[b][:], lhsT=blk[:, k * 128:(k + 1) * 128], rhs=rhs,
                                                                     start=(k == 0), stop=(k == nk - 1)),
                 reads=[("ring", slot)] + rkeys, writes=[("ps", b)])
        evac(b, n)

    def resid_add(self, m):
        def evac(b, n):
            sl = slice(n * TW, (n + 1) * TW)
            self.P.op("dve", lambda e: e.tensor_tensor(out=self.xT[:, m, sl], in0=self.xT[:, m, sl], in1=self.psb[b][:], op=ALU.add),
                      reads=[("ps", b), ("xT", m, n)], writes=[("xT", m, n)])
        return evac

    def lru_mixer(self, l):
        P = self.P
        xn = self.norm_xn("norm_mix_g", l)
        yb = self.carve(NCH * S // 2, BF16).rearrange("p (c s) -> p c s", c=NCH)
        hc = self.carve(1028, BF16)
        cacc = self.carve(S)
        recb = [self.carve(S // 2, BF16) for _ in range(2)]
        A, B, C = self.carve(S), self.carve(S), self.carve(S)
        Dg = self.carve(S // 2, BF16)
        P.op("dve", lambda e: e.memset(hc[:, 0:3], 0.0), writes=["hc"])
        xn_rhs = lambda k, n: (xn[:, k, n * TW:(n + 1) * TW], [("xn", k)])
        cw = lambda k, c: self.v("lru_conv_w", k * 8 + c)
        for nb in range(4):
            for cc in range(2):
                c = 2 * nb + cc
                slot = self.next_block()

                def evac_rec(b, n, c=c):
                    sl = slice(n * TW, (n + 1) * TW)
                    P.op("act", lambda e: e.activation(out=hc[:, 3 + n * TW:3 + (n + 1) * TW], in_=self.psb[b][:], func=AF.Copy),
                         reads=[("ps", b)], writes=["hc"])
                    P.op("act", lambda e: e.activation(out=cacc[:, sl], in_=self.psb[b][:], func=AF.Identity,
                                                       scale=cw(3, c), bias=self.v("lru_conv_b", c)),
                         reads=[("ps", b), "vecs"], writes=["cacc"])
                for n in range(NT):
                    self.linear_tile(slot, xn_rhs, NCH, n, evac_rec)
                self.done_block(slot)
                for k in (2, 1):
                    P.op("dve", lambda e, k=k, c=c: e.scalar_tensor_tensor(out=cacc[:], in0=hc[:, k:k + S], scalar=cw(k, c), in1=cacc[:],
                                                                           op0=ALU.mult, op1=ALU.add),
                         reads=["hc", "cacc", "vecs"], writes=["cacc"])
                rb = recb[cc]
                P.op("dve", lambda e, c=c, rb=rb: e.scalar_tensor_tensor(out=rb[:], in0=hc[:, 0:S], scalar=cw(0, c), in1=cacc[:],
                                                                         op0=ALU.mult, op1=ALU.add),
                     reads=["hc", "cacc", "vecs"], writes=[("recb", cc)])
            rec_rhs = lambda k, n: (recb[k][:, n * TW:(n + 1) * TW], [("recb", k)])
            for mo in range(2):
                c = 2 * nb + mo
                slot = self.next_block()

                def evac_g(b, n):
                    sl = slice(n * TW, (n + 1) * TW)
                    P.op("act", lambda e: e.activation(out=Dg[:, sl], in_=self.psb[b][:], func=AF.Gelu), reads=[("ps", b)], writes=["Dg"])
                for n in range(NT):
                    self.linear_tile(slot, xn_rhs, NCH, n, evac_g)
                self.done_block(slot)
                for (dst, dkey, bname) in ((A, "lA", "lru_b_a"), (C, "lC", "lru_b_x")):
                    slot = self.next_block()

                    def evac_s(b, n, dst=dst, dkey=dkey, bname=bname, c=c):
                        sl = slice(n * TW, (n + 1) * TW)
                        P.op("act", lambda e: e.activation(out=dst[:, sl], in_=self.psb[b][:], func=AF.Sigmoid, bias=self.v(bname, c)),
                             reads=[("ps", b), "vecs"], writes=[dkey])
                    for n in range(NT):
                        self.linear_tile(slot, rec_rhs, 2, n, evac_s)
                    self.done_block(slot)
                P.op("act", lambda e, c=c: e.activation(out=A[:], in_=A[:], func=AF.Exp, scale=self.lsc[:, c:c + 1]),
                     reads=["lA", "lsc"], writes=["lA"])
                P.op("dve", lambda e: e.tensor_tensor(out=B[:], in0=A[:], in1=A[:], op=ALU.mult), reads=["lA"], writes=["lB"])
                P.op("act", lambda e: e.activation(out=B[:], in_=B[:], func=AF.Sqrt, scale=-1.0, bias=1.0), reads=["lB"], writes=["lB"])
                rb = recb[mo]
                P.op("dve", lambda e, rb=rb: e.tensor_tensor(out=C[:], in0=C[:], in1=rb[:], op=ALU.mult),
                     reads=["lC", ("recb", mo)], writes=["lC"])
                P.op("dve", lambda e: e.tensor_tensor(out=C[:], in0=C[:], in1=B[:], op=ALU.mult), reads=["lC", "lB"], writes=["lC"])
                P.op("dve", lambda e: e.tensor_tensor_scan(out=B[:], data0=A[:], data1=C[:], initial=0.0, op0=ALU.mult, op1=ALU.add),
                     reads=["lA", "lC", "lB"], writes=["lB"])
                P.op("dve", lambda e, c=c: e.tensor_tensor(out=yb[:, c, :], in0=Dg[:], in1=B[:], op=ALU.mult),
                     reads=["Dg", "lB"], writes=[("yb", c)])
        yb_rhs = lambda k, n: (yb[:, k, n * TW:(n + 1) * TW], [("yb", k)])
        for m in range(NCH):
            slot = self.next_block()
            for n in range(NT):
                self.linear_tile(slot, yb_rhs, NCH, n, self.resid_add(m))
            self.done_block(slot)

    def cexp(self, n, lre_dt, lim_dt, out_r, out_i, tmps, itmp, tkey):
        P = self.P
        er, y, f, m = tmps
        kk = ("cx", tkey)
        P.op("act", lambda e: e.activation(out=er, in_=lre_dt, func=AF.Exp, scale=float(n)), reads=["s5in"], writes=[kk + ("er",)])
        for (dst, add) in ((out_i, 0.5), (out_r, 0.75)):
            P.op("dve", lambda e, add=add: e.tensor_scalar(out=y, in0=lim_dt, scalar1=float(n) / (2 * math.pi), scalar2=add,
                                                           op0=ALU.mult, op1=ALU.add), reads=["s5in"], writes=[kk + ("y",)])
            P.op("dve", lambda e: e.tensor_copy(out=itmp, in_=y), reads=[kk + ("y",)], writes=[kk + ("i",)])
            P.op("dve", lambda e: e.tensor_copy(out=f, in_=itmp), reads=[kk + ("i",)], writes=[kk + ("f",)])
            P.op("dve", lambda e: e.tensor_tensor(out=f, in0=y, in1=f, op=ALU.subtract), reads=[kk + ("y",), kk + ("f",)], writes=[kk + ("f",)])
            P.op("dve", lambda e: e.tensor_scalar(out=m, in0=f, scalar1=0.0, scalar2=None, op0=ALU.is_lt), reads=[kk + ("f",)], writes=[kk + ("m",)])
            P.op("dve", lambda e: e.tensor_tensor(out=f, in0=f, in1=m, op=ALU.add), reads=[kk + ("f",), kk + ("m",)], writes=[kk + ("f",)])
            P.op("dve", lambda e: e.tensor_scalar(out=m, in0=f, scalar1=1.0, scalar2=None, op0=ALU.is_ge), reads=[kk + ("f",)], writes=[kk + ("m",)])
            P.op("dve", lambda e: e.tensor_tensor(out=f, in0=f, in1=m, op=ALU.subtract), reads=[kk + ("f",), kk + ("m",)], writes=[kk + ("f",)])
            P.op("dve", lambda e: e.tensor_scalar(out=f, in0=f, scalar1=2 * math.pi, scalar2=-math.pi, op0=ALU.mult, op1=ALU.add),
                 reads=[kk + ("f",)], writes=[kk + ("f",)])
            P.op("dve", lambda e: e.tensor_scalar(out=f, in0=f, scalar1=3.14159, scalar2=-3.14159, op0=ALU.min, op1=ALU.max),
                 reads=[kk + ("f",)], writes=[kk + ("f",)])
            P.op("act", lambda e, dst=dst: e.activation(out=dst, in_=f, func=AF.Sin), reads=[kk + ("f",)], writes=[kk + ("o", add)])
        P.op("dve", lambda e: e.tensor_tensor(out=out_r, in0=out_r, in1=er, op=ALU.mult),
             reads=[kk + ("o", 0.75), kk + ("er",)], writes=[kk + ("o", 0.75), "s5tab"])
        P.op("dve", lambda e: e.tensor_tensor(out=out_i, in0=out_i, in1=er, op=ALU.mult),
             reads=[kk + ("o", 0.5), kk + ("er",)], writes=[kk + ("o", 0.5), "s5tab"])

    def s5_mixer(self, l):
        P = self.P
        xn = self.norm_xn("norm_mix_g", l)
        POWS = [1, 2, 3, 4, 5, 6, 7, 8, 16, 32, 64, 128, 256, 512, 1024]
        NPW = len(POWS)
        AR = self.carve(NPW * 64).rearrange("p (w g) -> p w g", w=NPW)
        AI = self.carve(NPW * 64).rearrange("p (w g) -> p w g", w=NPW)
        BTs = self.carve(64 * 128 // 2, BF16).rearrange("p (g z) -> p g z", g=64)
        Cc = self.carve(1024 // 2, BF16).rearrange("p (g h) -> p g h", g=64)
        Cpad = self.carve(8 * 128 // 2, BF16).rearrange("p (g c) -> p g c", g=8)
        mark = self.arena_off
        ident = self.c("ident", 0, 128)
        jt = self.c("jt", 0, 128)
        col_ld = self.carve(64)
        col_li = self.carve(64)
        ctm = [self.carve(64) for _ in range(4)]
        cit = self.carve(64).bitcast(I32)
        lre, lim, ldt = self.v("s5_lre", 0, 64), self.v("s5_lim", 0, 64), self.v("s5_ldt", 0, 64)

        def prep_in(dst_ld, dst_li, lre, lim, ldt, dtt):
            P.op("act", lambda e: e.activation(out=dtt, in_=ldt, func=AF.Exp), reads=["vecs", "s5row"], writes=["s5dtt"])
            P.op("dve", lambda e: e.tensor_scalar(out=dst_ld, in0=lre, scalar1=-1e-4, scalar2=None, op0=ALU.min),
                 reads=["vecs", "s5row"], writes=["s5in"])
            P.op("dve", lambda e: e.tensor_tensor(out=dst_ld, in0=dst_ld, in1=dtt, op=ALU.mult), reads=["s5in", "s5dtt"], writes=["s5in"])
            P.op("dve", lambda e: e.tensor_tensor(out=dst_li, in0=lim, in1=dtt, op=ALU.mult), reads=["s5dtt", "vecs", "s5row"], writes=["s5in"])
        prep_in(col_ld, col_li, lre, lim, ldt, ctm[0])
        for w, n in enumerate(POWS):
            self.cexp(n, col_ld, col_li, AR[:, w, :], AI[:, w, :], ctm, cit, "col")
        RW = 512
        R = [self.carve(RW) for _ in range(16)]
        rit = self.carve(RW).bitcast(I32)
        braw = self.carve(1024)
        bt1 = [self.carve(64) for _ in range(2)]
        bt2 = [self.carve(64) for _ in range(2)]
        r_lre, r_lim, r_ldt, r_ld, r_li, r_dt, lr, li_, sre, sim, den, t_ = R[:12]
        for c in range(NCH):
            for i, dstt in enumerate((r_lre, r_lim, r_ldt)):
                P.dma("sp", lambda e, i=i, c=c, dstt=dstt: e.dma_start(out=dstt[:], in_=self.s5row_d[:, i * 4096 + c * RW:i * 4096 + (c + 1) * RW]),
                      reads=["s5in", "s5dtt", "s5sre", "s5sim", "s5den", "s5t"], writes=["s5row"])
            P.dma("sp", lambda e, c=c: e.dma_start(out=braw[:], in_=self.s5bt_d[c]), reads=[("bt1", 0), ("bt1", 1), ("bt2", 0), ("bt2", 1)],
                  writes=["s5bt"])
            P.op("dve", lambda e: e.tensor_scalar(out=r_lre[:], in0=r_lre[:], scalar1=-1e-4, scalar2=None, op0=ALU.min), reads=["s5row"], writes=["s5row"])
            prep_in(r_ld[:], r_li[:], r_lre[:], r_lim[:], r_ldt[:], r_dt[:])
            self.cexp(1, r_ld[:], r_li[:], lr[:], li_[:], [R[12][:], R[13][:], R[14][:], R[15][:]], rit, "row")
            P.op("dve", lambda e: e.tensor_scalar(out=lr[:], in0=lr[:], scalar1=-1.0, scalar2=None, op0=ALU.add), reads=["s5tab"], writes=["s5tab"])
            P.op("dve", lambda e: e.tensor_tensor(out=den[:], in0=r_lre[:], in1=r_lre[:], op=ALU.mult), reads=["s5row"], writes=["s5den"])
            P.op("dve", lambda e: e.tensor_tensor(out=t_[:], in0=r_lim[:], in1=r_lim[:], op=ALU.mult), reads=["s5row"], writes=["s5t"])
            P.op("dve", lambda e: e.tensor_tensor(out=den[:], in0=den[:], in1=t_[:], op=ALU.add), reads=["s5den", "s5t"], writes=["s5den"])
            P.op("dve", lambda e: e.reciprocal(out=den[:], in_=den[:]), reads=["s5den"], writes=["s5den"])
            P.op("dve", lambda e: e.tensor_tensor(out=sre[:], in0=lr[:], in1=r_lre[:], op=ALU.mult), reads=["s5tab", "s5row"], writes=["s5sre"])
            P.op("dve", lambda e: e.tensor_tensor(out=t_[:], in0=li_[:], in1=r_lim[:], op=ALU.mult), reads=["s5tab", "s5row", "s5den"], writes=["s5t"])
            P.op("dve", lambda e: e.tensor_tensor(out=sre[:], in0=sre[:], in1=t_[:], op=ALU.add), reads=["s5sre", "s5t"], writes=["s5sre"])
            P.op("dve", lambda e: e.tensor_tensor(out=sre[:], in0=sre[:], in1=den[:], op=ALU.mult), reads=["s5sre", "s5den"], writes=["s5sre"])
            P.op("dve", lambda e: e.tensor_tensor(out=sim[:], in0=li_[:], in1=r_lre[:], op=ALU.mult), reads=["s5tab", "s5row"], writes=["s5sim"])
            P.op("dve", lambda e: e.tensor_tensor(out=t_[:], in0=lr[:], in1=r_lim[:], op=ALU.mult), reads=["s5tab", "s5row", "s5sre"], writes=["s5t"])
            P.op("dve", lambda e: e.tensor_tensor(out=sim[:], in0=sim[:], in1=t_[:], op=ALU.subtract), reads=["s5sim", "s5t"], writes=["s5sim"])
            P.op("dve", lambda e: e.tensor_tensor(out=sim[:], in0=sim[:], in1=den[:], op=ALU.mult), reads=["s5sim", "s5den"], writes=["s5sim"])
            for gl in range(8):
                g = c * 8 + gl
                bre = braw[:, gl * 64:(gl + 1) * 64]
                bim = braw[:, 512 + gl * 64:512 + (gl + 1) * 64]
                sr = sre[:, gl * 64:(gl + 1) * 64]
                si = sim[:, gl * 64:(gl + 1) * 64]
                i2 = g % 2
                eng2 = "dve" if g % 2 == 0 else "pool"
                for (o0, a_, b_, op2) in ((0, sr, si, ALU.subtract), (64, si, sr, ALU.add)):
                    P.op(eng2, lambda e, a_=a_, bre=bre, i2=i2: e.tensor_tensor(out=bt1[i2][:], in0=bre, in1=a_, op=ALU.mult),
                         reads=["s5bt", "s5sre", "s5sim"], writes=[("bt1", i2)])
                    P.op(eng2, lambda e, b_=b_, bim=bim, i2=i2: e.tensor_tensor(out=bt2[i2][:], in0=bim, in1=b_, op=ALU.mult),
                         reads=["s5bt", "s5sre", "s5sim"], writes=[("bt2", i2)])
                    P.op(eng2, lambda e, g=g, o0=o0, op2=op2, i2=i2: e.tensor_tensor(out=BTs[:, g, o0:o0 + 64], in0=bt1[i2][:], in1=bt2[i2][:], op=op2),
                         reads=[("bt1", i2), ("bt2", i2)], writes=[("BTs", g)])
        P.dma("pool", lambda e: e.dma_start(out=Cc.rearrange("p g h -> p (g h)"), in_=self.s5c_d[:, :]), writes=["Cc"])
        P.op("dve", lambda e: e.tensor_scalar(out=Cc[64:128, :, :], in0=Cc[64:128, :, :], scalar1=-1.0, scalar2=None, op0=ALU.mult),
             reads=["Cc"], writes=["Cc"])
        P.op("dve", lambda e: e.memset(Cpad.rearrange("p g c -> p (g c)"), 0.0), writes=[("Cpad", i) for i in range(8)])
        self.reset_arena(mark)
        NB = 4
        identb = self.carve(64, BF16)
        P.op("act", lambda e: e.activation(out=identb[:], in_=ident, func=AF.Copy), reads=["cst"], writes=["identb"])
        Zb = [self.carve(S // 2, BF16) for _ in range(NB)]
        Eb2 = [[self.carve(128, BF16) for _ in range(2)] for _ in range(NB)]
        MT = [[[self.carve(64, BF16) for _ in range(NPW)] for _ in range(NB)] for _ in range(2)]
        T1 = [self.carve(128) for _ in range(3)]
        ytmp = self.carve(TW)
        self.bank_mod = 4
        self.bank_i = 0
        pw = {n: i for i, n in enumerate(POWS)}
        t1c = [0]

        def mtgen(bi):
            for q, g in enumerate(range(bi * NB, (bi + 1) * NB)):
                for w in range(NPW):
                    ti = t1c[0] % 3
                    t1c[0] += 1
                    t1 = T1[ti]
                    P.op("dve", lambda e, t1=t1, w=w, g=g: e.tensor_scalar(out=t1[:], in0=ident, scalar1=AR[:, w, g:g + 1], scalar2=None, op0=ALU.mult),
                         reads=["cst", "s5tab"], writes=[("T1", ti)])
                    P.op("dve", lambda e, t1=t1, w=w, g=g, m_=MT[bi % 2][q][w]: e.scalar_tensor_tensor(out=m_[:], in0=jt, scalar=AI[:, w, g:g + 1], in1=t1[:],
                                                                                                   op0=ALU.mult, op1=ALU.add),
                         reads=["cst", "s5tab", ("T1", ti)], writes=[("MT", bi % 2, q)])
        mtgen(0)
        for bi in range(64 // NB):
            groups = list(range(bi * NB, (bi + 1) * NB))
            c = groups[0] // 8
            mtb = MT[bi % 2]
            if bi + 1 < 64 // NB:
                mtgen(bi + 1)
            for q, g in enumerate(groups):
                gl = g % 8
                P.op("pool", lambda e, gl=gl, g=g: e.tensor_copy(out=Cpad[:, gl, 16 * gl:16 * gl + 16], in_=Cc[:, g, :]),
                     reads=["Cc"], writes=[("Cpad", gl)])
            for n in range(NT):
                sl = slice(n * TW, (n + 1) * TW)
                for q, g in enumerate(groups):
                    b = self.bank()
                    P.op("pe", lambda e, b=b, g=g, sl=sl, c=c: e.matmul(self.psb[b][:], lhsT=BTs[:, g, :], rhs=xn[:, c, sl], start=True, stop=True),
                         reads=[("BTs", g), ("xn", c)], writes=[("ps", b)])
                    P.op("act", lambda e, b=b, q=q, sl=sl: e.activation(out=Zb[q][:, sl], in_=self.psb[b][:], func=AF.Copy),
                         reads=[("ps", b)], writes=[("Zb", q)])
            L1 = [(4, 1, 0, 1), (4, 3, 2, 1), (4, 2, 1, 1), (4, 3, 1, 2), (8, 4, 3, 1), (8, 5, 3, 2), (8, 6, 3, 3), (8, 7, 3, 4)]
            for (per, dcl, scl, pwr) in (L1 if S5_DBG[0] else ()):
                ncol = S // per
                for t0 in range(0, ncol, TW):
                    tn = min(TW, ncol - t0)
                    for q, g in enumerate(groups):
                        zv = Zb[q].rearrange("p (a r) -> p a r", r=per)
                        dst = zv[:, t0:t0 + tn, dcl]
                        src = zv[:, t0:t0 + tn, scl]
                        b = self.bank()
                        ps = self.psb[b]
                        P.op("pe", lambda e, ps=ps, dst=dst, tn=tn: e.matmul(ps[:, 0:tn], lhsT=identb[:], rhs=dst, start=True, stop=False),
                             reads=["identb", ("Zb", q)], writes=[("ps", b)])
                        P.op("pe", lambda e, ps=ps, src=src, tn=tn, m_=mtb[q][pw[pwr]]: e.matmul(ps[:, 0:tn], lhsT=m_[:], rhs=src, start=False, stop=True),
                             reads=[("MT", bi % 2, q), ("Zb", q)], writes=[("ps", b)])
                        P.op("act", lambda e, ps=ps, dst=dst, tn=tn: e.activation(out=dst, in_=ps[:, 0:tn], func=AF.Copy),
                             reads=[("ps", b)], writes=[("Zb", q)])
            for j in (range(8) if S5_DBG[1] else ()):
                k = 2 ** j
                for q, g in enumerate(groups):
                    eold = Zb[q].rearrange("p (a r) -> p a r", r=8)[:, :, 7] if j == 0 else Eb2[q][(j - 1) % 2][:, 0:256]
                    eold_sh = Zb[q].rearrange("p (a r) -> p a r", r=8)[:, 0:256 - k, 7] if j == 0 else Eb2[q][(j - 1) % 2][:, 0:256 - k]
                    enew = Eb2[q][j % 2]
                    b = self.bank()
                    ps = self.psb[b]
                    P.op("pe", lambda e, ps=ps, eold=eold: e.matmul(ps[:, 0:256], lhsT=identb[:], rhs=eold, start=True, stop=False),
                         reads=["identb", ("Zb", q), ("E", q, (j - 1) % 2)], writes=[("ps", b)])
                    P.op("pe", lambda e, ps=ps, eold_sh=eold_sh, k=k, m_=mtb[q][pw[8 * k]]: e.matmul(ps[:, k:256], lhsT=m_[:], rhs=eold_sh, start=False, stop=True),
                         reads=[("MT", bi % 2, q), ("Zb", q), ("E", q, (j - 1) % 2)], writes=[("ps", b)])
                    P.op("dve", lambda e, ps=ps, enew=enew: e.tensor_copy(out=enew[:, 0:256], in_=ps[:, 0:256]),
                         reads=[("ps", b)], writes=[("E", q, j % 2)])
            for ip in (range(4) if S5_DBG[2] else ()):
                for q, g in enumerate(groups):
                    efin = Eb2[q][7 % 2]
                    z8 = Zb[q].rearrange("p (a r) -> p a r", r=8)
                    b = self.bank()
                    ps = self.psb[b]
                    for t in range(2):
                        i = 2 * ip + t
                        P.op("pe", lambda e, ps=ps, t=t, i=i, z8=z8: e.matmul(ps[:, t * 256:(t + 1) * 256], lhsT=identb[:], rhs=z8[:, :, i],
                                                                            start=True, stop=False),
                             reads=["identb", ("Zb", q)], writes=[("ps", b)])
                        P.op("pe", lambda e, ps=ps, t=t, i=i, efin=efin, m_=mtb[q][pw[i + 1]]: e.matmul(
                            ps[:, t * 256 + 1:(t + 1) * 256], lhsT=m_[:], rhs=efin[:, 0:255], start=False, stop=True),
                            reads=[("MT", bi % 2, q), ("E", q, 1)], writes=[("ps", b)])
                    for t in range(2):
                        P.op("dve", lambda e, ps=ps, z8=z8, ip=ip, t=t: e.tensor_copy(out=z8[:, :, 2 * ip + t], in_=ps[:, t * 256:(t + 1) * 256]),
                             reads=[("ps", b)], writes=[("Zb", q)])
            for q, g in enumerate(groups):
                gl = g % 8
                for n in range(NT):
                    sl = slice(n * TW, (n + 1) * TW)
                    P.op("pe", lambda e, n=n, gl=gl, q=q, sl=sl: e.matmul(self.psb[4 + n][:], lhsT=Cpad[:, gl, :], rhs=Zb[q][:, sl],
                                                                        start=(gl == 0), stop=(gl == 7)),
                         reads=[("Cpad", gl), ("Zb", q)], writes=[("ps", 4 + n)])
            if groups[-1] % 8 == 7:
                for n in range(NT):
                    sl = slice(n * TW, (n + 1) * TW)
                    P.op("dve", lambda e, n=n, c=c, sl=sl: e.scalar_tensor_tensor(out=ytmp[:], in0=xn[:, c, sl], scalar=self.v("s5_d", c),
                                                                                  in1=self.psb[4 + n][:], op0=ALU.mult, op1=ALU.add),
                         reads=[("ps", 4 + n), ("xn", c), "vecs"], writes=["ytmp"])
                    P.op("act", lambda e, c=c, sl=sl: e.activation(out=xn[:, c, sl], in_=ytmp[:], func=AF.Gelu),
                         reads=["ytmp"], writes=[("xn", c)])
        self.reset_arena(mark)
        self.bank_mod = 8
        vt = [self.carve(TW) for _ in range(2)]
        gt = [self.carve(TW) for _ in range(2)]
        xn_rhs = lambda k, n: (xn[:, k, n * TW:(n + 1) * TW], [("xn", k)])
        cnt = [0]
        for m in range(NCH):
            sv, sg = self.next_block(), self.next_block()
            for n in range(NT):
                i = cnt[0] % 2
                cnt[0] += 1
                sl = slice(n * TW, (n + 1) * TW)

                def evac_v(b, n, i=i, m=m):
                    P.op("act", lambda e: e.activation(out=vt[i][:], in_=self.psb[b][:], func=AF.Identity, bias=self.v("s5_b_out", m)),
                         reads=[("ps", b), "vecs"], writes=[("vt", i)])

                def evac_g(b, n, i=i, m=m):
                    P.op("act", lambda e: e.activation(out=gt[i][:], in_=self.psb[b][:], func=AF.Sigmoid, bias=self.v("s5_b_out", 8 + m)),
                         reads=[("ps", b), "vecs"], writes=[("gt", i)])
                self.linear_tile(sv, xn_rhs, NCH, n, evac_v)
                self.linear_tile(sg, xn_rhs, NCH, n, evac_g)
                P.op("dve", lambda e, i=i: e.tensor_tensor(out=vt[i][:], in0=vt[i][:], in1=gt[i][:], op=ALU.mult),
                     reads=[("vt", i), ("gt", i)], writes=[("vt", i)])
                P.op("dve", lambda e, i=i, m=m, sl=sl: e.tensor_tensor(out=self.xT[:, m, sl], in0=self.xT[:, m, sl], in1=vt[i][:], op=ALU.add),
                     reads=[("vt", i), ("xT", m, n)], writes=[("xT", m, n)])
            self.done_block(sv)
            self.done_block(sg)

    def sb_mixer(self, l):
        P = self.P
        xn = self.norm_xn("norm_mix_g", l)
        cm = self.carve(4352 // 2, BF16)
        m01 = [cm[:, i * 512:(i + 1) * 512] for i in range(4)]
        negm = [cm[:, 2048 + i * 512:2048 + (i + 1) * 512] for i in range(4)]
        negtri = cm[:, 4096:4224]
        identb = cm[:, 4224:4352]
        for (a, b_) in ((0, 2048), (2048, 4096), (4096, 4352)):
            P.dma("pool", lambda e, a=a, b_=b_: e.dma_start(out=cm[:, a:b_], in_=self.m_d[:, a:b_]), writes=["cm"])
        ones2 = self.carve(64, BF16)
        negones = self.carve(64, BF16)
        P.op("dve", lambda e: e.memset(ones2[:], 0.0), writes=["ones2"])
        P.op("dve", lambda e: e.memset(ones2[0:64, 0:64], 1.0), reads=["ones2"], writes=["ones2"])
        P.op("dve", lambda e: e.memset(ones2[64:128, 64:128], 1.0), reads=["ones2"], writes=["ones2"])
        P.op("dve", lambda e: e.memset(negones[:], -1.0), writes=["negones"])
        qT = self.carve(2 * S // 2, BF16).rearrange("p (c s) -> p c s", c=2)
        kT = self.carve(2 * S // 2, BF16).rearrange("p (c s) -> p c s", c=2)
        vtm = self.carve(16 * 256 // 2, BF16).rearrange("p (t j) -> p t j", t=16)
        oT = self.carve(2 * S // 2, BF16).rearrange("p (c s) -> p c s", c=2)
        qraw = [self.carve(TW) for _ in range(2)]
        sqb = [self.carve(TW // 2, BF16) for _ in range(2)]
        rq = [self.carve(TW) for _ in range(2)]
        Eb = [self.carve(TW) for _ in range(2)]
        Lb = [self.carve(TW // 2, BF16) for _ in range(3)]
        att = [self.carve(TW // 2, BF16) for _ in range(3)]
        Lacc = self.carve(TW // 2, BF16)
        xn_rhs = lambda k, n: (xn[:, k, n * TW:(n + 1) * TW], [("xn", k)])
        cnt = [0]
        for qq in range(4):
            for (dstT, gap, dname) in ((qT, self.gqs[:, 0:1], "qT"), (kT, self.v("sb_k_g", 0), "kT")):
                for cc in range(2):
                    slot = self.next_block()

                    def evac_qk(b, n, cc=cc, dstT=dstT, gap=gap, dname=dname):
                        i = cnt[0] % 2
                        cnt[0] += 1
                        sl = slice(n * TW, (n + 1) * TW)
                        P.op("act", lambda e: e.activation(out=qraw[i][:], in_=self.psb[b][:], func=AF.Copy),
                             reads=[("ps", b)], writes=[("qraw", i)])
                        P.op("act", lambda e: e.activation(out=sqb[i][:], in_=self.psb[b][:], func=AF.Square),
                             reads=[("ps", b)], writes=[("sqb", i)])
                        b2 = self.bank()
                        P.op("pe", lambda e: e.matmul(self.psb[b2][:], lhsT=ones2[:], rhs=sqb[i][:], start=True, stop=True),
                             reads=["ones2", ("sqb", i)], writes=[("ps", b2)])
                        P.op("act", lambda e: e.activation(out=rq[i][:], in_=self.psb[b2][:], func=AF.Sqrt, scale=1.0 / 64, bias=EPS),
                             reads=[("ps", b2)], writes=[("rq", i)])
                        P.op("dve", lambda e: e.reciprocal(out=rq[i][:], in_=rq[i][:]), reads=[("rq", i)], writes=[("rq", i)])
                        P.op("dve", lambda e: e.scalar_tensor_tensor(out=dstT[:, cc, sl], in0=qraw[i][:], scalar=gap, in1=rq[i][:],
                                                                     op0=ALU.mult, op1=ALU.mult),
                             reads=[("qraw", i), ("rq", i), "vecs", "gqs"], writes=[(dname, cc, n)])
                    for n in range(NT):
                        self.linear_tile(slot, xn_rhs, NCH, n, evac_qk)
                    self.done_block(slot)
            sva, svb = self.next_block(), self.next_block()
            for tb in range(16):
                b = self.bank()
                for k in range(NCH):
                    slot = sva if k < 4 else svb
                    P.op("pe", lambda e, b=b, k=k, slot=slot, tb=tb: e.matmul(
                        self.psb[b][:, 0:256], lhsT=xn[:, k, tb * 128:(tb + 1) * 128], rhs=self.ring[slot][:, (k % 4) * 256:(k % 4 + 1) * 256],
                        start=(k == 0), stop=(k == NCH - 1)), reads=[("ring", slot), ("xn", k)], writes=[("ps", b)])
                P.op("act", lambda e, b=b, tb=tb: e.activation(out=vtm[:, tb, :], in_=self.psb[b][:, 0:256], func=AF.Copy),
                     reads=[("ps", b)], writes=[("vtm", tb)])
            self.done_block(sva)
            self.done_block(svb)
            self.bank_mod = 6
            self.bank_i = 0
            units = []
            for hq in range(SB_HEADS_DBG):
                for qb in range(4):
                    kbs = list(range(4 * qb + 3, -1, -1))
                    for ii, kb in enumerate(kbs):
                        units.append(dict(hq=hq, qb=qb, kb=kb, first=(ii == 0), last=(kb == 0), ui=len(units)))

            def front(u):
                hq, qb, kb, ui = u["hq"], u["qb"], u["kb"], u["ui"]
                cq, pb = hq // 2, 64 * (hq % 2)
                diag = kb >= 4 * qb
                oi = kb - 4 * qb
                b = self.bank()
                u["b"] = b
                ps = self.psb[b]
                e_i, l_i = ui % 2, ui % 3
                P.op("pe", lambda e: e.matmul(ps[:], lhsT=kT[pb:pb + 64, cq, kb * 128:(kb + 1) * 128], rhs=qT[pb:pb + 64, cq, qb * TW:(qb + 1) * TW],
                                              start=True, stop=False),
                     reads=[("kT", cq, kb // 4), ("qT", cq, qb)], writes=[("ps", b)])
                P.op("act", lambda e: e.activation(out=Eb[e_i][:], in_=ps[:], func=AF.Exp), reads=[("ps", b)], writes=[("Eb", e_i)])
                P.op("act", lambda e: e.activation(out=Lb[l_i][:], in_=Eb[e_i][:], func=AF.Ln, bias=1.0), reads=[("Eb", e_i)], writes=[("Lb", l_i)])
                if diag:
                    P.op("dve", lambda e: e.tensor_tensor(out=Lb[l_i][:], in0=Lb[l_i][:], in1=m01[oi], op=ALU.mult),
                         reads=[("Lb", l_i), "cm"], writes=[("Lb", l_i)])

            def mid(u):
                hq, qb, kb, ui, first, last, b = u["hq"], u["qb"], u["kb"], u["ui"], u["first"], u["last"], u["b"]
                diag = kb >= 4 * qb
                oi = kb - 4 * qb
                ps = self.psb[b]
                l_i, a_i = ui % 3, ui % 3
                more = (not first) or diag
                P.op("pe", lambda e: e.matmul(ps[:], lhsT=negtri, rhs=Lb[l_i][:], start=False, stop=not more),
                     reads=["cm", ("Lb", l_i)], writes=[("ps", b)])
                if not first:
                    P.op("pe", lambda e: e.matmul(ps[:], lhsT=negones[:], rhs=Lacc[:], start=False, stop=not diag),
                         reads=["negones", "Lacc"], writes=[("ps", b)])
                if diag:
                    P.op("pe", lambda e: e.matmul(ps[:], lhsT=identb, rhs=negm[oi], start=False, stop=True), reads=["cm"], writes=[("ps", b)])
                P.op("act", lambda e: e.activation(out=att[a_i][:], in_=ps[:], func=AF.Exp), reads=[("ps", b)], writes=[("att", a_i)])
                if not last:
                    if first:
                        P.op("dve", lambda e: e.tensor_copy(out=Lacc[:], in_=Lb[l_i][:]), reads=[("Lb", l_i)], writes=["Lacc"])
                    else:
                        P.op("dve", lambda e: e.tensor_tensor(out=Lacc[:], in0=Lacc[:], in1=Lb[l_i][:], op=ALU.add),
                             reads=[("Lb", l_i), "Lacc"], writes=["Lacc"])

            def tail(u):
                hq, qb, kb, ui, first, last = u["hq"], u["qb"], u["kb"], u["ui"], u["first"], u["last"]
                cq, pb = hq // 2, 64 * (hq % 2)
                a_i = ui % 3
                pob = 6 + (hq * 4 + qb) % 2
                po = self.psb[pob]
                P.op("pe", lambda e: e.matmul(po[pb:pb + 64, :], lhsT=vtm[:, kb, hq * 64:(hq + 1) * 64], rhs=att[a_i][:], start=first, stop=last),
                     reads=[("vtm", kb), ("att", a_i)], writes=[("ps", pob)])
                if last:
                    P.op("act", lambda e: e.activation(out=oT[pb:pb + 64, cq, qb * TW:(qb + 1) * TW], in_=po[pb:pb + 64, :], func=AF.Copy),
                         reads=[("ps", pob)], writes=[("oT", cq, qb)])
            nu = len(units)
            for i in range(nu + 2):
                if i < nu:
                    front(units[i])
                if 1 <= i <= nu:
                    mid(units[i - 1])
                if i >= 2:
                    tail(units[i - 2])
            self.bank_mod = 8
            so0, so1 = self.next_block(), self.next_block()
            for m in range(NCH):
                for n in range(NT):
                    sl = slice(n * TW, (n + 1) * TW)
                    b = self.bank()
                    for k, so in enumerate((so0, so1)):
                        P.op("pe", lambda e, b=b, k=k, so=so, m=m, sl=sl: e.matmul(
                            self.psb[b][:], lhsT=self.ring[so][:, m * 128:(m + 1) * 128], rhs=oT[:, k, sl], start=(k == 0), stop=(k == 1)),
                            reads=[("ring", so), ("oT", k, n)], writes=[("ps", b)])
                    self.resid_add(m)(b, n)
            self.done_block(so0)
            self.done_block(so1)

    def ffn(self, l):
        P = self.P
        xn = self.norm_xn("norm_ffn_g", l)
        hv = [self.carve((2 + S + 2) // 2, BF16) for _ in range(2)]
        hg = [self.carve((2 + S + 2) // 2, BF16) for _ in range(2)]
        cv = [self.carve(S)] * 2
        cg = [self.carve(S)] * 2
        GQ = 4
        gb = [[self.carve(S // 2, BF16) for _ in range(GQ)] for _ in range(2)]
        for buf in hv + hg:
            P.op("dve", lambda e, buf=buf: e.memset(buf[:, 0:2], 0.0), writes=[("fh", id(buf), n) for n in range(NT)])
        cwo = self.voff["ffn_conv_w"] + l * 3 * 44
        cbo = self.voff["ffn_conv_b"] + l * 44
        pend = []
        for f in range(NF):
            par = f % 2
            sv, sg, so = self.next_block(), self.next_block(), self.next_block()
            for (slot, h, cc, ch) in ((sv, hv[par], cv[par], f), (sg, hg[par], cg[par], NF + f)):
                blk = self.ring[slot]
                w0 = self.vecs[:, cwo + 0 * 44 + ch:cwo + 0 * 44 + ch + 1]
                w1 = self.vecs[:, cwo + 1 * 44 + ch:cwo + 1 * 44 + ch + 1]
                w2 = self.vecs[:, cwo + 2 * 44 + ch:cwo + 2 * 44 + ch + 1]
                bb = self.vecs[:, cbo + ch:cbo + ch + 1]
                for n in range(NT):
                    sl = slice(n * TW, (n + 1) * TW)
                    b = self.bank()
                    for k in range(NCH):
                        P.op("pe", lambda e, b=b, blk=blk, k=k, sl=sl: e.matmul(
                            self.psb[b][:], lhsT=blk[:, k * 128:(k + 1) * 128], rhs=xn[:, k, sl], start=(k == 0), stop=(k == NCH - 1)),
                            reads=[("ring", slot), ("xn", k)], writes=[("ps", b)])
                    P.op("act", lambda e, b=b, h=h, n=n: e.activation(out=h[:, 2 + n * TW:2 + (n + 1) * TW], in_=self.psb[b][:], func=AF.Copy),
                         reads=[("ps", b)], writes=[("fh", id(h), n)])
                    P.op("act", lambda e, b=b, cc=cc, sl=sl, w2=w2, bb=bb: e.activation(out=cc[:, sl], in_=self.psb[b][:], func=AF.Identity,
                                                                                         scale=w2, bias=bb),
                         reads=[("ps", b), "vecs"], writes=[("fc", id(cc))])
                P.op("dve", lambda e, h=h, cc=cc, w1=w1: e.scalar_tensor_tensor(out=cc[:], in0=h[:, 1:1 + S], scalar=w1, in1=cc[:],
                                                                                op0=ALU.mult, op1=ALU.add),
                     reads=[("fh", id(h), n) for n in range(NT)] + [("fc", id(cc)), "vecs"], writes=[("fc", id(cc))])
                P.op("dve", lambda e, h=h, cc=cc, w0=w0: e.scalar_tensor_tensor(out=cc[:], in0=h[:, 0:S], scalar=w0, in1=cc[:],
                                                                                op0=ALU.mult, op1=ALU.add),
                     reads=[("fh", id(h), n) for n in range(NT)] + [("fc", id(cc)), "vecs"], writes=[("fc", id(cc))])
            self.done_block(sv)
            self.done_block(sg)
            cgp, cvp = cg[par], cv[par]
            P.op("act", lambda e, cgp=cgp: e.activation(out=cgp[:], in_=cgp[:], func=AF.Silu),
                 reads=[("fc", id(cgp))], writes=[("fc", id(cgp))])
            g = gb[(f // GQ) % 2][f % GQ]
            P.op("dve", lambda e, g=g, cgp=cgp, cvp=cvp: e.tensor_tensor(out=g[:], in0=cgp[:], in1=cvp[:], op=ALU.mult),
                 reads=[("fc", id(cgp)), ("fc", id(cvp))], writes=[("fg", id(g))])
            pend.append((so, g))
            if len(pend) == GQ or f == NF - 1:
                for m in range(NCH):
                    for n in range(NT):
                        sl = slice(n * TW, (n + 1) * TW)
                        b = self.bank()
                        for ii, (so_, g_) in enumerate(pend):
                            P.op("pe", lambda e, b=b, m=m, sl=sl, so_=so_, g_=g_, ii=ii, np_=len(pend): e.matmul(
                                self.psb[b][:], lhsT=self.ring[so_][:, m * 128:(m + 1) * 128], rhs=g_[:, sl], start=(ii == 0), stop=(ii == np_ - 1)),
                                reads=[("ring", so_), ("fg", id(g_))], writes=[("ps", b)])
                        P.op("dve", lambda e, b=b, m=m, sl=sl: e.tensor_tensor(out=self.xT[:, m, sl], in0=self.xT[:, m, sl], in1=self.psb[b][:],
                                                                               op=ALU.add),
                             reads=[("ps", b), ("xT", m, n)], writes=[("xT", m, n)])
                for (so_, g_) in pend:
                    self.done_block(so_)
                pend = []

    def build(self):
        self.setup()
        for s in range(self.nseq):
            self.load_x(s)
            for l in self.layers:
                m = l % 4
                if m == 0:
                    self.pool_mixer(l)
                elif m == 1:
                    self.s5_mixer(l)
                elif m == 2:
                    self.lru_mixer(l)
                elif m == 3:
                    self.sb_mixer(l)
                self.ffn(l)
            self.store_x(s)
        assert self.blk_i == self.blk_total, (self.blk_i, self.blk_total)
        return self.P.finalize()


LAYERS = [0, 1, 2, 3]
NSEQ = 2
NCORES = 8


def run(inputs, layers=LAYERS, nseq=NSEQ, ncores=NCORES, trace=False):
    x = np.asarray(inputs["x"], np.float32)
    V = prep_vecs(inputs)
    C = prep_consts()
    vecs = V.pack()
    consts = C.pack()
    wts = prep_weights(inputs, layers)
    cmask = prep_cmask()
    s5row, s5bt, s5c = prep_s5(inputs)
    bld = Builder(layers, nseq, wts.shape[0], V.off, V.n, C.off, C.n)
    nc = bld.build()
    in_maps = []
    for core in range(ncores):
        xs = x[core * nseq:(core + 1) * nseq]
        xt = np.ascontiguousarray(xs.reshape(nseq, S, NCH, 128).transpose(0, 3, 2, 1))
        in_maps.append({"xT": xt, "wts": wts, "vecs": vecs, "consts": consts, "cmask": cmask,
                        "s5row": s5row, "s5bt": s5bt, "s5c": s5c})
    res = run_bass_kernel_spmd(nc, in_maps, core_ids=list(range(ncores)), trace=trace)
    outs = []
    for core in range(ncores):
        o = res.results[core]["outT"]
        outs.append(np.ascontiguousarray(o.transpose(0, 3, 2, 1)).reshape(nseq, S, D))
    return np.concatenate(outs, axis=0).astype(np.float32), res, bld


def kernel(**inputs):
    out, _, _ = run(inputs)
    return out
```
